# Optimizing a Trainium2 kernel written in Bass

```python
import jax, jax.numpy as jnp
from jax import lax
import numpy as np

D_MODEL = 2048
BATCH = 4
SEQ = 2048
DEPTH = 2
DEC_BATCH = 128
DEC_SEQ = 8
PAST_LEN = 16384
PAGE_SIZE = 128

N_META = 16
CONV_WIDTH = 3
CONV_HIST = CONV_WIDTH - 1
POOL_WINDOWS = (2, 4, 8, 16)
N_POOL_GROUPS = len(POOL_WINDOWS)
POOL_GROUP_DIM = D_MODEL // N_POOL_GROUPS
POOL_HIST = max(POOL_WINDOWS) - 1
D_FF = ((8 * D_MODEL + 3 * 256 - 1) // (3 * 256)) * 256
N_CONV_LAYERS = (DEPTH + 1) // 2
N_POOL_LAYERS = DEPTH // 2
EPS = 1e-6

kernel_name = "hybrid_shortconv_pool_decoder_step"


def rmsnorm(x, g):
    xf = x.astype(jnp.float32)
    y = xf * lax.rsqrt(jnp.mean(xf * xf, axis=-1, keepdims=True) + EPS)
    return (y * g.astype(jnp.float32)).astype(x.dtype)


def conv_mixer(h, hist, w_in, w_dw, w_out):
    T = h.shape[1]
    bcv = h @ w_in
    b, c, v = jnp.split(bcv, 3, axis=-1)
    u = c * v
    u_full = jnp.concatenate([hist.astype(u.dtype), u], axis=1)
    conv = sum(w_dw[k] * u_full[:, k:k + T] for k in range(CONV_WIDTH))
    y = (b * conv) @ w_out
    return y, u_full[:, -CONV_HIST:]


def pool_mixer(h, hist, t0, w_pool, scale):
    B, T, D = h.shape
    u = jnp.concatenate([hist.astype(h.dtype), h], axis=1)
    cs = jnp.cumsum(u.astype(jnp.float32), axis=1)
    cs = jnp.concatenate([jnp.zeros((B, 1, D), jnp.float32), cs], axis=1)
    end = cs[:, POOL_HIST + 1:POOL_HIST + 1 + T]
    start = jnp.concatenate(
        [cs[:, POOL_HIST + 1 - w:POOL_HIST + 1 - w + T, g * POOL_GROUP_DIM:(g + 1) * POOL_GROUP_DIM]
         for g, w in enumerate(POOL_WINDOWS)], axis=-1)
    win = jnp.repeat(jnp.asarray(POOL_WINDOWS, jnp.float32), POOL_GROUP_DIM)
    pos = (t0 + jnp.arange(T, dtype=jnp.int32)).astype(jnp.float32)
    count = jnp.minimum(win[None, :], pos[:, None] + 1.0)
    p = ((end - start) / count - h.astype(jnp.float32)).astype(h.dtype)
    pg = p.reshape(B, T, N_POOL_GROUPS, POOL_GROUP_DIM)
    out = jnp.einsum('btgc,gcd->btgd', pg, w_pool).reshape(B, T, D)
    return out * scale, u[:, -POOL_HIST:]


def swiglu(h, w_gu, w_down):
    g, u = jnp.split(h @ w_gu, 2, axis=-1)
    return (jax.nn.silu(g) * u) @ w_down


def trunk(x, conv_states, pool_states, t0, norm_mix, norm_ffn, norm_final,
          conv_w_in, conv_w_dw, conv_w_out, pool_w, pool_scale, ffn_w_gate_up, ffn_w_down):
    h = x
    new_conv, new_pool = [], []
    for i in range(DEPTH):
        n = rmsnorm(h, norm_mix[i])
        j = i // 2
        if i % 2 == 0:
            y, s = conv_mixer(n, conv_states[j], conv_w_in[j], conv_w_dw[j], conv_w_out[j])
            new_conv.append(s)
        else:
            y, s = pool_mixer(n, pool_states[j], t0, pool_w[j], pool_scale[j])
            new_pool.append(s)
        h = h + y
        h = h + swiglu(rmsnorm(h, norm_ffn[i]), ffn_w_gate_up[i], ffn_w_down[i])
    return rmsnorm(h, norm_final), jnp.stack(new_conv), jnp.stack(new_pool)


def setup_inputs(seed: int = 0) -> dict:
    key = jax.random.key(seed)
    ks = jax.random.split(key, 16)
    f32 = jnp.float32
    D, F = D_MODEL, D_FF
    nrm = lambda k, s, sc: jax.random.normal(k, s, f32) * sc
    return {
        "x_prompt": nrm(ks[0], (BATCH, SEQ, D), 1.0),
        "x_sample": nrm(ks[1], (DEC_BATCH, DEC_SEQ, D), 1.0),
        "state_conv": nrm(ks[2], (N_CONV_LAYERS, DEC_BATCH, CONV_HIST, D), 1.0),
        "state_pool": nrm(ks[3], (N_POOL_LAYERS, DEC_BATCH, POOL_HIST, D), 1.0),
        "meta_tokens": nrm(ks[4], (N_META, D), 1.0),
        "norm_mix": 1.0 + nrm(ks[5], (DEPTH, D), 0.05),
        "norm_ffn": 1.0 + nrm(ks[6], (DEPTH, D), 0.05),
        "norm_final": 1.0 + nrm(ks[7], (D,), 0.05),
        "conv_w_in": nrm(ks[8], (N_CONV_LAYERS, D, 3 * D), D ** -0.5),
        "conv_w_dw": nrm(ks[9], (N_CONV_LAYERS, CONV_WIDTH, D), CONV_WIDTH ** -0.5),
        "conv_w_out": nrm(ks[10], (N_CONV_LAYERS, D, D), D ** -0.5),
        "pool_w": nrm(ks[11], (N_POOL_LAYERS, N_POOL_GROUPS, POOL_GROUP_DIM, POOL_GROUP_DIM), POOL_GROUP_DIM ** -0.5),
        "pool_scale": 1.0 + nrm(ks[12], (N_POOL_LAYERS, D), 0.1),
        "ffn_w_gate_up": nrm(ks[13], (DEPTH, D, 2 * F), D ** -0.5),
        "ffn_w_down": nrm(ks[14], (DEPTH, F, D), F ** -0.5),
    }


def reference(x_prompt, x_sample, state_conv, state_pool, meta_tokens, norm_mix, norm_ffn, norm_final,
              conv_w_in, conv_w_dw, conv_w_out, pool_w, pool_scale, ffn_w_gate_up, ffn_w_down):
    weights = (norm_mix, norm_ffn, norm_final, conv_w_in, conv_w_dw, conv_w_out,
               pool_w, pool_scale, ffn_w_gate_up, ffn_w_down)
    B = x_prompt.shape[0]
    meta = jnp.broadcast_to(meta_tokens.astype(x_prompt.dtype)[None], (B, N_META, D_MODEL))
    xp = jnp.concatenate([meta, x_prompt], axis=1)
    zc = jnp.zeros((N_CONV_LAYERS, B, CONV_HIST, D_MODEL), x_prompt.dtype)
    zp = jnp.zeros((N_POOL_LAYERS, B, POOL_HIST, D_MODEL), x_prompt.dtype)
    yp, new_conv_prompt, new_pool_prompt = trunk(xp, zc, zp, 0, *weights)
    y_prompt = yp[:, N_META:]
    y_sample, new_conv_sample, new_pool_sample = trunk(x_sample, state_conv, state_pool, PAST_LEN, *weights)
    return (y_prompt, y_sample, new_conv_prompt, new_pool_prompt, new_conv_sample, new_pool_sample)
```

```python
import numpy as np
import concourse.bass as bass
import concourse.mybir as mybir
from concourse.bass_utils import run_bass_kernel_spmd

F32 = mybir.dt.float32
BF16 = mybir.dt.bfloat16
ALU = mybir.AluOpType
AF = mybir.ActivationFunctionType

N_CORES = 8
N_META = 16
HALO = 18
T = 8
CH = 2
PH = 15
EPS = 1e-6
IW = 48
OB = 1536


def make_cfg(D, F, SEQ, DEC_BATCH, PC):
    cfg = dict(D=D, F=F, SEQ=SEQ, DEC_BATCH=DEC_BATCH, PC=PC)
    cfg["NPO"] = (N_META + SEQ) // 2
    cfg["NSEQ"] = DEC_BATCH // N_CORES
    return cfg


FULL_CFG = make_cfg(2048, 5632, 2048, 128, 8)


class Prog:
    ENG = ("pe", "act", "dve", "pool", "sp")

    def __init__(self):
        self.streams = {e: [] for e in self.ENG}
        self.cnt = {}
        self.last_w = {}
        self.readers = {}
        self.waited = {e: {} for e in self.ENG}
        self.pending = {e: [] for e in self.ENG}
        self.semnames = []

    def _sem(self, name):
        if name not in self.cnt:
            self.cnt[name] = 0
            self.semnames.append(name)
        return name

    def op(self, eng, fn, reads=(), writes=(), sem=None, inc=None, signal=True, after=()):
        own = eng if eng in ("pe", "act", "dve", "pool") else None
        if sem is None:
            sem = own
            inc = 1
        self._sem(sem)
        deps = {}

        def add(d, raw):
            if d is None:
                return
            s, v = d
            if s == own and eng == "pe":
                return
            if v > deps.get(s, 0):
                deps[s] = v

        for k in reads:
            add(self.last_w.get(k), True)
        for k in after:
            add(self.last_w.get(k), True)
        for k in writes:
            add(self.last_w.get(k), False)
            for r in self.readers.get(k, {}).items():
                add(r, False)
        waits = []
        wd = self.waited[eng]
        for s, v in deps.items():
            if wd.get(s, 0) < v:
                wd[s] = v
                waits.append((s, v))
        self.pending[eng].append((tuple(reads), tuple(writes)))
        incspec = None
        if signal:
            self.cnt[sem] += inc
            val = self.cnt[sem]
            incspec = (sem, inc)
            for rd, wr in self.pending[eng]:
                for k in rd:
                    d = self.readers.setdefault(k, {})
                    if d.get(sem, 0) < val:
                        d[sem] = val
                for k in wr:
                    self.last_w[k] = (sem, val)
                    self.readers[k] = {}
            self.pending[eng] = []
        self.streams[eng].append((waits, fn, incspec))

    def alias(self, new_keys, old_keys):
        for nk in new_keys:
            rd = self.readers.setdefault(nk, {})
            for ok in old_keys:
                for s, v in self.readers.get(ok, {}).items():
                    if rd.get(s, 0) < v:
                        rd[s] = v
                lw = self.last_w.get(ok)
                if lw is not None and rd.get(lw[0], 0) < lw[1]:
                    rd[lw[0]] = lw[1]

    def final_wait(self, eng, sems):
        waits = [(s, self.cnt[s]) for s in sems if self.cnt.get(s, 0) > 0]
        self.streams[eng].append((waits, None, None))


def build_program(cfg):
    D, F, NPO, NSEQ, PC = cfg["D"], cfg["F"], cfg["NPO"], cfg["NSEQ"], cfg["PC"]
    DC, FC = D // 128, F // 128
    GC = DC // 4
    NP = HALO + NPO
    NS = NSEQ * T
    TOT = NP + NS
    NOUT = NPO + NS
    assert TOT % 2 == 0 and TOT <= OB
    TTs = [(c0, min(512, TOT - c0)) for c0 in range(0, TOT, 512)]
    UWP = CH + NP
    PBL = PH + NP + NSEQ * (PH + T)
    SH = (NSEQ // 2) * PH
    assert SH <= 128 and SH % 2 == 0 and NSEQ * CH <= 128
    NV = 9
    WSLOT = max(DC * 256, PC * 512, GC * 512)
    NWS = 3

    nc = bass.Bass("TRN2", target_bir_lowering=False)

    def din(name, shape):
        return nc.dram_tensor(name, list(shape), F32, kind="ExternalInput").ap()

    def dout(name, shape):
        return nc.dram_tensor(name, list(shape), F32, kind="ExternalOutput").ap()

    xin = din("xin", [TOT, D])
    sconv = din("sconv", [NSEQ * CH, D])
    spool = din("spool", [NSEQ * PH, D])
    vcd = din("vcols", [128, NV * DC])
    invd = din("invc", [128, 4 * IW])
    identd = din("ident", [128, 128])
    w_in = din("w_in", [D, 3 * D])
    w_out = din("w_out", [D, D])
    w_pool = din("w_pool", [4, D // 4, D // 4])
    w_gu = din("w_gu", [2, D, 2 * F])
    w_dn = din("w_dn", [2, F, D])
    y = dout("y", [NOUT, D])
    ocp = dout("ocp", [CH, D])
    opp = dout("opp", [PH, D])
    ocs = dout("ocs", [NSEQ * CH, D])
    ops = dout("ops", [NSEQ * PH, D])

    from contextlib import ExitStack
    with ExitStack() as es:
        def sb(name, shape, dt=F32):
            return es.enter_context(nc.sbuf_tensor(name, list(shape), dt))

        hT = sb("hT", [128, DC, TOT])
        nT = sb("nT", [128, DC, TOT], BF16)
        aT = sb("aT", [128, PC, TOT], BF16)
        wr = [sb(f"wr{s}", [128, WSLOT], BF16) for s in range(NWS)]
        acc = sb("acc", [128, TOT])
        rstd = sb("rstd", [128, TOT])
        tA = [sb(f"tA{i}", [128, TOT]) for i in range(2)]
        SCR = max(2 * UWP + 2 * NSEQ * (CH + T) + 2 * TOT, 4 * PBL)
        scr = sb("scr", [128, SCR])
        vc = sb("vc", [128, NV * DC])
        invc = sb("invc_sb", [128, 4 * IW])
        ident = sb("ident_sb", [128, 128])
        ones = sb("ones", [128, 128])
        rcols = sb("rcols", [128, 16])
        sscol = sb("sscol", [128, 16])
        sscol2 = sb("sscol2", [128, 32])
        rcol = sb("rcol", [128, 16])
        diag = [sb(f"diag{i}", [128, 128]) for i in range(2)]
        sc_t = [sb(f"sc_t{i}", [128, 128]) for i in range(2)]
        sp_t = [sb(f"sp_t{i}", [128, 2, 128]) for i in range(2)]
        ost = [sb(f"ost{i}", [128, 128]) for i in range(4)]
        ostg = [sb(f"ostg{i}", [128, 128]) for i in range(4)]
        ps = es.enter_context(nc.psum_tensor("ps", [128, 8 * 512], F32))
        sems = {}
        block = None

        o = 0
        ubP = []
        for i in range(2):
            ubP.append(scr[:, o:o + UWP]); o += UWP
        ubS = []
        for i in range(2):
            ubS.append(scr[:, o:o + NSEQ * (CH + T)].rearrange("p (s t) -> p s t", t=CH + T)); o += NSEQ * (CH + T)
        cvb = []
        for i in range(2):
            cvb.append(scr[:, o:o + TOT]); o += TOT
        L0_KEYS = ["ubP0", "ubP1", "ubS0", "ubS1", "cvP0", "cvP1", "cvS0", "cvS1"]
        pbs = [scr[:, 0:PBL], scr[:, PBL:2 * PBL]]
        t1 = scr[:, 2 * PBL:3 * PBL]
        t2 = scr[:, 3 * PBL:4 * PBL]
        L1_KEYS = ["pb0", "pb1", "pbh0", "pbh1", "t1", "t2"]
        HD = D // 2
        if PC * TOT // 2 >= 2 * D:
            aT32 = aT[:].bitcast(F32)
            aTflat = aT32.rearrange("p a b -> p (a b)")
            xh = [aTflat[:, q * HD:(q + 1) * HD] for q in range(4)]
        else:
            xts = sb("xts", [128, 2 * D])
            xh = [xts[:, q * HD:(q + 1) * HD] for q in range(4)]
        XT_KEYS = [("xt", q) for q in range(4)]
        AT_KEYS = [("aT", k) for k in range(PC)]

        def vcol(v, j):
            return vc[:, v * DC + j: v * DC + j + 1]

        def psO(X):
            return ps[:, X * OB:(X + 1) * OB]

        def plan(P, units, recording):
            op = P.op
            deferred = []

            MISC = [ps[:, 3072:3584], ps[:, 3584:4096]]
            misc_i = [0]

            def next_misc():
                misc_i[0] ^= 1
                return misc_i[0], MISC[misc_i[0]]

            pso_i = [0]

            def next_pso():
                pso_i[0] ^= 1
                return pso_i[0]

            ucount = [0]
            issued = [0]

            def load_unit(src, nk, ncols):
                i = ucount[0]
                ucount[0] += 1
                if recording:
                    units.append((src, nk, ncols))
                hi = i if recording else min(i + NWS - 1, len(units) - 1)
                while issued[0] <= hi:
                    n_ = issued[0]
                    issued[0] += 1
                    usrc, unk, uncols = units[n_]
                    s_ = n_ % NWS
                    uview = wr[s_][:, 0:unk * uncols].rearrange("p (k n) -> p k n", n=uncols)
                    op("pool", lambda e, uview=uview, usrc=usrc: e.dma_start(out=uview, in_=usrc),
                       writes=[("w", s_)], sem=f"w{s_}", inc=16)
                s = i % NWS
                view = wr[s][:, 0:nk * ncols].rearrange("p (k n) -> p k n", n=ncols)
                return view, ("w", s)

            def prefetch_first():
                if recording:
                    return
                while issued[0] < min(NWS, len(units)):
                    n_ = issued[0]
                    issued[0] += 1
                    usrc, unk, uncols = units[n_]
                    s_ = n_ % NWS
                    uview = wr[s_][:, 0:unk * uncols].rearrange("p (k n) -> p k n", n=uncols)
                    op("pool", lambda e, uview=uview, usrc=usrc: e.dma_start(out=uview, in_=usrc),
                       writes=[("w", s_)], after=[("xt", 0), ("xt", 1), ("xt", 2), ("xt", 3)], sem=f"w{s_}", inc=16)

            def flush_deferred():
                while deferred:
                    deferred.pop(0)()

            CL = [0]

            def tts():
                lo = CL[0]
                return [(c0, min(512, TOT - lo - c0)) for c0 in range(0, TOT - lo, 512)]

            def mm_terms(X, unit, ukey, col0, terms):
                TTl = tts()
                lo = CL[0]
                nmm = len(terms) * len(TTl)
                i = 0
                for idx, (ku, act, akey, ka) in enumerate(terms):
                    for (c0, n) in TTl:
                        i += 1
                        op("pe", lambda e, X=X, ku=ku, ka=ka, act=act, c0=c0, n=n, lo=lo, st=(idx == 0), sp_=(idx == len(terms) - 1):
                           e.matmul(out=ps[:, X * OB + c0: X * OB + c0 + n], lhsT=unit[:, ku, col0:col0 + 128],
                                    rhs=act[:, ka, lo + c0:lo + c0 + n], start=st, stop=sp_),
                           reads=[ukey] + [(ak_, ka) for ak_ in (akey if isinstance(akey, tuple) else (akey,))],
                           writes=[("ps", X)], signal=(i == nmm))
                flush_deferred()

            def mm_chunk(X, unit, ukey, col0, act, akey, ks):
                mm_terms(X, unit, ukey, col0, [(ku, act, akey, ka) for (ku, ka) in ks])

            def mm_pair(Xs, unit, ukey, act, akey, ks):
                TTl = tts()
                lo = CL[0]
                for idx, (ku, ka) in enumerate(ks):
                    for jj, X in enumerate(Xs):
                        for ti, (c0, n) in enumerate(TTl):
                            op("pe", lambda e, X=X, jj=jj, ku=ku, ka=ka, c0=c0, n=n, lo=lo, st=(idx == 0), sp_=(idx == len(ks) - 1):
                               e.matmul(out=ps[:, X * OB + c0: X * OB + c0 + n], lhsT=unit[:, ku, jj * 128:(jj + 1) * 128],
                                        rhs=act[:, ka, lo + c0:lo + c0 + n], start=st, stop=sp_),
                               reads=[ukey, (akey, ka)], writes=[("ps", X)],
                               signal=(idx == len(ks) - 1 and ti == len(TTl) - 1))
                flush_deferred()

            op("sp", lambda e: e.dma_start(out=vc[:, :], in_=vcd), writes=["vc"], sem="c0", inc=16)
            op("sp", lambda e: e.dma_start(out=ident[:, :], in_=identd), writes=["ident"], sem="c1", inc=16)
            op("sp", lambda e: e.dma_start(out=invc[:, :], in_=invd), writes=["invc"], sem="c2", inc=16)
            op("dve", lambda e: e.memset(ones[:, :], 1.0), writes=["ones"])
            op("dve", lambda e: e.memset(scr[:, :], 0.0), writes=L0_KEYS)

            P.alias(XT_KEYS, AT_KEYS)
            ntile = (TOT + 127) // 128
            evac_i = [0]
            HC = DC // 2
            GS = min(4, HC)

            def evac(fn_act, fn_dve, reads, writes):
                evac_i[0] ^= 1
                if evac_i[0]:
                    op("act", fn_act, reads=reads, writes=writes)
                else:
                    op("dve", fn_dve, reads=reads, writes=writes)

            junk = nT[:].rearrange("p a b -> p (a b)")[:, 0:HD]
            op("dve", lambda e: e.memset(sscol2[:, :], 1.0), writes=["sscol2"])
            for i in range(ntile):
                r0 = i * 128
                nr = min(128, TOT - r0)
                if i == (ntile * 6) // 10:
                    prefetch_first()
                for hf in range(2):
                    q = (2 * i + hf) % 4
                    op("sp", lambda e, q=q, r0=r0, nr=nr, hf=hf: e.dma_start(out=xh[q][0:nr, :],
                                                                             in_=xin[r0:r0 + nr, hf * HD:(hf + 1) * HD]),
                       writes=[("xt", q)], sem=f"x{q}", inc=16)
                    op("act", lambda e, q=q, i=i, hf=hf, nr=nr: e.activation(out=junk[0:nr, :], in_=xh[q][0:nr, :], func=AF.Square,
                                                                             accum_out=sscol2[0:nr, 2 * i + hf:2 * i + hf + 1]),
                       reads=[("xt", q)], writes=["sscol2", "junk"])
                    for jb in range(0, HC, GS):
                        mi, mb = next_misc()
                        for jj in range(GS):
                            jl = jb + jj
                            op("pe", lambda e, mb=mb, jj=jj, jl=jl, q=q, nr=nr:
                               e.transpose(out=mb[:, jj * 128: jj * 128 + nr], in_=xh[q][0:nr, jl * 128:(jl + 1) * 128],
                                           identity=ident[0:nr, 0:nr]),
                               reads=[("xt", q), "ident"], writes=[("psm", mi)], signal=(jj == GS - 1))
                        src = mb[:, 0:GS * 128].rearrange("p (a b) -> p a b", b=128)[:, :, 0:nr]
                        j0 = hf * HC + jb
                        dst = hT[:, j0:j0 + GS, r0:r0 + nr]
                        evac(lambda e, dst=dst, src=src: e.copy(out=dst, in_=src),
                             lambda e, dst=dst, src=src: e.tensor_copy(out=dst, in_=src),
                             reads=[("psm", mi)], writes=[("hT", j0 + q_) for q_ in range(GS)])
            P.alias(AT_KEYS, XT_KEYS)
            s2v = sscol2[:, 0:2 * ntile].rearrange("p (a b) -> p a b", b=2)
            op("dve", lambda e: e.tensor_tensor(out=sscol[:, 0:ntile], in0=s2v[:, :, 0], in1=s2v[:, :, 1], op=ALU.add),
               reads=["sscol2"], writes=["sscol"])

            def stats_chunk(j):
                b = tA[j % 2]
                bk = f"tA{j % 2}"
                lo = CL[0]
                op("act", lambda e, b=b, j=j, lo=lo: e.activation(out=b[:, lo:TOT], in_=hT[:, j, lo:TOT], func=AF.Square),
                   reads=[("hT", j)], writes=[bk])
                if j == 0:
                    op("dve", lambda e, b=b, lo=lo: e.tensor_copy(out=acc[:, lo:TOT], in_=b[:, lo:TOT]), reads=[bk], writes=["acc"])
                else:
                    op("dve", lambda e, b=b, lo=lo: e.tensor_tensor(out=acc[:, lo:TOT], in0=acc[:, lo:TOT], in1=b[:, lo:TOT], op=ALU.add),
                       reads=[bk, "acc"], writes=["acc"])

            NTT = (TOT + 127) // 128

            def col_sums(c_off, ntl, ncols_total):
                mi, mb = next_misc()
                for i in range(ntl):
                    r0 = i * 128
                    nr = min(128, ncols_total - r0)
                    op("pe", lambda e, mb=mb, i=i, r0=r0, nr=nr: e.matmul(out=mb[0:nr, 2 * i:2 * i + 2],
                                                                          lhsT=acc[:, c_off + r0:c_off + r0 + nr],
                                                                          rhs=ones[:, 0:2], start=True, stop=True),
                       reads=["acc", "ones"], writes=[("psm", mi)], signal=(i == ntl - 1))
                op("dve", lambda e, mb=mb: e.tensor_copy(out=sscol[:, 0:ntl],
                                                         in_=mb[:, 0:2 * ntl].rearrange("p (a b) -> p a b", b=2)[:, :, 0]),
                   reads=[("psm", mi)], writes=["sscol"])

            def rcol_from_sscol(ntl):
                op("dve", lambda e: e.tensor_scalar(out=rcol[:, 0:ntl], in0=sscol[:, 0:ntl], scalar1=1.0 / D, scalar2=EPS,
                                                    op0=ALU.mult, op1=ALU.add), reads=["sscol"], writes=["rcol"])
                op("act", lambda e: e.activation(out=rcol[:, 0:ntl], in_=rcol[:, 0:ntl], func=AF.Sqrt),
                   reads=["rcol"], writes=["rcol"])
                op("dve", lambda e: e.reciprocal(out=rcol[:, 0:ntl], in_=rcol[:, 0:ntl]), reads=["rcol"], writes=["rcol"])

            def rstd_rows():
                X = next_pso()
                for i in range(NTT):
                    r0 = i * 128
                    nr = min(128, TOT - r0)
                    dg = diag[i % 2]
                    op("dve", lambda e, dg=dg, i=i, nr=nr: e.tensor_scalar(out=dg[0:nr, 0:nr], in0=ident[0:nr, 0:nr],
                                                                           scalar1=rcol[0:nr, i:i + 1], scalar2=None, op0=ALU.mult),
                       reads=["rcol", "ident"], writes=[("diag", i % 2)])
                    op("pe", lambda e, X=X, dg=dg, r0=r0, nr=nr: e.matmul(out=ps[:, X * OB + r0: X * OB + r0 + nr],
                                                                          lhsT=ones[0:nr, :], rhs=dg[0:nr, 0:nr],
                                                                          start=True, stop=True),
                       reads=[("diag", i % 2), "ones"], writes=[("ps", X)])
                op("act", lambda e, X=X: e.copy(out=rstd[:, :], in_=psO(X)[:, 0:TOT]), reads=[("ps", X)], writes=["rstd"])

            def stats_finish():
                col_sums(0, NTT, TOT)
                rcol_from_sscol(NTT)
                rstd_rows()

            def apply_norm(v):
                for j in range(DC):
                    op("dve", lambda e, j=j, lo=CL[0]: e.scalar_tensor_tensor(out=nT[:, j, lo:TOT], in0=hT[:, j, lo:TOT], scalar=vcol(v, j),
                                                                           in1=rstd[:, lo:TOT], op0=ALU.mult, op1=ALU.mult),
                       reads=[("hT", j), "rstd", "vc"], writes=[("nT", j)])

            def resid_matmul(unit_src_fn, nk, last, gscale=None):
                for ob in range(D // 512):
                    unit, ukey = load_unit(unit_src_fn(ob), nk, 512)
                    for jo in range(4):
                        j = ob * 4 + jo
                        X = next_pso()
                        mm_chunk(X, unit, ukey, jo * 128, aT, "aT", [(k, k) for k in range(nk)])
                        op("dve", lambda e, j=j, X=X, lo=CL[0]: e.tensor_tensor(out=hT[:, j, lo:TOT], in0=hT[:, j, lo:TOT],
                                                                                in1=psO(X)[:, 0:TOT - lo], op=ALU.add),
                           reads=[("ps", X), ("hT", j)], writes=[("hT", j)])
                        if last:
                            stats_chunk(j)
                            if gscale is not None:
                                op("act", lambda e, j=j, lo=CL[0]: e.activation(out=hT[:, j, lo:TOT], in_=hT[:, j, lo:TOT], func=AF.Copy,
                                                                             scale=vcol(gscale, j)),
                                   reads=[("hT", j), "vc"], writes=[("hT", j)])

            def ffn(layer, gscale=None):
                parts = []
                f0 = 0
                nparts = (FC + PC - 1) // PC
                base, rem = FC // nparts, FC % nparts
                for pi in range(nparts):
                    n = base + (1 if pi < rem else 0)
                    parts.append((f0, n))
                    f0 += n
                for pi, (f0, n) in enumerate(parts):
                    for q0 in range(0, n, 2):
                        nq = min(2, n - q0)
                        fa = f0 + q0
                        gsrc = w_gu[layer, :, fa * 128:(fa + nq) * 128].rearrange("(k p) n -> p k n", p=128)
                        usrc = w_gu[layer, :, F + fa * 128:F + (fa + nq) * 128].rearrange("(k p) n -> p k n", p=128)
                        gun, gk = load_unit(gsrc, DC, nq * 128)
                        gXs = [next_pso() for jj in range(nq)]
                        first = (pi == 0 and q0 == 0 and nq == 2)
                        if first:
                            mm_pair(gXs, gun, gk, nT, "nT", [(k, k) for k in range(DC)])
                        for jj in range(nq):
                            X = gXs[jj]
                            if not first:
                                mm_chunk(X, gun, gk, jj * 128, nT, "nT", [(k, k) for k in range(DC)])
                            op("act", lambda e, jj=jj, X=X, lo=CL[0]: e.activation(out=tA[jj][:, lo:TOT], in_=psO(X)[:, 0:TOT - lo],
                                                                                func=AF.Silu),
                               reads=[("ps", X)], writes=[f"tA{jj}"])
                        uun, uk = load_unit(usrc, DC, nq * 128)
                        for jj in range(nq):
                            X = next_pso()
                            mm_chunk(X, uun, uk, jj * 128, nT, "nT", [(k, k) for k in range(DC)])
                            fl = q0 + jj
                            op("dve", lambda e, jj=jj, X=X, fl=fl, lo=CL[0]: e.tensor_tensor(out=aT[:, fl, lo:TOT], in0=tA[jj][:, lo:TOT],
                                                                                            in1=psO(X)[:, 0:TOT - lo], op=ALU.mult),
                               reads=[("ps", X), f"tA{jj}"], writes=[("aT", fl)])
                    resid_matmul(lambda ob, f0=f0, n=n: w_dn[layer, f0 * 128:(f0 + n) * 128, ob * 512:(ob + 1) * 512]
                                 .rearrange("(k p) n -> p k n", p=128), n, last=(pi == len(parts) - 1),
                                 gscale=(gscale if pi == len(parts) - 1 else None))

            so_i = [0]

            def state_out(src_ap, keys, ncols, dsts):
                def go():
                    mi, mb = next_misc()
                    op("pe", lambda e, mb=mb: e.transpose(out=mb[0:ncols, 0:128], in_=src_ap, identity=ident[:, :]),
                       reads=list(keys) + ["ident"], writes=[("psm", mi)])
                    s = so_i[0] % 4
                    so_i[0] += 1
                    op("act", lambda e, s=s, mb=mb: e.copy(out=ostg[s][0:ncols, :], in_=mb[0:ncols, 0:128]),
                       reads=[("psm", mi)], writes=[("ostg", s)])
                    for (r0, nrw, dst) in dsts:
                        op("sp", lambda e, s=s, r0=r0, nrw=nrw, dst=dst: e.dma_start(out=dst, in_=ostg[s][r0:r0 + nrw, :]),
                           reads=[("ostg", s)], sem=f"so{s}", inc=16)
                deferred.append(go)

            ost_i = [0]

            def next_ost():
                ost_i[0] = (ost_i[0] + 1) % len(ost)
                return ost[ost_i[0]], ("ost", ost_i[0])

            P.alias([("nT", j) for j in range(DC)], ["junk"])
            prefetch_first()
            rcol_from_sscol(NTT)
            rstd_rows()
            apply_norm(0)

            nparts0 = (DC + PC - 1) // PC
            assert DC % nparts0 == 0
            pc0 = DC // nparts0
            full_k = [(k, k) for k in range(DC)]
            for pa in range(nparts0):
                for q0 in range(0, pc0, 2):
                    nq = min(2, pc0 - q0)
                    ja = pa * pc0 + q0
                    csrc = w_in[:, D + ja * 128: D + (ja + nq) * 128].rearrange("(k p) n -> p k n", p=128)
                    vsrc = w_in[:, 2 * D + ja * 128: 2 * D + (ja + nq) * 128].rearrange("(k p) n -> p k n", p=128)
                    bsrc = w_in[:, ja * 128:(ja + nq) * 128].rearrange("(k p) n -> p k n", p=128)
                    for jj in range(nq):
                        j = ja + jj
                        s = j % 2
                        op("sp", lambda e, s=s, j=j: e.dma_start(out=sc_t[s][0:NSEQ * CH, :], in_=sconv[:, j * 128:(j + 1) * 128]),
                           writes=[("sc_t", s)], sem=f"sc{s}", inc=16)
                        mi, mb = next_misc()
                        op("pe", lambda e, mb=mb, s=s: e.transpose(out=mb[:, 0:NSEQ * CH], in_=sc_t[s][0:NSEQ * CH, :],
                                                                   identity=ident[0:NSEQ * CH, 0:NSEQ * CH]),
                           reads=[("sc_t", s), "ident"], writes=[("psm", mi)])
                        op("act", lambda e, mb=mb, jj=jj: e.copy(out=ubS[jj][:, :, 0:CH],
                                                                 in_=mb[:, 0:NSEQ * CH].rearrange("p (s r) -> p s r", r=CH)),
                           reads=[("psm", mi)], writes=[f"ubS{jj}"])
                    cun, ck = load_unit(csrc, DC, nq * 128)
                    cXs = [next_pso() for jj in range(nq)]
                    if pa == 0 and q0 == 0 and nq == 2:
                        mm_pair(cXs, cun, ck, nT, "nT", full_k)
                    for jj in range(nq):
                        X = cXs[jj]
                        if not (pa == 0 and q0 == 0 and nq == 2):
                            mm_chunk(X, cun, ck, jj * 128, nT, "nT", full_k)
                        op("act", lambda e, jj=jj, X=X: e.copy(out=tA[jj][:, :], in_=psO(X)[:, 0:TOT]),
                           reads=[("ps", X)], writes=[f"tA{jj}"])
                    vun, vk = load_unit(vsrc, DC, nq * 128)
                    for jj in range(nq):
                        j = ja + jj
                        X = next_pso()
                        mm_chunk(X, vun, vk, jj * 128, nT, "nT", full_k)
                        op("dve", lambda e, jj=jj, X=X: e.tensor_tensor(out=ubP[jj][:, CH:CH + NP], in0=tA[jj][:, 0:NP],
                                                                        in1=psO(X)[:, 0:NP], op=ALU.mult),
                           reads=[("ps", X), f"tA{jj}"], writes=[f"ubP{jj}"])
                        op("dve", lambda e, jj=jj, X=X: e.tensor_tensor(
                            out=ubS[jj][:, :, CH:CH + T], in0=tA[jj][:, NP:TOT].rearrange("p (s t) -> p s t", t=T),
                            in1=psO(X)[:, NP:TOT].rearrange("p (s t) -> p s t", t=T), op=ALU.mult),
                           reads=[("ps", X), f"tA{jj}"], writes=[f"ubS{jj}"])
                        cvP = cvb[jj][:, 0:NP]
                        cvS = cvb[jj][:, NP:TOT].rearrange("p (s t) -> p s t", t=T)
                        op("act", lambda e, jj=jj, j=j, cvP=cvP: e.activation(out=cvP, in_=ubP[jj][:, 2:2 + NP], func=AF.Copy,
                                                                              scale=vcol(7, j)),
                           reads=[f"ubP{jj}", "vc"], writes=[f"cvP{jj}"])
                        op("act", lambda e, jj=jj, j=j, cvS=cvS: e.activation(out=cvS, in_=ubS[jj][:, :, 2:2 + T], func=AF.Copy,
                                                                              scale=vcol(7, j)),
                           reads=[f"ubS{jj}", "vc"], writes=[f"cvS{jj}"])
                        for tap in (1, 0):
                            op("dve", lambda e, jj=jj, j=j, tap=tap, cvP=cvP: e.scalar_tensor_tensor(
                                out=cvP, in0=ubP[jj][:, tap:tap + NP], scalar=vcol(5 + tap, j), in1=cvP,
                                op0=ALU.mult, op1=ALU.add),
                               reads=[f"ubP{jj}", f"cvP{jj}", "vc"], writes=[f"cvP{jj}"])
                            op("dve", lambda e, jj=jj, j=j, tap=tap, cvS=cvS: e.scalar_tensor_tensor(
                                out=cvS, in0=ubS[jj][:, :, tap:tap + T], scalar=vcol(5 + tap, j), in1=cvS,
                                op0=ALU.mult, op1=ALU.add),
                               reads=[f"ubS{jj}", f"cvS{jj}", "vc"], writes=[f"cvS{jj}"])
                        so, sok = next_ost()
                        op("act", lambda e, jj=jj, so=so: e.copy(out=so[:, 0:CH], in_=ubP[jj][:, NP:NP + CH]),
                           reads=[f"ubP{jj}"], writes=[sok])
                        op("act", lambda e, jj=jj, so=so: e.copy(
                            out=so[:, CH:CH + NSEQ * CH].rearrange("p (s r) -> p s r", r=CH), in_=ubS[jj][:, :, T:T + CH]),
                           reads=[f"ubS{jj}"], writes=[sok])
                        state_out(so[:, 0:CH + NSEQ * CH], [sok], CH + NSEQ * CH,
                                  [(0, CH, ocp[:, j * 128:(j + 1) * 128]),
                                   (CH, NSEQ * CH, ocs[:, j * 128:(j + 1) * 128])])
                    bun, bk = load_unit(bsrc, DC, nq * 128)
                    for jj in range(nq):
                        X = next_pso()
                        mm_chunk(X, bun, bk, jj * 128, nT, "nT", full_k)
                        gl = q0 + jj
                        op("dve", lambda e, jj=jj, X=X, gl=gl: e.tensor_tensor(out=aT[:, gl, :], in0=cvb[jj][:, :],
                                                                              in1=psO(X)[:, 0:TOT], op=ALU.mult),
                           reads=[("ps", X), f"cvP{jj}", f"cvS{jj}"], writes=[("aT", gl)])
                resid_matmul(lambda ob, pa=pa: w_out[pa * pc0 * 128:(pa + 1) * pc0 * 128, ob * 512:(ob + 1) * 512]
                             .rearrange("(k p) n -> p k n", p=128), pc0, last=(pa == nparts0 - 1))
            stats_finish()
            apply_norm(2)
            ffn(0)

            stats_finish()
            flush_deferred()
            P.alias(L1_KEYS, L0_KEYS)
            for j in range(DC):
                P.alias([("nTfix", j)], [("nT", j)])
            SW = PH + T
            for pbx, pbk in zip(pbs, ("pbh0", "pbh1")):
                op("dve", lambda e, pbx=pbx: e.memset(pbx[:, 0:PH], 0.0), writes=[pbk])
            def spool_dma(j):
                s = j % 2
                for hh in range(2):
                    op("sp", lambda e, s=s, hh=hh, j=j: e.dma_start(out=sp_t[s][0:SH, hh, :],
                                                                    in_=spool[hh * SH:(hh + 1) * SH, j * 128:(j + 1) * 128]),
                       writes=[("sp_t", s)], sem=f"spl{s}", inc=16)

            def pool_prep(j):
                s = j % 2
                pb, pbk, pbh = pbs[s], f"pb{s}", f"pbh{s}"
                pbS = pb[:, PH + NP: PBL].rearrange("p (s t) -> p s t", t=SW)
                if j == 0:
                    spool_dma(0)
                if j + 1 < DC:
                    spool_dma(j + 1)
                mi, mb = next_misc()
                for hh in range(2):
                    op("pe", lambda e, mb=mb, s=s, hh=hh: e.transpose(out=mb[:, hh * SH:(hh + 1) * SH], in_=sp_t[s][0:SH, hh, :],
                                                                     identity=ident[0:SH, 0:SH]),
                       reads=[("sp_t", s), "ident"], writes=[("psm", mi)], signal=(hh == 1))
                op("act", lambda e, mb=mb, pbS=pbS: e.copy(out=pbS[:, :, 0:PH],
                                                           in_=mb[:, 0:2 * SH].rearrange("p (s r) -> p s r", r=PH)),
                   reads=[("psm", mi)], writes=[pbh])
                op("dve", lambda e, j=j, pb=pb: e.scalar_tensor_tensor(out=pb[:, PH:PH + NP], in0=hT[:, j, 0:NP], scalar=vcol(1, j),
                                                                       in1=rstd[:, 0:NP], op0=ALU.mult, op1=ALU.mult),
                   reads=[("hT", j), "rstd", "vc"], writes=[pbk])
                op("dve", lambda e, j=j, pbS=pbS: e.scalar_tensor_tensor(
                    out=pbS[:, :, PH:SW], in0=hT[:, j, NP:TOT].rearrange("p (s t) -> p s t", t=T), scalar=vcol(1, j),
                    in1=rstd[:, NP:TOT].rearrange("p (s t) -> p s t", t=T), op0=ALU.mult, op1=ALU.mult),
                   reads=[("hT", j), "rstd", "vc"], writes=[pbk])
                sl = ((j // GC) % 2) * GC + (j % GC)
                op("act", lambda e, sl=sl, pb=pb: e.activation(out=aT[:, sl, 0:NP], in_=pb[:, PH:PH + NP], func=AF.Copy, scale=-1.0),
                   reads=[pbk], writes=[("aT", sl)])
                op("act", lambda e, sl=sl, pbS=pbS: e.activation(out=aT[:, sl, NP:TOT].rearrange("p (s t) -> p s t", t=T),
                                                                 in_=pbS[:, :, PH:SW], func=AF.Copy, scale=-1.0),
                   reads=[pbk], writes=[("aT", sl)])
                so, sok = next_ost()
                op("act", lambda e, so=so, pb=pb: e.copy(out=so[:, 0:PH], in_=pb[:, NP:NP + PH]), reads=[pbk], writes=[sok])
                state_out(so[:, 0:PH], [sok], PH, [(0, PH, opp[:, j * 128:(j + 1) * 128])])
                for hh in range(2):
                    so, sok = next_ost()
                    op("act", lambda e, so=so, hh=hh, pbS=pbS: e.copy(
                        out=so[:, 0:SH].rearrange("p (s r) -> p s r", r=PH),
                        in_=pbS[:, hh * (NSEQ // 2):(hh + 1) * (NSEQ // 2), T:SW]),
                       reads=[pbk, pbh], writes=[sok])
                    state_out(so[:, 0:SH], [sok], SH, [(0, SH, ops[hh * SH:(hh + 1) * SH, j * 128:(j + 1) * 128])])
                flush_deferred()

            last_fin = [1]

            def pool_compute(j):
                g = j // GC
                w = 2 ** (g + 1)
                s = j % 2
                pb, pbk, pbh = pbs[s], f"pb{s}", f"pbh{s}"
                pbS = pb[:, PH + NP: PBL].rearrange("p (s t) -> p s t", t=SW)
                cur, curk = pb, [pbk, pbh]
                bufs = [(t1, "t1"), (t2, "t2")]
                sh = 1
                bi = 1 - last_fin[0]
                while sh < w:
                    nb, nbk = bufs[bi % 2]
                    bi += 1
                    lo = 2 * sh - 1
                    op("dve", lambda e, nb=nb, cur=cur, lo=lo, sh=sh: e.tensor_tensor(
                        out=nb[:, lo:PBL], in0=cur[:, lo:PBL], in1=cur[:, lo - sh:PBL - sh], op=ALU.add),
                       reads=curk, writes=[nbk])
                    cur, curk = nb, [nbk]
                    sh *= 2
                last_fin[0] = (bi - 1) % 2
                curS = cur[:, PH + NP: PBL].rearrange("p (s t) -> p s t", t=SW)
                op("act", lambda e, j=j, cur=cur, w=w: e.activation(out=nT[:, j, IW:NP], in_=cur[:, PH + IW:PH + NP], func=AF.Copy,
                                                                    scale=1.0 / w),
                   reads=curk, writes=[("nT", j)])
                op("act", lambda e, j=j, curS=curS, w=w: e.activation(out=nT[:, j, NP:TOT].rearrange("p (s t) -> p s t", t=T),
                                                                      in_=curS[:, :, PH:SW], func=AF.Copy, scale=1.0 / w),
                   reads=curk, writes=[("nT", j)])
                op("dve", lambda e, j=j, cur=cur, g=g: e.tensor_tensor(out=nT[:, j, 0:IW], in0=cur[:, PH:PH + IW],
                                                                      in1=invc[:, g * IW:(g + 1) * IW], op=ALU.mult),
                   reads=curk + ["invc"], writes=[("nTfix", j)])

            mmq = []
            tailq = []

            def plan_mm(n):
                for _ in range(n):
                    if not mmq:
                        return
                    g_, jo, unit, ukey = mmq.pop(0)
                    jj_ = g_ * GC + jo
                    X = next_pso()
                    mm_terms(X, unit, ukey, jo * 128,
                             [(k, nT, ("nT", "nTfix"), g_ * GC + k) for k in range(GC)] +
                             [(k, aT, "aT", (g_ % 2) * GC + k) for k in range(GC)])

                    def tail_a(jj_=jj_, X=X):
                        op("dve", lambda e, j=jj_, X=X, lo=CL[0]: e.scalar_tensor_tensor(out=hT[:, j, lo:TOT], in0=psO(X)[:, 0:TOT - lo],
                                                                                      scalar=vcol(8, j), in1=hT[:, j, lo:TOT],
                                                                                      op0=ALU.mult, op1=ALU.add),
                           reads=[("ps", X), ("hT", jj_), "vc"], writes=[("hT", jj_)])
                    tailq.append((tail_a, jj_))

            def run_tails():
                js = []
                while tailq:
                    fn, jj_ = tailq.pop(0)
                    fn()
                    js.append(jj_)
                for jj_ in js:
                    stats_chunk(jj_)

            CL[0] = HALO
            pool_prep(0)
            for j in range(DC):
                if j + 1 < DC:
                    pool_prep(j + 1)
                pool_compute(j)
                run_tails()
                if j % GC == GC - 1:
                    g = j // GC
                    unit, ukey = load_unit(w_pool[g].rearrange("(k p) n -> p k n", p=128), GC, GC * 128)
                    for jo in range(GC):
                        mmq.append((g, jo, unit, ukey))
                plan_mm(2)
            while mmq or tailq:
                run_tails()
                plan_mm(2)
            stats_finish()
            apply_norm(3)
            ffn(1, gscale=4)

            P.alias(XT_KEYS, AT_KEYS)
            notile = (NOUT + 127) // 128
            col_sums(HALO, notile, NOUT)
            rcol_from_sscol(notile)
            for i in range(notile):
                r0 = i * 128
                nr = min(128, NOUT - r0)
                c0 = HALO + r0
                for hf in range(2):
                    q = (2 * i + hf) % 4
                    for jb in range(0, HC, GS):
                        mi, mb = next_misc()
                        for jj in range(GS):
                            j = hf * HC + jb + jj
                            op("pe", lambda e, mb=mb, jj=jj, j=j, c0=c0, nr=nr:
                               e.transpose(out=mb[0:nr, jj * 128:(jj + 1) * 128], in_=hT[:, j, c0:c0 + nr], identity=ident[:, :]),
                               reads=[("hT", j), "ident"], writes=[("psm", mi)], signal=(jj == GS - 1))
                        dst = xh[q][0:nr, jb * 128:(jb + GS) * 128]
                        src = mb[0:nr, 0:GS * 128]
                        rc = rcol[0:nr, i:i + 1]
                        evac(lambda e, dst=dst, src=src, rc=rc: e.activation(out=dst, in_=src, func=AF.Copy, scale=rc),
                             lambda e, dst=dst, src=src, rc=rc: e.tensor_scalar(out=dst, in0=src, scalar1=rc, scalar2=None, op0=ALU.mult),
                             reads=[("psm", mi), "rcol"], writes=[("xt", q)])
                    op("sp", lambda e, q=q, r0=r0, nr=nr, hf=hf: e.dma_start(out=y[r0:r0 + nr, hf * HD:(hf + 1) * HD], in_=xh[q][0:nr, :]),
                       reads=[("xt", q)], sem=f"yo{q}", inc=16)
            flush_deferred()
            P.final_wait("sp", [n for n in P.semnames if n.startswith("yo") or n.startswith("so")])

        P1 = Prog()
        units = []
        plan(P1, units, True)
        P = Prog()
        plan(P, units, False)

        for name in P.semnames:
            sems[name] = es.enter_context(nc.semaphore(name))
        with nc.Block() as block:
            def emit(ename):
                def body(eng):
                    for waits, fn, incspec in P.streams[ename]:
                        for (sname, val) in waits:
                            eng.wait_ge(sems[sname], val)
                        if fn is None:
                            continue
                        ins = fn(eng)
                        if incspec is not None:
                            ins.then_inc(sems[incspec[0]], incspec[1])
                return body
            block.tensor(emit("pe"))
            block.scalar(emit("act"))
            block.vector(emit("dve"))
            block.gpsimd(emit("pool"))
            block.sync(emit("sp"))
    return nc


def _col_layout(v, DC):
    return np.ascontiguousarray(v.reshape(DC, 128).T)


def run(cfg, x_prompt, x_sample, state_conv, state_pool, meta_tokens, norm_mix, norm_ffn, norm_final,
        conv_w_in, conv_w_dw, conv_w_out, pool_w, pool_scale, ffn_w_gate_up, ffn_w_down, trace=False):
    D, F, NPO, NSEQ = cfg["D"], cfg["F"], cfg["NPO"], cfg["NSEQ"]
    DC = D // 128
    B = x_prompt.shape[0]
    SEQ = x_prompt.shape[1]
    f32 = np.float32
    nc = build_program(cfg)

    vecs = [norm_mix[0], norm_mix[1], norm_ffn[0], norm_ffn[1], norm_final,
            conv_w_dw[0, 0], conv_w_dw[0, 1], conv_w_dw[0, 2], pool_scale[0]]
    vcols = np.ascontiguousarray(np.concatenate([_col_layout(np.asarray(v, f32), DC) for v in vecs], axis=1))
    ident = np.eye(128, dtype=f32)
    shared = dict(
        vcols=vcols, ident=ident,
        w_in=np.ascontiguousarray(conv_w_in[0], f32), w_out=np.ascontiguousarray(conv_w_out[0], f32),
        w_pool=np.ascontiguousarray(pool_w[0], f32), w_gu=np.ascontiguousarray(ffn_w_gate_up, f32),
        w_dn=np.ascontiguousarray(ffn_w_down, f32),
    )
    wins = np.array([2.0, 4.0, 8.0, 16.0], f32)
    in_maps = []
    for c in range(N_CORES):
        b, half = c // 2, c % 2
        full = np.concatenate([np.asarray(meta_tokens, f32), np.asarray(x_prompt[b], f32)], axis=0)
        start = half * NPO
        lo = start - HALO
        if lo < 0:
            seg = np.concatenate([np.zeros((-lo, D), f32), full[0:start + NPO]], axis=0)
        else:
            seg = full[lo:start + NPO]
        xs = np.asarray(x_sample[c * NSEQ:(c + 1) * NSEQ], f32).reshape(NSEQ * T, D)
        xin = np.ascontiguousarray(np.concatenate([seg, xs], axis=0))
        pos = (lo + np.arange(IW)).astype(np.float64)
        cnt = np.minimum(wins[:, None].astype(np.float64), np.maximum(pos[None, :], 0.0) + 1.0)
        invc = np.broadcast_to((1.0 / cnt).astype(f32).reshape(1, 4 * IW), (128, 4 * IW))
        m = dict(shared)
        m.update(
            xin=xin,
            sconv=np.ascontiguousarray(np.asarray(state_conv[0, c * NSEQ:(c + 1) * NSEQ], f32).reshape(NSEQ * CH, D)),
            spool=np.ascontiguousarray(np.asarray(state_pool[0, c * NSEQ:(c + 1) * NSEQ], f32).reshape(NSEQ * PH, D)),
            invc=np.ascontiguousarray(invc),
        )
        in_maps.append(m)
    res = run_bass_kernel_spmd(nc, in_maps, core_ids=list(range(N_CORES)), trace=trace)
    R = res.results
    DECB = x_sample.shape[0]
    y_prompt = np.empty((B, SEQ, D), f32)
    y_sample = np.empty((DECB, T, D), f32)
    ncp = np.empty((1, B, CH, D), f32)
    npp = np.empty((1, B, PH, D), f32)
    ncs = np.empty((1, DECB, CH, D), f32)
    nps = np.empty((1, DECB, PH, D), f32)
    for c in range(N_CORES):
        b, half = c // 2, c % 2
        yc = np.asarray(R[c]["y"])
        if half == 0:
            y_prompt[b, 0:NPO - N_META] = yc[N_META:NPO]
        else:
            y_prompt[b, NPO - N_META:] = yc[0:NPO]
            ncp[0, b] = np.asarray(R[c]["ocp"])
            npp[0, b] = np.asarray(R[c]["opp"])
        y_sample[c * NSEQ:(c + 1) * NSEQ] = yc[NPO:].reshape(NSEQ, T, D)
        ncs[0, c * NSEQ:(c + 1) * NSEQ] = np.asarray(R[c]["ocs"]).reshape(NSEQ, CH, D)
        nps[0, c * NSEQ:(c + 1) * NSEQ] = np.asarray(R[c]["ops"]).reshape(NSEQ, PH, D)
    out = (y_prompt, y_sample, ncp, npp, ncs, nps)
    if trace:
        return out, res
    return out


def kernel(x_prompt, x_sample, state_conv, state_pool, meta_tokens, norm_mix, norm_ffn, norm_final,
           conv_w_in, conv_w_dw, conv_w_out, pool_w, pool_scale, ffn_w_gate_up, ffn_w_down):
    args = [np.asarray(a) for a in (x_prompt, x_sample, state_conv, state_pool, meta_tokens, norm_mix, norm_ffn,
                                    norm_final, conv_w_in, conv_w_dw, conv_w_out, pool_w, pool_scale,
                                    ffn_w_gate_up, ffn_w_down)]
    return run(FULL_CFG, *args)
```

```python
import numpy as np
import concourse.bass as bass
import concourse.mybir as mybir
from concourse.bass_utils import run_bass_kernel_spmd

F32 = mybir.dt.float32
BF16 = mybir.dt.bfloat16
ALU = mybir.AluOpType
AF = mybir.ActivationFunctionType

N_CORES = 8
N_META = 16
HALO = 18
T = 8
CH = 2
PH = 15
EPS = 1e-6
IW = 48
OB = 1536


def make_cfg(D, F, SEQ, DEC_BATCH, PC):
    cfg = dict(D=D, F=F, SEQ=SEQ, DEC_BATCH=DEC_BATCH, PC=PC)
    cfg["NPO"] = (N_META + SEQ) // 2
    cfg["NSEQ"] = DEC_BATCH // N_CORES
    return cfg


FULL_CFG = make_cfg(2048, 5632, 2048, 128, 8)


class Prog:
    ENG = ("pe", "act", "dve", "pool", "sp")

    def __init__(self):
        self.streams = {e: [] for e in self.ENG}
        self.cnt = {}
        self.last_w = {}
        self.readers = {}
        self.waited = {e: {} for e in self.ENG}
        self.pending = {e: [] for e in self.ENG}
        self.semnames = []

    def _sem(self, name):
        if name not in self.cnt:
            self.cnt[name] = 0
            self.semnames.append(name)
        return name

    def op(self, eng, fn, reads=(), writes=(), sem=None, inc=None, signal=True, after=()):
        own = eng if eng in ("pe", "act", "dve", "pool") else None
        if sem is None:
            sem = own
            inc = 1
        self._sem(sem)
        deps = {}

        def add(d, raw):
            if d is None:
                return
            s, v = d
            if s == own and eng == "pe":
                return
            if v > deps.get(s, 0):
                deps[s] = v

        for k in reads:
            add(self.last_w.get(k), True)
        for k in after:
            add(self.last_w.get(k), True)
        for k in writes:
            add(self.last_w.get(k), False)
            for r in self.readers.get(k, {}).items():
                add(r, False)
        waits = []
        wd = self.waited[eng]
        for s, v in deps.items():
            if wd.get(s, 0) < v:
                wd[s] = v
                waits.append((s, v))
        self.pending[eng].append((tuple(reads), tuple(writes)))
        incspec = None
        if signal:
            self.cnt[sem] += inc
            val = self.cnt[sem]
            incspec = (sem, inc)
            for rd, wr in self.pending[eng]:
                for k in rd:
                    d = self.readers.setdefault(k, {})
                    if d.get(sem, 0) < val:
                        d[sem] = val
                for k in wr:
                    self.last_w[k] = (sem, val)
                    self.readers[k] = {}
            self.pending[eng] = []
        self.streams[eng].append((waits, fn, incspec))

    def alias(self, new_keys, old_keys):
        for nk in new_keys:
            rd = self.readers.setdefault(nk, {})
            for ok in old_keys:
                for s, v in self.readers.get(ok, {}).items():
                    if rd.get(s, 0) < v:
                        rd[s] = v
                lw = self.last_w.get(ok)
                if lw is not None and rd.get(lw[0], 0) < lw[1]:
                    rd[lw[0]] = lw[1]

    def final_wait(self, eng, sems):
        waits = [(s, self.cnt[s]) for s in sems if self.cnt.get(s, 0) > 0]
        self.streams[eng].append((waits, None, None))


def build_program(cfg):
    D, F, NPO, NSEQ, PC = cfg["D"], cfg["F"], cfg["NPO"], cfg["NSEQ"], cfg["PC"]
    DC, FC = D // 128, F // 128
    GC = DC // 4
    NP = HALO + NPO
    NS = NSEQ * T
    TOT = NP + NS
    NOUT = NPO + NS
    assert TOT % 2 == 0 and TOT <= OB
    TTs = [(c0, min(512, TOT - c0)) for c0 in range(0, TOT, 512)]
    UWP = CH + NP
    PBL = PH + NP + NSEQ * (PH + T)
    SH = (NSEQ // 2) * PH
    assert SH <= 128 and SH % 2 == 0 and NSEQ * CH <= 128
    NV = 9
    WSLOT = max(DC * 256, PC * 512, GC * 512)
    NWS = 3

    nc = bass.Bass("TRN2", target_bir_lowering=False)

    def din(name, shape):
        return nc.dram_tensor(name, list(shape), F32, kind="ExternalInput").ap()

    def dout(name, shape):
        return nc.dram_tensor(name, list(shape), F32, kind="ExternalOutput").ap()

    xin = din("xin", [TOT, D])
    sconv = din("sconv", [NSEQ * CH, D])
    spool = din("spool", [NSEQ * PH, D])
    vcd = din("vcols", [128, NV * DC])
    invd = din("invc", [128, 4 * IW])
    identd = din("ident", [128, 128])
    w_in = din("w_in", [D, 3 * D])
    w_out = din("w_out", [D, D])
    w_pool = din("w_pool", [4, D // 4, D // 4])
    w_gu = din("w_gu", [2, D, 2 * F])
    w_dn = din("w_dn", [2, F, D])
    y = dout("y", [NOUT, D])
    ocp = dout("ocp", [CH, D])
    opp = dout("opp", [PH, D])
    ocs = dout("ocs", [NSEQ * CH, D])
    ops = dout("ops", [NSEQ * PH, D])

    from contextlib import ExitStack
    with ExitStack() as es:
        def sb(name, shape, dt=F32):
            return es.enter_context(nc.sbuf_tensor(name, list(shape), dt))

        hT = sb("hT", [128, DC, TOT])
        nT = sb("nT", [128, DC, TOT], BF16)
        aT = sb("aT", [128, PC, TOT], BF16)
        wr = [sb(f"wr{s}", [128, WSLOT], BF16) for s in range(NWS)]
        acc = sb("acc", [128, TOT])
        rstd = sb("rstd", [128, TOT])
        tA = [sb(f"tA{i}", [128, TOT]) for i in range(2)]
        SCR = max(2 * UWP + 2 * NSEQ * (CH + T) + 2 * TOT, 4 * PBL)
        scr = sb("scr", [128, SCR])
        vc = sb("vc", [128, NV * DC])
        invc = sb("invc_sb", [128, 4 * IW])
        ident = sb("ident_sb", [128, 128])
        ones = sb("ones", [128, 128])
        rcols = sb("rcols", [128, 16])
        sscol = sb("sscol", [128, 16])
        sscol2 = sb("sscol2", [128, 32])
        rcol = sb("rcol", [128, 16])
        diag = [sb(f"diag{i}", [128, 128]) for i in range(2)]
        sc_t = [sb(f"sc_t{i}", [128, 128]) for i in range(2)]
        sp_t = [sb(f"sp_t{i}", [128, 2, 128]) for i in range(2)]
        ost = [sb(f"ost{i}", [128, 128]) for i in range(4)]
        ostg = [sb(f"ostg{i}", [128, 128]) for i in range(4)]
        ps = es.enter_context(nc.psum_tensor("ps", [128, 8 * 512], F32))
        sems = {}
        block = None

        o = 0
        ubP = []
        for i in range(2):
            ubP.append(scr[:, o:o + UWP]); o += UWP
        ubS = []
        for i in range(2):
            ubS.append(scr[:, o:o + NSEQ * (CH + T)].rearrange("p (s t) -> p s t", t=CH + T)); o += NSEQ * (CH + T)
        cvb = []
        for i in range(2):
            cvb.append(scr[:, o:o + TOT]); o += TOT
        L0_KEYS = ["ubP0", "ubP1", "ubS0", "ubS1", "cvP0", "cvP1", "cvS0", "cvS1"]
        pbs = [scr[:, 0:PBL], scr[:, PBL:2 * PBL]]
        t1 = scr[:, 2 * PBL:3 * PBL]
        t2 = scr[:, 3 * PBL:4 * PBL]
        L1_KEYS = ["pb0", "pb1", "pbh0", "pbh1", "t1", "t2"]
        HD = D // 2
        if PC * TOT // 2 >= 2 * D:
            aT32 = aT[:].bitcast(F32)
            aTflat = aT32.rearrange("p a b -> p (a b)")
            xh = [aTflat[:, q * HD:(q + 1) * HD] for q in range(4)]
        else:
            xts = sb("xts", [128, 2 * D])
            xh = [xts[:, q * HD:(q + 1) * HD] for q in range(4)]
        XT_KEYS = [("xt", q) for q in range(4)]
        AT_KEYS = [("aT", k) for k in range(PC)]

        def vcol(v, j):
            return vc[:, v * DC + j: v * DC + j + 1]

        def psO(X):
            return ps[:, X * OB:(X + 1) * OB]

        def plan(P, units, recording):
            op = P.op
            deferred = []

            MISC = [ps[:, 3072:3584], ps[:, 3584:4096]]
            misc_i = [0]

            def next_misc():
                misc_i[0] ^= 1
                return misc_i[0], MISC[misc_i[0]]

            pso_i = [0]

            def next_pso():
                pso_i[0] ^= 1
                return pso_i[0]

            ucount = [0]
            issued = [0]

            def load_unit(src, nk, ncols):
                i = ucount[0]
                ucount[0] += 1
                if recording:
                    units.append((src, nk, ncols))
                hi = i if recording else min(i + NWS - 1, len(units) - 1)
                while issued[0] <= hi:
                    n_ = issued[0]
                    issued[0] += 1
                    usrc, unk, uncols = units[n_]
                    s_ = n_ % NWS
                    uview = wr[s_][:, 0:unk * uncols].rearrange("p (k n) -> p k n", n=uncols)
                    op("pool", lambda e, uview=uview, usrc=usrc: e.dma_start(out=uview, in_=usrc),
                       writes=[("w", s_)], sem=f"w{s_}", inc=16)
                s = i % NWS
                view = wr[s][:, 0:nk * ncols].rearrange("p (k n) -> p k n", n=ncols)
                return view, ("w", s)

            def prefetch_first():
                if recording:
                    return
                while issued[0] < min(NWS, len(units)):
                    n_ = issued[0]
                    issued[0] += 1
                    usrc, unk, uncols = units[n_]
                    s_ = n_ % NWS
                    uview = wr[s_][:, 0:unk * uncols].rearrange("p (k n) -> p k n", n=uncols)
                    op("pool", lambda e, uview=uview, usrc=usrc: e.dma_start(out=uview, in_=usrc),
                       writes=[("w", s_)], after=[("xt", 0), ("xt", 1), ("xt", 2), ("xt", 3)], sem=f"w{s_}", inc=16)

            def flush_deferred():
                while deferred:
                    deferred.pop(0)()

            CL = [0]

            def tts():
                lo = CL[0]
                return [(c0, min(512, TOT - lo - c0)) for c0 in range(0, TOT - lo, 512)]

            def mm_terms(X, unit, ukey, col0, terms):
                TTl = tts()
                lo = CL[0]
                nmm = len(terms) * len(TTl)
                i = 0
                for idx, (ku, act, akey, ka) in enumerate(terms):
                    for (c0, n) in TTl:
                        i += 1
                        op("pe", lambda e, X=X, ku=ku, ka=ka, act=act, c0=c0, n=n, lo=lo, st=(idx == 0), sp_=(idx == len(terms) - 1):
                           e.matmul(out=ps[:, X * OB + c0: X * OB + c0 + n], lhsT=unit[:, ku, col0:col0 + 128],
                                    rhs=act[:, ka, lo + c0:lo + c0 + n], start=st, stop=sp_),
                           reads=[ukey] + [(ak_, ka) for ak_ in (akey if isinstance(akey, tuple) else (akey,))],
                           writes=[("ps", X)], signal=(i == nmm))
                flush_deferred()

            def mm_chunk(X, unit, ukey, col0, act, akey, ks):
                mm_terms(X, unit, ukey, col0, [(ku, act, akey, ka) for (ku, ka) in ks])

            def mm_pair(Xs, unit, ukey, act, akey, ks):
                TTl = tts()
                lo = CL[0]
                for idx, (ku, ka) in enumerate(ks):
                    for jj, X in enumerate(Xs):
                        for ti, (c0, n) in enumerate(TTl):
                            op("pe", lambda e, X=X, jj=jj, ku=ku, ka=ka, c0=c0, n=n, lo=lo, st=(idx == 0), sp_=(idx == len(ks) - 1):
                               e.matmul(out=ps[:, X * OB + c0: X * OB + c0 + n], lhsT=unit[:, ku, jj * 128:(jj + 1) * 128],
                                        rhs=act[:, ka, lo + c0:lo + c0 + n], start=st, stop=sp_),
                               reads=[ukey, (akey, ka)], writes=[("ps", X)],
                               signal=(idx == len(ks) - 1 and ti == len(TTl) - 1))
                flush_deferred()

            op("sp", lambda e: e.dma_start(out=vc[:, :], in_=vcd), writes=["vc"], sem="c0", inc=16)
            op("sp", lambda e: e.dma_start(out=ident[:, :], in_=identd), writes=["ident"], sem="c1", inc=16)
            op("sp", lambda e: e.dma_start(out=invc[:, :], in_=invd), writes=["invc"], sem="c2", inc=16)
            op("dve", lambda e: e.memset(ones[:, :], 1.0), writes=["ones"])
            op("dve", lambda e: e.memset(scr[:, :], 0.0), writes=L0_KEYS)

            P.alias(XT_KEYS, AT_KEYS)
            ntile = (TOT + 127) // 128
            evac_i = [0]
            HC = DC // 2
            GS = min(4, HC)

            def evac(fn_act, fn_dve, reads, writes):
                evac_i[0] ^= 1
                if evac_i[0]:
                    op("act", fn_act, reads=reads, writes=writes)
                else:
                    op("dve", fn_dve, reads=reads, writes=writes)

            junk = nT[:].rearrange("p a b -> p (a b)")[:, 0:HD]
            op("dve", lambda e: e.memset(sscol2[:, :], 1.0), writes=["sscol2"])
            for i in range(ntile):
                r0 = i * 128
                nr = min(128, TOT - r0)
                if i == (ntile * 6) // 10:
                    prefetch_first()
                for hf in range(2):
                    q = (2 * i + hf) % 4
                    op("sp", lambda e, q=q, r0=r0, nr=nr, hf=hf: e.dma_start(out=xh[q][0:nr, :],
                                                                             in_=xin[r0:r0 + nr, hf * HD:(hf + 1) * HD]),
                       writes=[("xt", q)], sem=f"x{q}", inc=16)
                    op("act", lambda e, q=q, i=i, hf=hf, nr=nr: e.activation(out=junk[0:nr, :], in_=xh[q][0:nr, :], func=AF.Square,
                                                                             accum_out=sscol2[0:nr, 2 * i + hf:2 * i + hf + 1]),
                       reads=[("xt", q)], writes=["sscol2", "junk"])
                    for jb in range(0, HC, GS):
                        mi, mb = next_misc()
                        for jj in range(GS):
                            jl = jb + jj
                            op("pe", lambda e, mb=mb, jj=jj, jl=jl, q=q, nr=nr:
                               e.transpose(out=mb[:, jj * 128: jj * 128 + nr], in_=xh[q][0:nr, jl * 128:(jl + 1) * 128],
                                           identity=ident[0:nr, 0:nr]),
                               reads=[("xt", q), "ident"], writes=[("psm", mi)], signal=(jj == GS - 1))
                        src = mb[:, 0:GS * 128].rearrange("p (a b) -> p a b", b=128)[:, :, 0:nr]
                        j0 = hf * HC + jb
                        dst = hT[:, j0:j0 + GS, r0:r0 + nr]
                        evac(lambda e, dst=dst, src=src: e.copy(out=dst, in_=src),
                             lambda e, dst=dst, src=src: e.tensor_copy(out=dst, in_=src),
                             reads=[("psm", mi)], writes=[("hT", j0 + q_) for q_ in range(GS)])
            P.alias(AT_KEYS, XT_KEYS)
            s2v = sscol2[:, 0:2 * ntile].rearrange("p (a b) -> p a b", b=2)
            op("dve", lambda e: e.tensor_tensor(out=sscol[:, 0:ntile], in0=s2v[:, :, 0], in1=s2v[:, :, 1], op=ALU.add),
               reads=["sscol2"], writes=["sscol"])

            def stats_chunk(j):
                b = tA[j % 2]
                bk = f"tA{j % 2}"
                lo = CL[0]
                op("act", lambda e, b=b, j=j, lo=lo: e.activation(out=b[:, lo:TOT], in_=hT[:, j, lo:TOT], func=AF.Square),
                   reads=[("hT", j)], writes=[bk])
                if j == 0:
                    op("dve", lambda e, b=b, lo=lo: e.tensor_copy(out=acc[:, lo:TOT], in_=b[:, lo:TOT]), reads=[bk], writes=["acc"])
                else:
                    op("dve", lambda e, b=b, lo=lo: e.tensor_tensor(out=acc[:, lo:TOT], in0=acc[:, lo:TOT], in1=b[:, lo:TOT], op=ALU.add),
                       reads=[bk, "acc"], writes=["acc"])

            NTT = (TOT + 127) // 128

            def col_sums(c_off, ntl, ncols_total):
                mi, mb = next_misc()
                for i in range(ntl):
                    r0 = i * 128
                    nr = min(128, ncols_total - r0)
                    op("pe", lambda e, mb=mb, i=i, r0=r0, nr=nr: e.matmul(out=mb[0:nr, 2 * i:2 * i + 2],
                                                                          lhsT=acc[:, c_off + r0:c_off + r0 + nr],
                                                                          rhs=ones[:, 0:2], start=True, stop=True),
                       reads=["acc", "ones"], writes=[("psm", mi)], signal=(i == ntl - 1))
                nfull = ncols_total // 128
                mbv = mb[:, 0:2 * ntl].rearrange("p (a b) -> p a b", b=2)
                if nfull > 0:
                    op("dve", lambda e, mbv=mbv: e.tensor_copy(out=sscol[:, 0:nfull], in_=mbv[:, 0:nfull, 0]),
                       reads=[("psm", mi)], writes=["sscol"])
                if nfull < ntl:
                    nrl = ncols_total - nfull * 128
                    op("dve", lambda e, mbv=mbv: e.tensor_copy(out=sscol[0:nrl, nfull:ntl], in_=mbv[0:nrl, nfull:ntl, 0]),
                       reads=[("psm", mi)], writes=["sscol"])

            def rcol_from_sscol(ntl):
                op("dve", lambda e: e.tensor_scalar(out=rcol[:, 0:ntl], in0=sscol[:, 0:ntl], scalar1=1.0 / D, scalar2=EPS,
                                                    op0=ALU.mult, op1=ALU.add), reads=["sscol"], writes=["rcol"])
                op("act", lambda e: e.activation(out=rcol[:, 0:ntl], in_=rcol[:, 0:ntl], func=AF.Sqrt),
                   reads=["rcol"], writes=["rcol"])
                op("dve", lambda e: e.reciprocal(out=rcol[:, 0:ntl], in_=rcol[:, 0:ntl]), reads=["rcol"], writes=["rcol"])

            def rstd_rows():
                X = next_pso()
                for i in range(NTT):
                    r0 = i * 128
                    nr = min(128, TOT - r0)
                    dg = diag[i % 2]
                    op("dve", lambda e, dg=dg, i=i, nr=nr: e.tensor_scalar(out=dg[0:nr, 0:nr], in0=ident[0:nr, 0:nr],
                                                                           scalar1=rcol[0:nr, i:i + 1], scalar2=None, op0=ALU.mult),
                       reads=["rcol", "ident"], writes=[("diag", i % 2)])
                    op("pe", lambda e, X=X, dg=dg, r0=r0, nr=nr: e.matmul(out=ps[:, X * OB + r0: X * OB + r0 + nr],
                                                                          lhsT=ones[0:nr, :], rhs=dg[0:nr, 0:nr],
                                                                          start=True, stop=True),
                       reads=[("diag", i % 2), "ones"], writes=[("ps", X)])
                op("act", lambda e, X=X: e.copy(out=rstd[:, :], in_=psO(X)[:, 0:TOT]), reads=[("ps", X)], writes=["rstd"])

            def stats_finish():
                col_sums(0, NTT, TOT)
                rcol_from_sscol(NTT)
                rstd_rows()

            def apply_norm(v):
                for j in range(DC):
                    op("dve", lambda e, j=j, lo=CL[0]: e.scalar_tensor_tensor(out=nT[:, j, lo:TOT], in0=hT[:, j, lo:TOT], scalar=vcol(v, j),
                                                                           in1=rstd[:, lo:TOT], op0=ALU.mult, op1=ALU.mult),
                       reads=[("hT", j), "rstd", "vc"], writes=[("nT", j)])

            def resid_matmul(unit_src_fn, nk, last, gscale=None):
                for ob in range(D // 512):
                    unit, ukey = load_unit(unit_src_fn(ob), nk, 512)
                    for jo in range(4):
                        j = ob * 4 + jo
                        X = next_pso()
                        mm_chunk(X, unit, ukey, jo * 128, aT, "aT", [(k, k) for k in range(nk)])
                        op("dve", lambda e, j=j, X=X, lo=CL[0]: e.tensor_tensor(out=hT[:, j, lo:TOT], in0=hT[:, j, lo:TOT],
                                                                                in1=psO(X)[:, 0:TOT - lo], op=ALU.add),
                           reads=[("ps", X), ("hT", j)], writes=[("hT", j)])
                        if last:
                            stats_chunk(j)
                            if gscale is not None:
                                op("act", lambda e, j=j, lo=CL[0]: e.activation(out=hT[:, j, lo:TOT], in_=hT[:, j, lo:TOT], func=AF.Copy,
                                                                             scale=vcol(gscale, j)),
                                   reads=[("hT", j), "vc"], writes=[("hT", j)])

            def ffn(layer, gscale=None):
                parts = []
                f0 = 0
                nparts = (FC + PC - 1) // PC
                base, rem = FC // nparts, FC % nparts
                for pi in range(nparts):
                    n = base + (1 if pi < rem else 0)
                    parts.append((f0, n))
                    f0 += n
                for pi, (f0, n) in enumerate(parts):
                    for q0 in range(0, n, 2):
                        nq = min(2, n - q0)
                        fa = f0 + q0
                        gsrc = w_gu[layer, :, fa * 128:(fa + nq) * 128].rearrange("(k p) n -> p k n", p=128)
                        usrc = w_gu[layer, :, F + fa * 128:F + (fa + nq) * 128].rearrange("(k p) n -> p k n", p=128)
                        gun, gk = load_unit(gsrc, DC, nq * 128)
                        gXs = [next_pso() for jj in range(nq)]
                        first = (pi == 0 and q0 == 0 and nq == 2)
                        if first:
                            mm_pair(gXs, gun, gk, nT, "nT", [(k, k) for k in range(DC)])
                        for jj in range(nq):
                            X = gXs[jj]
                            if not first:
                                mm_chunk(X, gun, gk, jj * 128, nT, "nT", [(k, k) for k in range(DC)])
                            op("act", lambda e, jj=jj, X=X, lo=CL[0]: e.activation(out=tA[jj][:, lo:TOT], in_=psO(X)[:, 0:TOT - lo],
                                                                                func=AF.Silu),
                               reads=[("ps", X)], writes=[f"tA{jj}"])
                        uun, uk = load_unit(usrc, DC, nq * 128)
                        for jj in range(nq):
                            X = next_pso()
                            mm_chunk(X, uun, uk, jj * 128, nT, "nT", [(k, k) for k in range(DC)])
                            fl = q0 + jj
                            op("dve", lambda e, jj=jj, X=X, fl=fl, lo=CL[0]: e.tensor_tensor(out=aT[:, fl, lo:TOT], in0=tA[jj][:, lo:TOT],
                                                                                            in1=psO(X)[:, 0:TOT - lo], op=ALU.mult),
                               reads=[("ps", X), f"tA{jj}"], writes=[("aT", fl)])
                    resid_matmul(lambda ob, f0=f0, n=n: w_dn[layer, f0 * 128:(f0 + n) * 128, ob * 512:(ob + 1) * 512]
                                 .rearrange("(k p) n -> p k n", p=128), n, last=(pi == len(parts) - 1),
                                 gscale=(gscale if pi == len(parts) - 1 else None))

            so_i = [0]

            def state_out(src_ap, keys, ncols, dsts):
                def go():
                    mi, mb = next_misc()
                    op("pe", lambda e, mb=mb: e.transpose(out=mb[0:ncols, 0:128], in_=src_ap, identity=ident[:, :]),
                       reads=list(keys) + ["ident"], writes=[("psm", mi)])
                    s = so_i[0] % 4
                    so_i[0] += 1
                    op("act", lambda e, s=s, mb=mb: e.copy(out=ostg[s][0:ncols, :], in_=mb[0:ncols, 0:128]),
                       reads=[("psm", mi)], writes=[("ostg", s)])
                    for (r0, nrw, dst) in dsts:
                        op("sp", lambda e, s=s, r0=r0, nrw=nrw, dst=dst: e.dma_start(out=dst, in_=ostg[s][r0:r0 + nrw, :]),
                           reads=[("ostg", s)], sem=f"so{s}", inc=16)
                deferred.append(go)

            ost_i = [0]

            def next_ost():
                ost_i[0] = (ost_i[0] + 1) % len(ost)
                return ost[ost_i[0]], ("ost", ost_i[0])

            P.alias([("nT", j) for j in range(DC)], ["junk"])
            prefetch_first()
            rcol_from_sscol(NTT)
            rstd_rows()
            apply_norm(0)

            nparts0 = (DC + PC - 1) // PC
            assert DC % nparts0 == 0
            pc0 = DC // nparts0
            full_k = [(k, k) for k in range(DC)]
            for pa in range(nparts0):
                for q0 in range(0, pc0, 2):
                    nq = min(2, pc0 - q0)
                    ja = pa * pc0 + q0
                    csrc = w_in[:, D + ja * 128: D + (ja + nq) * 128].rearrange("(k p) n -> p k n", p=128)
                    vsrc = w_in[:, 2 * D + ja * 128: 2 * D + (ja + nq) * 128].rearrange("(k p) n -> p k n", p=128)
                    bsrc = w_in[:, ja * 128:(ja + nq) * 128].rearrange("(k p) n -> p k n", p=128)
                    for jj in range(nq):
                        j = ja + jj
                        s = j % 2
                        op("sp", lambda e, s=s, j=j: e.dma_start(out=sc_t[s][0:NSEQ * CH, :], in_=sconv[:, j * 128:(j + 1) * 128]),
                           writes=[("sc_t", s)], sem=f"sc{s}", inc=16)
                        mi, mb = next_misc()
                        op("pe", lambda e, mb=mb, s=s: e.transpose(out=mb[:, 0:NSEQ * CH], in_=sc_t[s][0:NSEQ * CH, :],
                                                                   identity=ident[0:NSEQ * CH, 0:NSEQ * CH]),
                           reads=[("sc_t", s), "ident"], writes=[("psm", mi)])
                        op("act", lambda e, mb=mb, jj=jj: e.copy(out=ubS[jj][:, :, 0:CH],
                                                                 in_=mb[:, 0:NSEQ * CH].rearrange("p (s r) -> p s r", r=CH)),
                           reads=[("psm", mi)], writes=[f"ubS{jj}"])
                    cun, ck = load_unit(csrc, DC, nq * 128)
                    cXs = [next_pso() for jj in range(nq)]
                    if pa == 0 and q0 == 0 and nq == 2:
                        mm_pair(cXs, cun, ck, nT, "nT", full_k)
                    for jj in range(nq):
                        X = cXs[jj]
                        if not (pa == 0 and q0 == 0 and nq == 2):
                            mm_chunk(X, cun, ck, jj * 128, nT, "nT", full_k)
                        op("act", lambda e, jj=jj, X=X: e.copy(out=tA[jj][:, :], in_=psO(X)[:, 0:TOT]),
                           reads=[("ps", X)], writes=[f"tA{jj}"])
                    vun, vk = load_unit(vsrc, DC, nq * 128)
                    for jj in range(nq):
                        j = ja + jj
                        X = next_pso()
                        mm_chunk(X, vun, vk, jj * 128, nT, "nT", full_k)
                        op("dve", lambda e, jj=jj, X=X: e.tensor_tensor(out=ubP[jj][:, CH:CH + NP], in0=tA[jj][:, 0:NP],
                                                                        in1=psO(X)[:, 0:NP], op=ALU.mult),
                           reads=[("ps", X), f"tA{jj}"], writes=[f"ubP{jj}"])
                        op("dve", lambda e, jj=jj, X=X: e.tensor_tensor(
                            out=ubS[jj][:, :, CH:CH + T], in0=tA[jj][:, NP:TOT].rearrange("p (s t) -> p s t", t=T),
                            in1=psO(X)[:, NP:TOT].rearrange("p (s t) -> p s t", t=T), op=ALU.mult),
                           reads=[("ps", X), f"tA{jj}"], writes=[f"ubS{jj}"])
                        cvP = cvb[jj][:, 0:NP]
                        cvS = cvb[jj][:, NP:TOT].rearrange("p (s t) -> p s t", t=T)
                        op("act", lambda e, jj=jj, j=j, cvP=cvP: e.activation(out=cvP, in_=ubP[jj][:, 2:2 + NP], func=AF.Copy,
                                                                              scale=vcol(7, j)),
                           reads=[f"ubP{jj}", "vc"], writes=[f"cvP{jj}"])
                        op("act", lambda e, jj=jj, j=j, cvS=cvS: e.activation(out=cvS, in_=ubS[jj][:, :, 2:2 + T], func=AF.Copy,
                                                                              scale=vcol(7, j)),
                           reads=[f"ubS{jj}", "vc"], writes=[f"cvS{jj}"])
                        for tap in (1, 0):
                            op("dve", lambda e, jj=jj, j=j, tap=tap, cvP=cvP: e.scalar_tensor_tensor(
                                out=cvP, in0=ubP[jj][:, tap:tap + NP], scalar=vcol(5 + tap, j), in1=cvP,
                                op0=ALU.mult, op1=ALU.add),
                               reads=[f"ubP{jj}", f"cvP{jj}", "vc"], writes=[f"cvP{jj}"])
                            op("dve", lambda e, jj=jj, j=j, tap=tap, cvS=cvS: e.scalar_tensor_tensor(
                                out=cvS, in0=ubS[jj][:, :, tap:tap + T], scalar=vcol(5 + tap, j), in1=cvS,
                                op0=ALU.mult, op1=ALU.add),
                               reads=[f"ubS{jj}", f"cvS{jj}", "vc"], writes=[f"cvS{jj}"])
                        so, sok = next_ost()
                        op("act", lambda e, jj=jj, so=so: e.copy(out=so[:, 0:CH], in_=ubP[jj][:, NP:NP + CH]),
                           reads=[f"ubP{jj}"], writes=[sok])
                        op("act", lambda e, jj=jj, so=so: e.copy(
                            out=so[:, CH:CH + NSEQ * CH].rearrange("p (s r) -> p s r", r=CH), in_=ubS[jj][:, :, T:T + CH]),
                           reads=[f"ubS{jj}"], writes=[sok])
                        state_out(so[:, 0:CH + NSEQ * CH], [sok], CH + NSEQ * CH,
                                  [(0, CH, ocp[:, j * 128:(j + 1) * 128]),
                                   (CH, NSEQ * CH, ocs[:, j * 128:(j + 1) * 128])])
                    bun, bk = load_unit(bsrc, DC, nq * 128)
                    for jj in range(nq):
                        X = next_pso()
                        mm_chunk(X, bun, bk, jj * 128, nT, "nT", full_k)
                        gl = q0 + jj
                        op("dve", lambda e, jj=jj, X=X, gl=gl: e.tensor_tensor(out=aT[:, gl, :], in0=cvb[jj][:, :],
                                                                              in1=psO(X)[:, 0:TOT], op=ALU.mult),
                           reads=[("ps", X), f"cvP{jj}", f"cvS{jj}"], writes=[("aT", gl)])
                resid_matmul(lambda ob, pa=pa: w_out[pa * pc0 * 128:(pa + 1) * pc0 * 128, ob * 512:(ob + 1) * 512]
                             .rearrange("(k p) n -> p k n", p=128), pc0, last=(pa == nparts0 - 1))
            stats_finish()
            apply_norm(2)
            ffn(0)

            stats_finish()
            flush_deferred()
            P.alias(L1_KEYS, L0_KEYS)
            for j in range(DC):
                P.alias([("nTfix", j)], [("nT", j)])
            SW = PH + T
            for pbx, pbk in zip(pbs, ("pbh0", "pbh1")):
                op("dve", lambda e, pbx=pbx: e.memset(pbx[:, 0:PH], 0.0), writes=[pbk])
            def spool_dma(j):
                s = j % 2
                for hh in range(2):
                    op("sp", lambda e, s=s, hh=hh, j=j: e.dma_start(out=sp_t[s][0:SH, hh, :],
                                                                    in_=spool[hh * SH:(hh + 1) * SH, j * 128:(j + 1) * 128]),
                       writes=[("sp_t", s)], sem=f"spl{s}", inc=16)

            def pool_prep(j):
                s = j % 2
                pb, pbk, pbh = pbs[s], f"pb{s}", f"pbh{s}"
                pbS = pb[:, PH + NP: PBL].rearrange("p (s t) -> p s t", t=SW)
                if j == 0:
                    spool_dma(0)
                if j + 1 < DC:
                    spool_dma(j + 1)
                mi, mb = next_misc()
                for hh in range(2):
                    op("pe", lambda e, mb=mb, s=s, hh=hh: e.transpose(out=mb[:, hh * SH:(hh + 1) * SH], in_=sp_t[s][0:SH, hh, :],
                                                                     identity=ident[0:SH, 0:SH]),
                       reads=[("sp_t", s), "ident"], writes=[("psm", mi)], signal=(hh == 1))
                op("act", lambda e, mb=mb, pbS=pbS: e.copy(out=pbS[:, :, 0:PH],
                                                           in_=mb[:, 0:2 * SH].rearrange("p (s r) -> p s r", r=PH)),
                   reads=[("psm", mi)], writes=[pbh])
                op("dve", lambda e, j=j, pb=pb: e.scalar_tensor_tensor(out=pb[:, PH:PH + NP], in0=hT[:, j, 0:NP], scalar=vcol(1, j),
                                                                       in1=rstd[:, 0:NP], op0=ALU.mult, op1=ALU.mult),
                   reads=[("hT", j), "rstd", "vc"], writes=[pbk])
                op("dve", lambda e, j=j, pbS=pbS: e.scalar_tensor_tensor(
                    out=pbS[:, :, PH:SW], in0=hT[:, j, NP:TOT].rearrange("p (s t) -> p s t", t=T), scalar=vcol(1, j),
                    in1=rstd[:, NP:TOT].rearrange("p (s t) -> p s t", t=T), op0=ALU.mult, op1=ALU.mult),
                   reads=[("hT", j), "rstd", "vc"], writes=[pbk])
                sl = ((j // GC) % 2) * GC + (j % GC)
                op("act", lambda e, sl=sl, pb=pb: e.activation(out=aT[:, sl, 0:NP], in_=pb[:, PH:PH + NP], func=AF.Copy, scale=-1.0),
                   reads=[pbk], writes=[("aT", sl)])
                op("act", lambda e, sl=sl, pbS=pbS: e.activation(out=aT[:, sl, NP:TOT].rearrange("p (s t) -> p s t", t=T),
                                                                 in_=pbS[:, :, PH:SW], func=AF.Copy, scale=-1.0),
                   reads=[pbk], writes=[("aT", sl)])
                so, sok = next_ost()
                op("act", lambda e, so=so, pb=pb: e.copy(out=so[:, 0:PH], in_=pb[:, NP:NP + PH]), reads=[pbk], writes=[sok])
                state_out(so[:, 0:PH], [sok], PH, [(0, PH, opp[:, j * 128:(j + 1) * 128])])
                for hh in range(2):
                    so, sok = next_ost()
                    op("act", lambda e, so=so, hh=hh, pbS=pbS: e.copy(
                        out=so[:, 0:SH].rearrange("p (s r) -> p s r", r=PH),
                        in_=pbS[:, hh * (NSEQ // 2):(hh + 1) * (NSEQ // 2), T:SW]),
                       reads=[pbk, pbh], writes=[sok])
                    state_out(so[:, 0:SH], [sok], SH, [(0, SH, ops[hh * SH:(hh + 1) * SH, j * 128:(j + 1) * 128])])
                flush_deferred()

            last_fin = [1]

            def pool_compute(j):
                g = j // GC
                w = 2 ** (g + 1)
                s = j % 2
                pb, pbk, pbh = pbs[s], f"pb{s}", f"pbh{s}"
                pbS = pb[:, PH + NP: PBL].rearrange("p (s t) -> p s t", t=SW)
                cur, curk = pb, [pbk, pbh]
                bufs = [(t1, "t1"), (t2, "t2")]
                sh = 1
                bi = 1 - last_fin[0]
                while sh < w:
                    nb, nbk = bufs[bi % 2]
                    bi += 1
                    lo = 2 * sh - 1
                    op("dve", lambda e, nb=nb, cur=cur, lo=lo, sh=sh: e.tensor_tensor(
                        out=nb[:, lo:PBL], in0=cur[:, lo:PBL], in1=cur[:, lo - sh:PBL - sh], op=ALU.add),
                       reads=curk, writes=[nbk])
                    cur, curk = nb, [nbk]
                    sh *= 2
                last_fin[0] = (bi - 1) % 2
                curS = cur[:, PH + NP: PBL].rearrange("p (s t) -> p s t", t=SW)
                op("act", lambda e, j=j, cur=cur, w=w: e.activation(out=nT[:, j, IW:NP], in_=cur[:, PH + IW:PH + NP], func=AF.Copy,
                                                                    scale=1.0 / w),
                   reads=curk, writes=[("nT", j)])
                op("act", lambda e, j=j, curS=curS, w=w: e.activation(out=nT[:, j, NP:TOT].rearrange("p (s t) -> p s t", t=T),
                                                                      in_=curS[:, :, PH:SW], func=AF.Copy, scale=1.0 / w),
                   reads=curk, writes=[("nT", j)])
                op("dve", lambda e, j=j, cur=cur, g=g: e.tensor_tensor(out=nT[:, j, 0:IW], in0=cur[:, PH:PH + IW],
                                                                      in1=invc[:, g * IW:(g + 1) * IW], op=ALU.mult),
                   reads=curk + ["invc"], writes=[("nTfix", j)])

            mmq = []
            tailq = []

            def plan_mm(n):
                for _ in range(n):
                    if not mmq:
                        return
                    g_, jo, unit, ukey = mmq.pop(0)
                    jj_ = g_ * GC + jo
                    X = next_pso()
                    mm_terms(X, unit, ukey, jo * 128,
                             [(k, nT, ("nT", "nTfix"), g_ * GC + k) for k in range(GC)] +
                             [(k, aT, "aT", (g_ % 2) * GC + k) for k in range(GC)])

                    def tail_a(jj_=jj_, X=X):
                        op("dve", lambda e, j=jj_, X=X, lo=CL[0]: e.scalar_tensor_tensor(out=hT[:, j, lo:TOT], in0=psO(X)[:, 0:TOT - lo],
                                                                                      scalar=vcol(8, j), in1=hT[:, j, lo:TOT],
                                                                                      op0=ALU.mult, op1=ALU.add),
                           reads=[("ps", X), ("hT", jj_), "vc"], writes=[("hT", jj_)])
                    tailq.append((tail_a, jj_))

            def run_tails():
                js = []
                while tailq:
                    fn, jj_ = tailq.pop(0)
                    fn()
                    js.append(jj_)
                for jj_ in js:
                    stats_chunk(jj_)

            CL[0] = HALO
            pool_prep(0)
            for j in range(DC):
                if j + 1 < DC:
                    pool_prep(j + 1)
                pool_compute(j)
                run_tails()
                if j % GC == GC - 1:
                    g = j // GC
                    unit, ukey = load_unit(w_pool[g].rearrange("(k p) n -> p k n", p=128), GC, GC * 128)
                    for jo in range(GC):
                        mmq.append((g, jo, unit, ukey))
                plan_mm(2)
            while mmq or tailq:
                run_tails()
                plan_mm(2)
            stats_finish()
            apply_norm(3)
            ffn(1, gscale=4)

            P.alias(XT_KEYS, AT_KEYS)
            notile = (NOUT + 127) // 128
            col_sums(HALO, notile, NOUT)
            rcol_from_sscol(notile)
            for i in range(notile):
                r0 = i * 128
                nr = min(128, NOUT - r0)
                c0 = HALO + r0
                for hf in range(2):
                    q = (2 * i + hf) % 4
                    for jb in range(0, HC, GS):
                        mi, mb = next_misc()
                        for jj in range(GS):
                            j = hf * HC + jb + jj
                            op("pe", lambda e, mb=mb, jj=jj, j=j, c0=c0, nr=nr:
                               e.transpose(out=mb[0:nr, jj * 128:(jj + 1) * 128], in_=hT[:, j, c0:c0 + nr], identity=ident[:, :]),
                               reads=[("hT", j), "ident"], writes=[("psm", mi)], signal=(jj == GS - 1))
                        dst = xh[q][0:nr, jb * 128:(jb + GS) * 128]
                        src = mb[0:nr, 0:GS * 128]
                        rc = rcol[0:nr, i:i + 1]
                        evac(lambda e, dst=dst, src=src, rc=rc: e.activation(out=dst, in_=src, func=AF.Copy, scale=rc),
                             lambda e, dst=dst, src=src, rc=rc: e.tensor_scalar(out=dst, in0=src, scalar1=rc, scalar2=None, op0=ALU.mult),
                             reads=[("psm", mi), "rcol"], writes=[("xt", q)])
                    op("sp", lambda e, q=q, r0=r0, nr=nr, hf=hf: e.dma_start(out=y[r0:r0 + nr, hf * HD:(hf + 1) * HD], in_=xh[q][0:nr, :]),
                       reads=[("xt", q)], sem=f"yo{q}", inc=16)
            flush_deferred()
            P.final_wait("sp", [n for n in P.semnames if n.startswith("yo") or n.startswith("so")])

        P1 = Prog()
        units = []
        plan(P1, units, True)
        P = Prog()
        plan(P, units, False)

        for name in P.semnames:
            sems[name] = es.enter_context(nc.semaphore(name))
        with nc.Block() as block:
            def emit(ename):
                def body(eng):
                    for waits, fn, incspec in P.streams[ename]:
                        for (sname, val) in waits:
                            eng.wait_ge(sems[sname], val)
                        if fn is None:
                            continue
                        ins = fn(eng)
                        if incspec is not None:
                            ins.then_inc(sems[incspec[0]], incspec[1])
                return body
            block.tensor(emit("pe"))
            block.scalar(emit("act"))
            block.vector(emit("dve"))
            block.gpsimd(emit("pool"))
            block.sync(emit("sp"))
    return nc


def _col_layout(v, DC):
    return np.ascontiguousarray(v.reshape(DC, 128).T)


def run(cfg, x_prompt, x_sample, state_conv, state_pool, meta_tokens, norm_mix, norm_ffn, norm_final,
        conv_w_in, conv_w_dw, conv_w_out, pool_w, pool_scale, ffn_w_gate_up, ffn_w_down, trace=False):
    D, F, NPO, NSEQ = cfg["D"], cfg["F"], cfg["NPO"], cfg["NSEQ"]
    DC = D // 128
    B = x_prompt.shape[0]
    SEQ = x_prompt.shape[1]
    f32 = np.float32
    nc = build_program(cfg)

    vecs = [norm_mix[0], norm_mix[1], norm_ffn[0], norm_ffn[1], norm_final,
            conv_w_dw[0, 0], conv_w_dw[0, 1], conv_w_dw[0, 2], pool_scale[0]]
    vcols = np.ascontiguousarray(np.concatenate([_col_layout(np.asarray(v, f32), DC) for v in vecs], axis=1))
    ident = np.eye(128, dtype=f32)
    shared = dict(
        vcols=vcols, ident=ident,
        w_in=np.ascontiguousarray(conv_w_in[0], f32), w_out=np.ascontiguousarray(conv_w_out[0], f32),
        w_pool=np.ascontiguousarray(pool_w[0], f32), w_gu=np.ascontiguousarray(ffn_w_gate_up, f32),
        w_dn=np.ascontiguousarray(ffn_w_down, f32),
    )
    wins = np.array([2.0, 4.0, 8.0, 16.0], f32)
    in_maps = []
    for c in range(N_CORES):
        b, half = c // 2, c % 2
        full = np.concatenate([np.asarray(meta_tokens, f32), np.asarray(x_prompt[b], f32)], axis=0)
        start = half * NPO
        lo = start - HALO
        if lo < 0:
            seg = np.concatenate([np.zeros((-lo, D), f32), full[0:start + NPO]], axis=0)
        else:
            seg = full[lo:start + NPO]
        xs = np.asarray(x_sample[c * NSEQ:(c + 1) * NSEQ], f32).reshape(NSEQ * T, D)
        xin = np.ascontiguousarray(np.concatenate([seg, xs], axis=0))
        pos = (lo + np.arange(IW)).astype(np.float64)
        cnt = np.minimum(wins[:, None].astype(np.float64), np.maximum(pos[None, :], 0.0) + 1.0)
        invc = np.broadcast_to((1.0 / cnt).astype(f32).reshape(1, 4 * IW), (128, 4 * IW))
        m = dict(shared)
        m.update(
            xin=xin,
            sconv=np.ascontiguousarray(np.asarray(state_conv[0, c * NSEQ:(c + 1) * NSEQ], f32).reshape(NSEQ * CH, D)),
            spool=np.ascontiguousarray(np.asarray(state_pool[0, c * NSEQ:(c + 1) * NSEQ], f32).reshape(NSEQ * PH, D)),
            invc=np.ascontiguousarray(invc),
        )
        in_maps.append(m)
    res = run_bass_kernel_spmd(nc, in_maps, core_ids=list(range(N_CORES)), trace=trace)
    R = res.results
    DECB = x_sample.shape[0]
    y_prompt = np.empty((B, SEQ, D), f32)
    y_sample = np.empty((DECB, T, D), f32)
    ncp = np.empty((1, B, CH, D), f32)
    npp = np.empty((1, B, PH, D), f32)
    ncs = np.empty((1, DECB, CH, D), f32)
    nps = np.empty((1, DECB, PH, D), f32)
    for c in range(N_CORES):
        b, half = c // 2, c % 2
        yc = np.asarray(R[c]["y"])
        if half == 0:
            y_prompt[b, 0:NPO - N_META] = yc[N_META:NPO]
        else:
            y_prompt[b, NPO - N_META:] = yc[0:NPO]
            ncp[0, b] = np.asarray(R[c]["ocp"])
            npp[0, b] = np.asarray(R[c]["opp"])
        y_sample[c * NSEQ:(c + 1) * NSEQ] = yc[NPO:].reshape(NSEQ, T, D)
        ncs[0, c * NSEQ:(c + 1) * NSEQ] = np.asarray(R[c]["ocs"]).reshape(NSEQ, CH, D)
        nps[0, c * NSEQ:(c + 1) * NSEQ] = np.asarray(R[c]["ops"]).reshape(NSEQ, PH, D)
    out = (y_prompt, y_sample, ncp, npp, ncs, nps)
    if trace:
        return out, res
    return out


def kernel(x_prompt, x_sample, state_conv, state_pool, meta_tokens, norm_mix, norm_ffn, norm_final,
           conv_w_in, conv_w_dw, conv_w_out, pool_w, pool_scale, ffn_w_gate_up, ffn_w_down):
    args = [np.asarray(a) for a in (x_prompt, x_sample, state_conv, state_pool, meta_tokens, norm_mix, norm_ffn,
                                    norm_final, conv_w_in, conv_w_dw, conv_w_out, pool_w, pool_scale,
                                    ffn_w_gate_up, ffn_w_down)]
    return run(FULL_CFG, *args)
```

```python
import numpy as np
import concourse.bass as bass
import concourse.mybir as mybir
from concourse.bass_utils import run_bass_kernel_spmd

F32 = mybir.dt.float32
BF16 = mybir.dt.bfloat16
ALU = mybir.AluOpType
AF = mybir.ActivationFunctionType

N_CORES = 8
N_META = 16
HALO = 18
T = 8
CH = 2
PH = 15
EPS = 1e-6
IW = 48
OB = 1536


def make_cfg(D, F, SEQ, DEC_BATCH, PC):
    cfg = dict(D=D, F=F, SEQ=SEQ, DEC_BATCH=DEC_BATCH, PC=PC)
    cfg["NPO"] = (N_META + SEQ) // 2
    cfg["NSEQ"] = DEC_BATCH // N_CORES
    return cfg


FULL_CFG = make_cfg(2048, 5632, 2048, 128, 8)


class Prog:
    ENG = ("pe", "act", "dve", "pool", "sp")

    def __init__(self):
        self.streams = {e: [] for e in self.ENG}
        self.cnt = {}
        self.last_w = {}
        self.readers = {}
        self.waited = {e: {} for e in self.ENG}
        self.pending = {e: [] for e in self.ENG}
        self.semnames = []

    def _sem(self, name):
        if name not in self.cnt:
            self.cnt[name] = 0
            self.semnames.append(name)
        return name

    def op(self, eng, fn, reads=(), writes=(), sem=None, inc=None, signal=True, after=()):
        own = eng if eng in ("pe", "act", "dve", "pool") else None
        if sem is None:
            sem = own
            inc = 1
        self._sem(sem)
        deps = {}

        def add(d, raw):
            if d is None:
                return
            s, v = d
            if s == own and eng == "pe":
                return
            if v > deps.get(s, 0):
                deps[s] = v

        for k in reads:
            add(self.last_w.get(k), True)
        for k in after:
            add(self.last_w.get(k), True)
        for k in writes:
            add(self.last_w.get(k), False)
            for r in self.readers.get(k, {}).items():
                add(r, False)
        waits = []
        wd = self.waited[eng]
        for s, v in deps.items():
            if wd.get(s, 0) < v:
                wd[s] = v
                waits.append((s, v))
        self.pending[eng].append((tuple(reads), tuple(writes)))
        incspec = None
        if signal:
            self.cnt[sem] += inc
            val = self.cnt[sem]
            incspec = (sem, inc)
            for rd, wr in self.pending[eng]:
                for k in rd:
                    d = self.readers.setdefault(k, {})
                    if d.get(sem, 0) < val:
                        d[sem] = val
                for k in wr:
                    self.last_w[k] = (sem, val)
                    self.readers[k] = {}
            self.pending[eng] = []
        self.streams[eng].append((waits, fn, incspec))

    def alias(self, new_keys, old_keys):
        for nk in new_keys:
            rd = self.readers.setdefault(nk, {})
            for ok in old_keys:
                for s, v in self.readers.get(ok, {}).items():
                    if rd.get(s, 0) < v:
                        rd[s] = v
                lw = self.last_w.get(ok)
                if lw is not None and rd.get(lw[0], 0) < lw[1]:
                    rd[lw[0]] = lw[1]

    def final_wait(self, eng, sems):
        waits = [(s, self.cnt[s]) for s in sems if self.cnt.get(s, 0) > 0]
        self.streams[eng].append((waits, None, None))


def build_program(cfg):
    D, F, NPO, NSEQ, PC = cfg["D"], cfg["F"], cfg["NPO"], cfg["NSEQ"], cfg["PC"]
    DC, FC = D // 128, F // 128
    GC = DC // 4
    NP = HALO + NPO
    NS = NSEQ * T
    TOT = NP + NS
    NOUT = NPO + NS
    assert TOT % 2 == 0 and TOT <= OB
    TTs = [(c0, min(512, TOT - c0)) for c0 in range(0, TOT, 512)]
    UWP = CH + NP
    PBL = PH + NP + NSEQ * (PH + T)
    SH = (NSEQ // 2) * PH
    assert SH <= 128 and SH % 2 == 0 and NSEQ * CH <= 128
    NV = 9
    WSLOT = max(DC * 256, PC * 512, GC * 512)
    NWS = 3

    nc = bass.Bass("TRN2", target_bir_lowering=False)

    def din(name, shape):
        return nc.dram_tensor(name, list(shape), F32, kind="ExternalInput").ap()

    def dout(name, shape):
        return nc.dram_tensor(name, list(shape), F32, kind="ExternalOutput").ap()

    xin = din("xin", [TOT, D])
    sconv = din("sconv", [NSEQ * CH, D])
    spool = din("spool", [NSEQ * PH, D])
    vcd = din("vcols", [128, NV * DC])
    invd = din("invc", [128, 4 * IW])
    identd = din("ident", [128, 128])
    w_in = din("w_in", [D, 3 * D])
    w_out = din("w_out", [D, D])
    w_pool = din("w_pool", [4, D // 4, D // 4])
    w_gu = din("w_gu", [2, D, 2 * F])
    w_dn = din("w_dn", [2, F, D])
    y = dout("y", [NOUT, D])
    ocp = dout("ocp", [CH, D])
    opp = dout("opp", [PH, D])
    ocs = dout("ocs", [NSEQ * CH, D])
    ops = dout("ops", [NSEQ * PH, D])

    from contextlib import ExitStack
    with ExitStack() as es:
        def sb(name, shape, dt=F32):
            return es.enter_context(nc.sbuf_tensor(name, list(shape), dt))

        hT = sb("hT", [128, DC, TOT])
        nT = sb("nT", [128, DC, TOT], BF16)
        aT = sb("aT", [128, PC, TOT], BF16)
        wr = [sb(f"wr{s}", [128, WSLOT], BF16) for s in range(NWS)]
        acc = sb("acc", [128, TOT])
        rstd = sb("rstd", [128, TOT])
        tA = [sb(f"tA{i}", [128, TOT]) for i in range(2)]
        SCR = max(2 * UWP + 2 * NSEQ * (CH + T) + 2 * TOT, 4 * PBL)
        scr = sb("scr", [128, SCR])
        vc = sb("vc", [128, NV * DC])
        invc = sb("invc_sb", [128, 4 * IW])
        ident = sb("ident_sb", [128, 128])
        ones = sb("ones", [128, 128])
        rcols = sb("rcols", [128, 16])
        sscol = sb("sscol", [128, 16])
        sscol2 = sb("sscol2", [128, 32])
        rcol = sb("rcol", [128, 16])
        diag = [sb(f"diag{i}", [128, 128]) for i in range(2)]
        sc_t = [sb(f"sc_t{i}", [128, 128]) for i in range(2)]
        sp_t = [sb(f"sp_t{i}", [128, 2, 128]) for i in range(2)]
        ost = [sb(f"ost{i}", [128, 128]) for i in range(4)]
        ostg = [sb(f"ostg{i}", [128, 128]) for i in range(4)]
        ps = es.enter_context(nc.psum_tensor("ps", [128, 8 * 512], F32))
        sems = {}
        block = None

        o = 0
        ubP = []
        for i in range(2):
            ubP.append(scr[:, o:o + UWP]); o += UWP
        ubS = []
        for i in range(2):
            ubS.append(scr[:, o:o + NSEQ * (CH + T)].rearrange("p (s t) -> p s t", t=CH + T)); o += NSEQ * (CH + T)
        cvb = []
        for i in range(2):
            cvb.append(scr[:, o:o + TOT]); o += TOT
        L0_KEYS = ["ubP0", "ubP1", "ubS0", "ubS1", "cvP0", "cvP1", "cvS0", "cvS1"]
        pbs = [scr[:, 0:PBL], scr[:, PBL:2 * PBL]]
        t1 = scr[:, 2 * PBL:3 * PBL]
        t2 = scr[:, 3 * PBL:4 * PBL]
        L1_KEYS = ["pb0", "pb1", "pbh0", "pbh1", "t1", "t2"]
        HD = D // 2
        if PC * TOT // 2 >= 2 * D:
            aT32 = aT[:].bitcast(F32)
            aTflat = aT32.rearrange("p a b -> p (a b)")
            xh = [aTflat[:, q * HD:(q + 1) * HD] for q in range(4)]
        else:
            xts = sb("xts", [128, 2 * D])
            xh = [xts[:, q * HD:(q + 1) * HD] for q in range(4)]
        XT_KEYS = [("xt", q) for q in range(4)]
        AT_KEYS = [("aT", k) for k in range(PC)]

        def vcol(v, j):
            return vc[:, v * DC + j: v * DC + j + 1]

        def psO(X):
            return ps[:, X * OB:(X + 1) * OB]

        def plan(P, units, recording):
            op = P.op
            deferred = []

            MISC = [ps[:, 3072:3584], ps[:, 3584:4096]]
            misc_i = [0]

            def next_misc():
                misc_i[0] ^= 1
                return misc_i[0], MISC[misc_i[0]]

            pso_i = [0]

            def next_pso():
                pso_i[0] ^= 1
                return pso_i[0]

            ucount = [0]
            issued = [0]

            def load_unit(src, nk, ncols):
                i = ucount[0]
                ucount[0] += 1
                if recording:
                    units.append((src, nk, ncols))
                hi = i if recording else min(i + NWS - 1, len(units) - 1)
                while issued[0] <= hi:
                    n_ = issued[0]
                    issued[0] += 1
                    usrc, unk, uncols = units[n_]
                    s_ = n_ % NWS
                    uview = wr[s_][:, 0:unk * uncols].rearrange("p (k n) -> p k n", n=uncols)
                    op("pool", lambda e, uview=uview, usrc=usrc: e.dma_start(out=uview, in_=usrc),
                       writes=[("w", s_)], sem=f"w{s_}", inc=16)
                s = i % NWS
                view = wr[s][:, 0:nk * ncols].rearrange("p (k n) -> p k n", n=ncols)
                return view, ("w", s)

            def prefetch_first():
                if recording:
                    return
                while issued[0] < min(NWS, len(units)):
                    n_ = issued[0]
                    issued[0] += 1
                    usrc, unk, uncols = units[n_]
                    s_ = n_ % NWS
                    uview = wr[s_][:, 0:unk * uncols].rearrange("p (k n) -> p k n", n=uncols)
                    op("pool", lambda e, uview=uview, usrc=usrc: e.dma_start(out=uview, in_=usrc),
                       writes=[("w", s_)], after=[("xt", 0), ("xt", 1), ("xt", 2), ("xt", 3)], sem=f"w{s_}", inc=16)

            def flush_deferred():
                while deferred:
                    deferred.pop(0)()

            CL = [0]

            def tts():
                lo = CL[0]
                return [(c0, min(512, TOT - lo - c0)) for c0 in range(0, TOT - lo, 512)]

            def mm_terms(X, unit, ukey, col0, terms):
                TTl = tts()
                lo = CL[0]
                nmm = len(terms) * len(TTl)
                i = 0
                for idx, (ku, act, akey, ka) in enumerate(terms):
                    for (c0, n) in TTl:
                        i += 1
                        op("pe", lambda e, X=X, ku=ku, ka=ka, act=act, c0=c0, n=n, lo=lo, st=(idx == 0), sp_=(idx == len(terms) - 1):
                           e.matmul(out=ps[:, X * OB + c0: X * OB + c0 + n], lhsT=unit[:, ku, col0:col0 + 128],
                                    rhs=act[:, ka, lo + c0:lo + c0 + n], start=st, stop=sp_),
                           reads=[ukey] + [(ak_, ka) for ak_ in (akey if isinstance(akey, tuple) else (akey,))],
                           writes=[("ps", X)], signal=(i == nmm))
                flush_deferred()

            def mm_chunk(X, unit, ukey, col0, act, akey, ks):
                mm_terms(X, unit, ukey, col0, [(ku, act, akey, ka) for (ku, ka) in ks])

            def mm_pair(Xs, unit, ukey, act, akey, ks):
                TTl = tts()
                lo = CL[0]
                for idx, (ku, ka) in enumerate(ks):
                    for jj, X in enumerate(Xs):
                        for ti, (c0, n) in enumerate(TTl):
                            op("pe", lambda e, X=X, jj=jj, ku=ku, ka=ka, c0=c0, n=n, lo=lo, st=(idx == 0), sp_=(idx == len(ks) - 1):
                               e.matmul(out=ps[:, X * OB + c0: X * OB + c0 + n], lhsT=unit[:, ku, jj * 128:(jj + 1) * 128],
                                        rhs=act[:, ka, lo + c0:lo + c0 + n], start=st, stop=sp_),
                               reads=[ukey, (akey, ka)], writes=[("ps", X)],
                               signal=(idx == len(ks) - 1 and ti == len(TTl) - 1))
                flush_deferred()

            op("sp", lambda e: e.dma_start(out=vc[:, :], in_=vcd), writes=["vc"], sem="c0", inc=16)
            op("sp", lambda e: e.dma_start(out=ident[:, :], in_=identd), writes=["ident"], sem="c1", inc=16)
            op("sp", lambda e: e.dma_start(out=invc[:, :], in_=invd), writes=["invc"], sem="c2", inc=16)
            op("dve", lambda e: e.memset(ones[:, :], 1.0), writes=["ones"])
            op("dve", lambda e: e.memset(scr[:, :], 0.0), writes=L0_KEYS)

            P.alias(XT_KEYS, AT_KEYS)
            ntile = (TOT + 127) // 128
            evac_i = [0]
            HC = DC // 2
            GS = min(4, HC)

            def evac(fn_act, fn_dve, reads, writes):
                evac_i[0] ^= 1
                if evac_i[0]:
                    op("act", fn_act, reads=reads, writes=writes)
                else:
                    op("dve", fn_dve, reads=reads, writes=writes)

            junk = nT[:].rearrange("p a b -> p (a b)")[:, 0:HD]
            op("dve", lambda e: e.memset(sscol2[:, :], 1.0), writes=["sscol2"])
            for i in range(ntile):
                r0 = i * 128
                nr = min(128, TOT - r0)
                if i == (ntile * 6) // 10:
                    prefetch_first()
                for hf in range(2):
                    q = (2 * i + hf) % 4
                    op("sp", lambda e, q=q, r0=r0, nr=nr, hf=hf: e.dma_start(out=xh[q][0:nr, :],
                                                                             in_=xin[r0:r0 + nr, hf * HD:(hf + 1) * HD]),
                       writes=[("xt", q)], sem=f"x{q}", inc=16)
                    op("act", lambda e, q=q, i=i, hf=hf, nr=nr: e.activation(out=junk[0:nr, :], in_=xh[q][0:nr, :], func=AF.Square,
                                                                             accum_out=sscol2[0:nr, 2 * i + hf:2 * i + hf + 1]),
                       reads=[("xt", q)], writes=["sscol2", "junk"])
                    for jb in range(0, HC, GS):
                        mi, mb = next_misc()
                        for jj in range(GS):
                            jl = jb + jj
                            op("pe", lambda e, mb=mb, jj=jj, jl=jl, q=q, nr=nr:
                               e.transpose(out=mb[:, jj * 128: jj * 128 + nr], in_=xh[q][0:nr, jl * 128:(jl + 1) * 128],
                                           identity=ident[0:nr, 0:nr]),
                               reads=[("xt", q), "ident"], writes=[("psm", mi)], signal=(jj == GS - 1))
                        src = mb[:, 0:GS * 128].rearrange("p (a b) -> p a b", b=128)[:, :, 0:nr]
                        j0 = hf * HC + jb
                        dst = hT[:, j0:j0 + GS, r0:r0 + nr]
                        evac(lambda e, dst=dst, src=src: e.copy(out=dst, in_=src),
                             lambda e, dst=dst, src=src: e.tensor_copy(out=dst, in_=src),
                             reads=[("psm", mi)], writes=[("hT", j0 + q_) for q_ in range(GS)])
            P.alias(AT_KEYS, XT_KEYS)
            s2v = sscol2[:, 0:2 * ntile].rearrange("p (a b) -> p a b", b=2)
            op("dve", lambda e: e.tensor_tensor(out=sscol[:, 0:ntile], in0=s2v[:, :, 0], in1=s2v[:, :, 1], op=ALU.add),
               reads=["sscol2"], writes=["sscol"])

            def stats_chunk(j):
                b = tA[j % 2]
                bk = f"tA{j % 2}"
                lo = CL[0]
                op("act", lambda e, b=b, j=j, lo=lo: e.activation(out=b[:, lo:TOT], in_=hT[:, j, lo:TOT], func=AF.Square),
                   reads=[("hT", j)], writes=[bk])
                if j == 0:
                    op("dve", lambda e, b=b, lo=lo: e.tensor_copy(out=acc[:, lo:TOT], in_=b[:, lo:TOT]), reads=[bk], writes=["acc"])
                else:
                    op("dve", lambda e, b=b, lo=lo: e.tensor_tensor(out=acc[:, lo:TOT], in0=acc[:, lo:TOT], in1=b[:, lo:TOT], op=ALU.add),
                       reads=[bk, "acc"], writes=["acc"])

            NTT = (TOT + 127) // 128

            def col_sums(c_off, ntl, ncols_total):
                mi, mb = next_misc()
                for i in range(ntl):
                    r0 = i * 128
                    nr = min(128, ncols_total - r0)
                    op("pe", lambda e, mb=mb, i=i, r0=r0, nr=nr: e.matmul(out=mb[0:nr, 2 * i:2 * i + 2],
                                                                          lhsT=acc[:, c_off + r0:c_off + r0 + nr],
                                                                          rhs=ones[:, 0:2], start=True, stop=True),
                       reads=["acc", "ones"], writes=[("psm", mi)], signal=(i == ntl - 1))
                nfull = ncols_total // 128
                mbv = mb[:, 0:2 * ntl].rearrange("p (a b) -> p a b", b=2)
                if nfull > 0:
                    op("dve", lambda e, mbv=mbv: e.tensor_copy(out=sscol[:, 0:nfull], in_=mbv[:, 0:nfull, 0]),
                       reads=[("psm", mi)], writes=["sscol"])
                if nfull < ntl:
                    nrl = ncols_total - nfull * 128
                    op("dve", lambda e, mbv=mbv: e.tensor_copy(out=sscol[0:nrl, nfull:ntl], in_=mbv[0:nrl, nfull:ntl, 0]),
                       reads=[("psm", mi)], writes=["sscol"])

            def rcol_from_sscol(ntl):
                op("dve", lambda e: e.tensor_scalar(out=rcol[:, 0:ntl], in0=sscol[:, 0:ntl], scalar1=1.0 / D, scalar2=EPS,
                                                    op0=ALU.mult, op1=ALU.add), reads=["sscol"], writes=["rcol"])
                op("act", lambda e: e.activation(out=rcol[:, 0:ntl], in_=rcol[:, 0:ntl], func=AF.Sqrt),
                   reads=["rcol"], writes=["rcol"])
                op("dve", lambda e: e.reciprocal(out=rcol[:, 0:ntl], in_=rcol[:, 0:ntl]), reads=["rcol"], writes=["rcol"])

            def rstd_rows():
                X = next_pso()
                nfull = TOT // 128
                R = tA[0]
                rkeys = [("Rw", i) for i in range(nfull)]
                P.alias(rkeys, ["tA0"])
                for i in range(nfull):
                    op("dve", lambda e, i=i: e.tensor_scalar(out=R[:, i * 128:(i + 1) * 128], in0=ident[:, :],
                                                             scalar1=rcol[:, i:i + 1], scalar2=None, op0=ALU.mult),
                       reads=["rcol", "ident"], writes=[("Rw", i)])
                    if i % 4 == 3 or i == nfull - 1:
                        c0 = (i // 4) * 512
                        c1 = (i + 1) * 128
                        op("pe", lambda e, X=X, c0=c0, c1=c1: e.matmul(out=ps[:, X * OB + c0: X * OB + c1], lhsT=ones[:, :],
                                                                       rhs=R[:, c0:c1], start=True, stop=True),
                           reads=[("Rw", q) for q in range((i // 4) * 4, i + 1)] + ["ones"], writes=[("ps", X)])
                if nfull * 128 < TOT:
                    r0 = nfull * 128
                    nr = TOT - r0
                    dg = diag[0]
                    op("dve", lambda e, dg=dg, nr=nr: e.tensor_scalar(out=dg[0:nr, 0:nr], in0=ident[0:nr, 0:nr],
                                                                      scalar1=rcol[0:nr, nfull:nfull + 1], scalar2=None, op0=ALU.mult),
                       reads=["rcol", "ident"], writes=[("diag", 0)])
                    op("pe", lambda e, X=X, dg=dg, r0=r0, nr=nr: e.matmul(out=ps[:, X * OB + r0: X * OB + r0 + nr],
                                                                          lhsT=ones[0:nr, :], rhs=dg[0:nr, 0:nr],
                                                                          start=True, stop=True),
                       reads=[("diag", 0), "ones"], writes=[("ps", X)])
                P.alias(["tA0"], rkeys)
                op("act", lambda e, X=X: e.copy(out=rstd[:, :], in_=psO(X)[:, 0:TOT]), reads=[("ps", X)], writes=["rstd"])

            def stats_finish():
                col_sums(0, NTT, TOT)
                rcol_from_sscol(NTT)
                rstd_rows()

            def apply_norm(v):
                for j in range(DC):
                    op("dve", lambda e, j=j, lo=CL[0]: e.scalar_tensor_tensor(out=nT[:, j, lo:TOT], in0=hT[:, j, lo:TOT], scalar=vcol(v, j),
                                                                           in1=rstd[:, lo:TOT], op0=ALU.mult, op1=ALU.mult),
                       reads=[("hT", j), "rstd", "vc"], writes=[("nT", j)])

            def resid_matmul(unit_src_fn, nk, last, gscale=None):
                for ob in range(D // 512):
                    unit, ukey = load_unit(unit_src_fn(ob), nk, 512)
                    for jo in range(4):
                        j = ob * 4 + jo
                        X = next_pso()
                        mm_chunk(X, unit, ukey, jo * 128, aT, "aT", [(k, k) for k in range(nk)])
                        op("dve", lambda e, j=j, X=X, lo=CL[0]: e.tensor_tensor(out=hT[:, j, lo:TOT], in0=hT[:, j, lo:TOT],
                                                                                in1=psO(X)[:, 0:TOT - lo], op=ALU.add),
                           reads=[("ps", X), ("hT", j)], writes=[("hT", j)])
                        if last:
                            stats_chunk(j)
                            if gscale is not None:
                                op("act", lambda e, j=j, lo=CL[0]: e.activation(out=hT[:, j, lo:TOT], in_=hT[:, j, lo:TOT], func=AF.Copy,
                                                                             scale=vcol(gscale, j)),
                                   reads=[("hT", j), "vc"], writes=[("hT", j)])

            def ffn(layer, gscale=None):
                parts = []
                f0 = 0
                nparts = (FC + PC - 1) // PC
                base, rem = FC // nparts, FC % nparts
                for pi in range(nparts):
                    n = base + (1 if pi < rem else 0)
                    parts.append((f0, n))
                    f0 += n
                for pi, (f0, n) in enumerate(parts):
                    for q0 in range(0, n, 2):
                        nq = min(2, n - q0)
                        fa = f0 + q0
                        gsrc = w_gu[layer, :, fa * 128:(fa + nq) * 128].rearrange("(k p) n -> p k n", p=128)
                        usrc = w_gu[layer, :, F + fa * 128:F + (fa + nq) * 128].rearrange("(k p) n -> p k n", p=128)
                        gun, gk = load_unit(gsrc, DC, nq * 128)
                        gXs = [next_pso() for jj in range(nq)]
                        first = (pi == 0 and q0 == 0 and nq == 2)
                        if first:
                            mm_pair(gXs, gun, gk, nT, "nT", [(k, k) for k in range(DC)])
                        for jj in range(nq):
                            X = gXs[jj]
                            if not first:
                                mm_chunk(X, gun, gk, jj * 128, nT, "nT", [(k, k) for k in range(DC)])
                            op("act", lambda e, jj=jj, X=X, lo=CL[0]: e.activation(out=tA[jj][:, lo:TOT], in_=psO(X)[:, 0:TOT - lo],
                                                                                func=AF.Silu),
                               reads=[("ps", X)], writes=[f"tA{jj}"])
                        uun, uk = load_unit(usrc, DC, nq * 128)
                        for jj in range(nq):
                            X = next_pso()
                            mm_chunk(X, uun, uk, jj * 128, nT, "nT", [(k, k) for k in range(DC)])
                            fl = q0 + jj
                            op("dve", lambda e, jj=jj, X=X, fl=fl, lo=CL[0]: e.tensor_tensor(out=aT[:, fl, lo:TOT], in0=tA[jj][:, lo:TOT],
                                                                                            in1=psO(X)[:, 0:TOT - lo], op=ALU.mult),
                               reads=[("ps", X), f"tA{jj}"], writes=[("aT", fl)])
                    resid_matmul(lambda ob, f0=f0, n=n: w_dn[layer, f0 * 128:(f0 + n) * 128, ob * 512:(ob + 1) * 512]
                                 .rearrange("(k p) n -> p k n", p=128), n, last=(pi == len(parts) - 1),
                                 gscale=(gscale if pi == len(parts) - 1 else None))

            so_i = [0]

            def state_out(src_ap, keys, ncols, dsts):
                def go():
                    mi, mb = next_misc()
                    op("pe", lambda e, mb=mb: e.transpose(out=mb[0:ncols, 0:128], in_=src_ap, identity=ident[:, :]),
                       reads=list(keys) + ["ident"], writes=[("psm", mi)])
                    s = so_i[0] % 4
                    so_i[0] += 1
                    op("act", lambda e, s=s, mb=mb: e.copy(out=ostg[s][0:ncols, :], in_=mb[0:ncols, 0:128]),
                       reads=[("psm", mi)], writes=[("ostg", s)])
                    for (r0, nrw, dst) in dsts:
                        op("sp", lambda e, s=s, r0=r0, nrw=nrw, dst=dst: e.dma_start(out=dst, in_=ostg[s][r0:r0 + nrw, :]),
                           reads=[("ostg", s)], sem=f"so{s}", inc=16)
                deferred.append(go)

            ost_i = [0]

            def next_ost():
                ost_i[0] = (ost_i[0] + 1) % len(ost)
                return ost[ost_i[0]], ("ost", ost_i[0])

            P.alias([("nT", j) for j in range(DC)], ["junk"])
            prefetch_first()
            rcol_from_sscol(NTT)
            rstd_rows()
            apply_norm(0)

            nparts0 = (DC + PC - 1) // PC
            assert DC % nparts0 == 0
            pc0 = DC // nparts0
            full_k = [(k, k) for k in range(DC)]
            for pa in range(nparts0):
                for q0 in range(0, pc0, 2):
                    nq = min(2, pc0 - q0)
                    ja = pa * pc0 + q0
                    csrc = w_in[:, D + ja * 128: D + (ja + nq) * 128].rearrange("(k p) n -> p k n", p=128)
                    vsrc = w_in[:, 2 * D + ja * 128: 2 * D + (ja + nq) * 128].rearrange("(k p) n -> p k n", p=128)
                    bsrc = w_in[:, ja * 128:(ja + nq) * 128].rearrange("(k p) n -> p k n", p=128)
                    for jj in range(nq):
                        j = ja + jj
                        s = j % 2
                        op("sp", lambda e, s=s, j=j: e.dma_start(out=sc_t[s][0:NSEQ * CH, :], in_=sconv[:, j * 128:(j + 1) * 128]),
                           writes=[("sc_t", s)], sem=f"sc{s}", inc=16)
                        mi, mb = next_misc()
                        op("pe", lambda e, mb=mb, s=s: e.transpose(out=mb[:, 0:NSEQ * CH], in_=sc_t[s][0:NSEQ * CH, :],
                                                                   identity=ident[0:NSEQ * CH, 0:NSEQ * CH]),
                           reads=[("sc_t", s), "ident"], writes=[("psm", mi)])
                        op("act", lambda e, mb=mb, jj=jj: e.copy(out=ubS[jj][:, :, 0:CH],
                                                                 in_=mb[:, 0:NSEQ * CH].rearrange("p (s r) -> p s r", r=CH)),
                           reads=[("psm", mi)], writes=[f"ubS{jj}"])
                    cun, ck = load_unit(csrc, DC, nq * 128)
                    cXs = [next_pso() for jj in range(nq)]
                    if pa == 0 and q0 == 0 and nq == 2:
                        mm_pair(cXs, cun, ck, nT, "nT", full_k)
                    for jj in range(nq):
                        X = cXs[jj]
                        if not (pa == 0 and q0 == 0 and nq == 2):
                            mm_chunk(X, cun, ck, jj * 128, nT, "nT", full_k)
                        op("act", lambda e, jj=jj, X=X: e.copy(out=tA[jj][:, :], in_=psO(X)[:, 0:TOT]),
                           reads=[("ps", X)], writes=[f"tA{jj}"])
                    vun, vk = load_unit(vsrc, DC, nq * 128)
                    for jj in range(nq):
                        j = ja + jj
                        X = next_pso()
                        mm_chunk(X, vun, vk, jj * 128, nT, "nT", full_k)
                        op("dve", lambda e, jj=jj, X=X: e.tensor_tensor(out=ubP[jj][:, CH:CH + NP], in0=tA[jj][:, 0:NP],
                                                                        in1=psO(X)[:, 0:NP], op=ALU.mult),
                           reads=[("ps", X), f"tA{jj}"], writes=[f"ubP{jj}"])
                        op("dve", lambda e, jj=jj, X=X: e.tensor_tensor(
                            out=ubS[jj][:, :, CH:CH + T], in0=tA[jj][:, NP:TOT].rearrange("p (s t) -> p s t", t=T),
                            in1=psO(X)[:, NP:TOT].rearrange("p (s t) -> p s t", t=T), op=ALU.mult),
                           reads=[("ps", X), f"tA{jj}"], writes=[f"ubS{jj}"])
                        cvP = cvb[jj][:, 0:NP]
                        cvS = cvb[jj][:, NP:TOT].rearrange("p (s t) -> p s t", t=T)
                        op("act", lambda e, jj=jj, j=j, cvP=cvP: e.activation(out=cvP, in_=ubP[jj][:, 2:2 + NP], func=AF.Copy,
                                                                              scale=vcol(7, j)),
                           reads=[f"ubP{jj}", "vc"], writes=[f"cvP{jj}"])
                        op("act", lambda e, jj=jj, j=j, cvS=cvS: e.activation(out=cvS, in_=ubS[jj][:, :, 2:2 + T], func=AF.Copy,
                                                                              scale=vcol(7, j)),
                           reads=[f"ubS{jj}", "vc"], writes=[f"cvS{jj}"])
                        for tap in (1, 0):
                            op("dve", lambda e, jj=jj, j=j, tap=tap, cvP=cvP: e.scalar_tensor_tensor(
                                out=cvP, in0=ubP[jj][:, tap:tap + NP], scalar=vcol(5 + tap, j), in1=cvP,
                                op0=ALU.mult, op1=ALU.add),
                               reads=[f"ubP{jj}", f"cvP{jj}", "vc"], writes=[f"cvP{jj}"])
                            op("dve", lambda e, jj=jj, j=j, tap=tap, cvS=cvS: e.scalar_tensor_tensor(
                                out=cvS, in0=ubS[jj][:, :, tap:tap + T], scalar=vcol(5 + tap, j), in1=cvS,
                                op0=ALU.mult, op1=ALU.add),
                               reads=[f"ubS{jj}", f"cvS{jj}", "vc"], writes=[f"cvS{jj}"])
                        so, sok = next_ost()
                        op("act", lambda e, jj=jj, so=so: e.copy(out=so[:, 0:CH], in_=ubP[jj][:, NP:NP + CH]),
                           reads=[f"ubP{jj}"], writes=[sok])
                        op("act", lambda e, jj=jj, so=so: e.copy(
                            out=so[:, CH:CH + NSEQ * CH].rearrange("p (s r) -> p s r", r=CH), in_=ubS[jj][:, :, T:T + CH]),
                           reads=[f"ubS{jj}"], writes=[sok])
                        state_out(so[:, 0:CH + NSEQ * CH], [sok], CH + NSEQ * CH,
                                  [(0, CH, ocp[:, j * 128:(j + 1) * 128]),
                                   (CH, NSEQ * CH, ocs[:, j * 128:(j + 1) * 128])])
                    bun, bk = load_unit(bsrc, DC, nq * 128)
                    for jj in range(nq):
                        X = next_pso()
                        mm_chunk(X, bun, bk, jj * 128, nT, "nT", full_k)
                        gl = q0 + jj
                        op("dve", lambda e, jj=jj, X=X, gl=gl: e.tensor_tensor(out=aT[:, gl, :], in0=cvb[jj][:, :],
                                                                              in1=psO(X)[:, 0:TOT], op=ALU.mult),
                           reads=[("ps", X), f"cvP{jj}", f"cvS{jj}"], writes=[("aT", gl)])
                resid_matmul(lambda ob, pa=pa: w_out[pa * pc0 * 128:(pa + 1) * pc0 * 128, ob * 512:(ob + 1) * 512]
                             .rearrange("(k p) n -> p k n", p=128), pc0, last=(pa == nparts0 - 1))
            stats_finish()
            apply_norm(2)
            ffn(0)

            stats_finish()
            flush_deferred()
            P.alias(L1_KEYS, L0_KEYS)
            for j in range(DC):
                P.alias([("nTfix", j)], [("nT", j)])
            SW = PH + T
            for pbx, pbk in zip(pbs, ("pbh0", "pbh1")):
                op("dve", lambda e, pbx=pbx: e.memset(pbx[:, 0:PH], 0.0), writes=[pbk])
            def spool_dma(j):
                s = j % 2
                for hh in range(2):
                    op("sp", lambda e, s=s, hh=hh, j=j: e.dma_start(out=sp_t[s][0:SH, hh, :],
                                                                    in_=spool[hh * SH:(hh + 1) * SH, j * 128:(j + 1) * 128]),
                       writes=[("sp_t", s)], sem=f"spl{s}", inc=16)

            def pool_prep(j):
                s = j % 2
                pb, pbk, pbh = pbs[s], f"pb{s}", f"pbh{s}"
                pbS = pb[:, PH + NP: PBL].rearrange("p (s t) -> p s t", t=SW)
                if j == 0:
                    spool_dma(0)
                if j + 1 < DC:
                    spool_dma(j + 1)
                mi, mb = next_misc()
                for hh in range(2):
                    op("pe", lambda e, mb=mb, s=s, hh=hh: e.transpose(out=mb[:, hh * SH:(hh + 1) * SH], in_=sp_t[s][0:SH, hh, :],
                                                                     identity=ident[0:SH, 0:SH]),
                       reads=[("sp_t", s), "ident"], writes=[("psm", mi)], signal=(hh == 1))
                op("act", lambda e, mb=mb, pbS=pbS: e.copy(out=pbS[:, :, 0:PH],
                                                           in_=mb[:, 0:2 * SH].rearrange("p (s r) -> p s r", r=PH)),
                   reads=[("psm", mi)], writes=[pbh])
                op("dve", lambda e, j=j, pb=pb: e.scalar_tensor_tensor(out=pb[:, PH:PH + NP], in0=hT[:, j, 0:NP], scalar=vcol(1, j),
                                                                       in1=rstd[:, 0:NP], op0=ALU.mult, op1=ALU.mult),
                   reads=[("hT", j), "rstd", "vc"], writes=[pbk])
                op("dve", lambda e, j=j, pbS=pbS: e.scalar_tensor_tensor(
                    out=pbS[:, :, PH:SW], in0=hT[:, j, NP:TOT].rearrange("p (s t) -> p s t", t=T), scalar=vcol(1, j),
                    in1=rstd[:, NP:TOT].rearrange("p (s t) -> p s t", t=T), op0=ALU.mult, op1=ALU.mult),
                   reads=[("hT", j), "rstd", "vc"], writes=[pbk])
                sl = ((j // GC) % 2) * GC + (j % GC)
                op("act", lambda e, sl=sl, pb=pb: e.activation(out=aT[:, sl, 0:NP], in_=pb[:, PH:PH + NP], func=AF.Copy, scale=-1.0),
                   reads=[pbk], writes=[("aT", sl)])
                op("act", lambda e, sl=sl, pbS=pbS: e.activation(out=aT[:, sl, NP:TOT].rearrange("p (s t) -> p s t", t=T),
                                                                 in_=pbS[:, :, PH:SW], func=AF.Copy, scale=-1.0),
                   reads=[pbk], writes=[("aT", sl)])
                so, sok = next_ost()
                op("act", lambda e, so=so, pb=pb: e.copy(out=so[:, 0:PH], in_=pb[:, NP:NP + PH]), reads=[pbk], writes=[sok])
                state_out(so[:, 0:PH], [sok], PH, [(0, PH, opp[:, j * 128:(j + 1) * 128])])
                for hh in range(2):
                    so, sok = next_ost()
                    op("act", lambda e, so=so, hh=hh, pbS=pbS: e.copy(
                        out=so[:, 0:SH].rearrange("p (s r) -> p s r", r=PH),
                        in_=pbS[:, hh * (NSEQ // 2):(hh + 1) * (NSEQ // 2), T:SW]),
                       reads=[pbk, pbh], writes=[sok])
                    state_out(so[:, 0:SH], [sok], SH, [(0, SH, ops[hh * SH:(hh + 1) * SH, j * 128:(j + 1) * 128])])
                flush_deferred()

            last_fin = [1]

            def pool_compute(j):
                g = j // GC
                w = 2 ** (g + 1)
                s = j % 2
                pb, pbk, pbh = pbs[s], f"pb{s}", f"pbh{s}"
                pbS = pb[:, PH + NP: PBL].rearrange("p (s t) -> p s t", t=SW)
                cur, curk = pb, [pbk, pbh]
                bufs = [(t1, "t1"), (t2, "t2")]
                sh = 1
                bi = 1 - last_fin[0]
                while sh < w:
                    nb, nbk = bufs[bi % 2]
                    bi += 1
                    lo = 2 * sh - 1
                    op("dve", lambda e, nb=nb, cur=cur, lo=lo, sh=sh: e.tensor_tensor(
                        out=nb[:, lo:PBL], in0=cur[:, lo:PBL], in1=cur[:, lo - sh:PBL - sh], op=ALU.add),
                       reads=curk, writes=[nbk])
                    cur, curk = nb, [nbk]
                    sh *= 2
                last_fin[0] = (bi - 1) % 2
                curS = cur[:, PH + NP: PBL].rearrange("p (s t) -> p s t", t=SW)
                op("act", lambda e, j=j, cur=cur, w=w: e.activation(out=nT[:, j, IW:NP], in_=cur[:, PH + IW:PH + NP], func=AF.Copy,
                                                                    scale=1.0 / w),
                   reads=curk, writes=[("nT", j)])
                op("act", lambda e, j=j, curS=curS, w=w: e.activation(out=nT[:, j, NP:TOT].rearrange("p (s t) -> p s t", t=T),
                                                                      in_=curS[:, :, PH:SW], func=AF.Copy, scale=1.0 / w),
                   reads=curk, writes=[("nT", j)])
                op("dve", lambda e, j=j, cur=cur, g=g: e.tensor_tensor(out=nT[:, j, 0:IW], in0=cur[:, PH:PH + IW],
                                                                      in1=invc[:, g * IW:(g + 1) * IW], op=ALU.mult),
                   reads=curk + ["invc"], writes=[("nTfix", j)])

            mmq = []
            tailq = []

            def plan_mm(n):
                for _ in range(n):
                    if not mmq:
                        return
                    g_, jo, unit, ukey = mmq.pop(0)
                    jj_ = g_ * GC + jo
                    X = next_pso()
                    mm_terms(X, unit, ukey, jo * 128,
                             [(k, nT, ("nT", "nTfix"), g_ * GC + k) for k in range(GC)] +
                             [(k, aT, "aT", (g_ % 2) * GC + k) for k in range(GC)])

                    def tail_a(jj_=jj_, X=X):
                        op("dve", lambda e, j=jj_, X=X, lo=CL[0]: e.scalar_tensor_tensor(out=hT[:, j, lo:TOT], in0=psO(X)[:, 0:TOT - lo],
                                                                                      scalar=vcol(8, j), in1=hT[:, j, lo:TOT],
                                                                                      op0=ALU.mult, op1=ALU.add),
                           reads=[("ps", X), ("hT", jj_), "vc"], writes=[("hT", jj_)])
                    tailq.append((tail_a, jj_))

            def run_tails():
                js = []
                while tailq:
                    fn, jj_ = tailq.pop(0)
                    fn()
                    js.append(jj_)
                for jj_ in js:
                    stats_chunk(jj_)

            CL[0] = HALO
            pool_prep(0)
            for j in range(DC):
                if j + 1 < DC:
                    pool_prep(j + 1)
                pool_compute(j)
                run_tails()
                if j % GC == GC - 1:
                    g = j // GC
                    unit, ukey = load_unit(w_pool[g].rearrange("(k p) n -> p k n", p=128), GC, GC * 128)
                    for jo in range(GC):
                        mmq.append((g, jo, unit, ukey))
                plan_mm(2)
            while mmq or tailq:
                run_tails()
                plan_mm(2)
            stats_finish()
            apply_norm(3)
            ffn(1, gscale=4)

            P.alias(XT_KEYS, AT_KEYS)
            notile = (NOUT + 127) // 128
            col_sums(HALO, notile, NOUT)
            rcol_from_sscol(notile)
            for i in range(notile):
                r0 = i * 128
                nr = min(128, NOUT - r0)
                c0 = HALO + r0
                for hf in range(2):
                    q = (2 * i + hf) % 4
                    for jb in range(0, HC, GS):
                        mi, mb = next_misc()
                        for jj in range(GS):
                            j = hf * HC + jb + jj
                            op("pe", lambda e, mb=mb, jj=jj, j=j, c0=c0, nr=nr:
                               e.transpose(out=mb[0:nr, jj * 128:(jj + 1) * 128], in_=hT[:, j, c0:c0 + nr], identity=ident[:, :]),
                               reads=[("hT", j), "ident"], writes=[("psm", mi)], signal=(jj == GS - 1))
                        dst = xh[q][0:nr, jb * 128:(jb + GS) * 128]
                        src = mb[0:nr, 0:GS * 128]
                        rc = rcol[0:nr, i:i + 1]
                        evac(lambda e, dst=dst, src=src, rc=rc: e.activation(out=dst, in_=src, func=AF.Copy, scale=rc),
                             lambda e, dst=dst, src=src, rc=rc: e.tensor_scalar(out=dst, in0=src, scalar1=rc, scalar2=None, op0=ALU.mult),
                             reads=[("psm", mi), "rcol"], writes=[("xt", q)])
                    op("sp", lambda e, q=q, r0=r0, nr=nr, hf=hf: e.dma_start(out=y[r0:r0 + nr, hf * HD:(hf + 1) * HD], in_=xh[q][0:nr, :]),
                       reads=[("xt", q)], sem=f"yo{q}", inc=16)
            flush_deferred()
            P.final_wait("sp", [n for n in P.semnames if n.startswith("yo") or n.startswith("so")])

        P1 = Prog()
        units = []
        plan(P1, units, True)
        P = Prog()
        plan(P, units, False)

        for name in P.semnames:
            sems[name] = es.enter_context(nc.semaphore(name))
        with nc.Block() as block:
            def emit(ename):
                def body(eng):
                    for waits, fn, incspec in P.streams[ename]:
                        for (sname, val) in waits:
                            eng.wait_ge(sems[sname], val)
                        if fn is None:
                            continue
                        ins = fn(eng)
                        if incspec is not None:
                            ins.then_inc(sems[incspec[0]], incspec[1])
                return body
            block.tensor(emit("pe"))
            block.scalar(emit("act"))
            block.vector(emit("dve"))
            block.gpsimd(emit("pool"))
            block.sync(emit("sp"))
    return nc


def _col_layout(v, DC):
    return np.ascontiguousarray(v.reshape(DC, 128).T)


def run(cfg, x_prompt, x_sample, state_conv, state_pool, meta_tokens, norm_mix, norm_ffn, norm_final,
        conv_w_in, conv_w_dw, conv_w_out, pool_w, pool_scale, ffn_w_gate_up, ffn_w_down, trace=False):
    D, F, NPO, NSEQ = cfg["D"], cfg["F"], cfg["NPO"], cfg["NSEQ"]
    DC = D // 128
    B = x_prompt.shape[0]
    SEQ = x_prompt.shape[1]
    f32 = np.float32
    nc = build_program(cfg)

    vecs = [norm_mix[0], norm_mix[1], norm_ffn[0], norm_ffn[1], norm_final,
            conv_w_dw[0, 0], conv_w_dw[0, 1], conv_w_dw[0, 2], pool_scale[0]]
    vcols = np.ascontiguousarray(np.concatenate([_col_layout(np.asarray(v, f32), DC) for v in vecs], axis=1))
    ident = np.eye(128, dtype=f32)
    shared = dict(
        vcols=vcols, ident=ident,
        w_in=np.ascontiguousarray(conv_w_in[0], f32), w_out=np.ascontiguousarray(conv_w_out[0], f32),
        w_pool=np.ascontiguousarray(pool_w[0], f32), w_gu=np.ascontiguousarray(ffn_w_gate_up, f32),
        w_dn=np.ascontiguousarray(ffn_w_down, f32),
    )
    wins = np.array([2.0, 4.0, 8.0, 16.0], f32)
    in_maps = []
    for c in range(N_CORES):
        b, half = c // 2, c % 2
        full = np.concatenate([np.asarray(meta_tokens, f32), np.asarray(x_prompt[b], f32)], axis=0)
        start = half * NPO
        lo = start - HALO
        if lo < 0:
            seg = np.concatenate([np.zeros((-lo, D), f32), full[0:start + NPO]], axis=0)
        else:
            seg = full[lo:start + NPO]
        xs = np.asarray(x_sample[c * NSEQ:(c + 1) * NSEQ], f32).reshape(NSEQ * T, D)
        xin = np.ascontiguousarray(np.concatenate([seg, xs], axis=0))
        pos = (lo + np.arange(IW)).astype(np.float64)
        cnt = np.minimum(wins[:, None].astype(np.float64), np.maximum(pos[None, :], 0.0) + 1.0)
        invc = np.broadcast_to((1.0 / cnt).astype(f32).reshape(1, 4 * IW), (128, 4 * IW))
        m = dict(shared)
        m.update(
            xin=xin,
            sconv=np.ascontiguousarray(np.asarray(state_conv[0, c * NSEQ:(c + 1) * NSEQ], f32).reshape(NSEQ * CH, D)),
            spool=np.ascontiguousarray(np.asarray(state_pool[0, c * NSEQ:(c + 1) * NSEQ], f32).reshape(NSEQ * PH, D)),
            invc=np.ascontiguousarray(invc),
        )
        in_maps.append(m)
    res = run_bass_kernel_spmd(nc, in_maps, core_ids=list(range(N_CORES)), trace=trace)
    R = res.results
    DECB = x_sample.shape[0]
    y_prompt = np.empty((B, SEQ, D), f32)
    y_sample = np.empty((DECB, T, D), f32)
    ncp = np.empty((1, B, CH, D), f32)
    npp = np.empty((1, B, PH, D), f32)
    ncs = np.empty((1, DECB, CH, D), f32)
    nps = np.empty((1, DECB, PH, D), f32)
    for c in range(N_CORES):
        b, half = c // 2, c % 2
        yc = np.asarray(R[c]["y"])
        if half == 0:
            y_prompt[b, 0:NPO - N_META] = yc[N_META:NPO]
        else:
            y_prompt[b, NPO - N_META:] = yc[0:NPO]
            ncp[0, b] = np.asarray(R[c]["ocp"])
            npp[0, b] = np.asarray(R[c]["opp"])
        y_sample[c * NSEQ:(c + 1) * NSEQ] = yc[NPO:].reshape(NSEQ, T, D)
        ncs[0, c * NSEQ:(c + 1) * NSEQ] = np.asarray(R[c]["ocs"]).reshape(NSEQ, CH, D)
        nps[0, c * NSEQ:(c + 1) * NSEQ] = np.asarray(R[c]["ops"]).reshape(NSEQ, PH, D)
    out = (y_prompt, y_sample, ncp, npp, ncs, nps)
    if trace:
        return out, res
    return out


def kernel(x_prompt, x_sample, state_conv, state_pool, meta_tokens, norm_mix, norm_ffn, norm_final,
           conv_w_in, conv_w_dw, conv_w_out, pool_w, pool_scale, ffn_w_gate_up, ffn_w_down):
    args = [np.asarray(a) for a in (x_prompt, x_sample, state_conv, state_pool, meta_tokens, norm_mix, norm_ffn,
                                    norm_final, conv_w_in, conv_w_dw, conv_w_out, pool_w, pool_scale,
                                    ffn_w_gate_up, ffn_w_down)]
    return run(FULL_CFG, *args)
```

```python
import numpy as np
import concourse.bass as bass
import concourse.mybir as mybir
from concourse.bass_utils import run_bass_kernel_spmd

F32 = mybir.dt.float32
BF16 = mybir.dt.bfloat16
ALU = mybir.AluOpType
AF = mybir.ActivationFunctionType

N_CORES = 8
N_META = 16
HALO = 18
T = 8
CH = 2
PH = 15
EPS = 1e-6
IW = 48
OB = 1536


def make_cfg(D, F, SEQ, DEC_BATCH, PC):
    cfg = dict(D=D, F=F, SEQ=SEQ, DEC_BATCH=DEC_BATCH, PC=PC)
    cfg["NPO"] = (N_META + SEQ) // 2
    cfg["NSEQ"] = DEC_BATCH // N_CORES
    return cfg


FULL_CFG = make_cfg(2048, 5632, 2048, 128, 8)


class Prog:
    ENG = ("pe", "act", "dve", "pool", "sp")

    def __init__(self):
        self.streams = {e: [] for e in self.ENG}
        self.cnt = {}
        self.last_w = {}
        self.readers = {}
        self.waited = {e: {} for e in self.ENG}
        self.pending = {e: [] for e in self.ENG}
        self.semnames = []

    def _sem(self, name):
        if name not in self.cnt:
            self.cnt[name] = 0
            self.semnames.append(name)
        return name

    def op(self, eng, fn, reads=(), writes=(), sem=None, inc=None, signal=True, after=()):
        own = eng if eng in ("pe", "act", "dve", "pool") else None
        if sem is None:
            sem = own
            inc = 1
        self._sem(sem)
        deps = {}

        def add(d, raw):
            if d is None:
                return
            s, v = d
            if s == own and eng == "pe":
                return
            if v > deps.get(s, 0):
                deps[s] = v

        for k in reads:
            add(self.last_w.get(k), True)
        for k in after:
            add(self.last_w.get(k), True)
        for k in writes:
            add(self.last_w.get(k), False)
            for r in self.readers.get(k, {}).items():
                add(r, False)
        waits = []
        wd = self.waited[eng]
        for s, v in deps.items():
            if wd.get(s, 0) < v:
                wd[s] = v
                waits.append((s, v))
        self.pending[eng].append((tuple(reads), tuple(writes)))
        incspec = None
        if signal:
            self.cnt[sem] += inc
            val = self.cnt[sem]
            incspec = (sem, inc)
            for rd, wr in self.pending[eng]:
                for k in rd:
                    d = self.readers.setdefault(k, {})
                    if d.get(sem, 0) < val:
                        d[sem] = val
                for k in wr:
                    self.last_w[k] = (sem, val)
                    self.readers[k] = {}
            self.pending[eng] = []
        self.streams[eng].append((waits, fn, incspec))

    def alias(self, new_keys, old_keys):
        for nk in new_keys:
            rd = self.readers.setdefault(nk, {})
            for ok in old_keys:
                for s, v in self.readers.get(ok, {}).items():
                    if rd.get(s, 0) < v:
                        rd[s] = v
                lw = self.last_w.get(ok)
                if lw is not None and rd.get(lw[0], 0) < lw[1]:
                    rd[lw[0]] = lw[1]

    def final_wait(self, eng, sems):
        waits = [(s, self.cnt[s]) for s in sems if self.cnt.get(s, 0) > 0]
        self.streams[eng].append((waits, None, None))


def build_program(cfg):
    D, F, NPO, NSEQ, PC = cfg["D"], cfg["F"], cfg["NPO"], cfg["NSEQ"], cfg["PC"]
    DC, FC = D // 128, F // 128
    GC = DC // 4
    NP = HALO + NPO
    NS = NSEQ * T
    TOT = NP + NS
    NOUT = NPO + NS
    assert TOT % 2 == 0 and TOT <= OB
    TTs = [(c0, min(512, TOT - c0)) for c0 in range(0, TOT, 512)]
    UWP = CH + NP
    PBL = PH + NP + NSEQ * (PH + T)
    SH = (NSEQ // 2) * PH
    assert SH <= 128 and SH % 2 == 0 and NSEQ * CH <= 128
    NV = 9
    WSLOT = max(DC * 256, PC * 512, GC * 512)
    NWS = 3

    nc = bass.Bass("TRN2", target_bir_lowering=False)

    def din(name, shape):
        return nc.dram_tensor(name, list(shape), F32, kind="ExternalInput").ap()

    def dout(name, shape):
        return nc.dram_tensor(name, list(shape), F32, kind="ExternalOutput").ap()

    xin = din("xin", [TOT, D])
    sconv = din("sconv", [NSEQ * CH, D])
    spool = din("spool", [NSEQ * PH, D])
    vcd = din("vcols", [128, NV * DC])
    invd = din("invc", [128, 4 * IW])
    identd = din("ident", [128, 128])
    w_in = din("w_in", [D, 3 * D])
    w_out = din("w_out", [D, D])
    w_pool = din("w_pool", [4, D // 4, D // 4])
    w_gu = din("w_gu", [2, D, 2 * F])
    w_dn = din("w_dn", [2, F, D])
    y = dout("y", [NOUT, D])
    ocp = dout("ocp", [CH, D])
    opp = dout("opp", [PH, D])
    ocs = dout("ocs", [NSEQ * CH, D])
    ops = dout("ops", [NSEQ * PH, D])

    from contextlib import ExitStack
    with ExitStack() as es:
        def sb(name, shape, dt=F32):
            return es.enter_context(nc.sbuf_tensor(name, list(shape), dt))

        hT = sb("hT", [128, DC, TOT])
        nT = sb("nT", [128, DC, TOT], BF16)
        aT = sb("aT", [128, PC, TOT], BF16)
        wr = [sb(f"wr{s}", [128, WSLOT], BF16) for s in range(NWS)]
        acc = sb("acc", [128, TOT])
        rstd = sb("rstd", [128, TOT])
        tA = [sb(f"tA{i}", [128, TOT]) for i in range(2)]
        SCR = max(2 * UWP + 2 * NSEQ * (CH + T) + 2 * TOT, 4 * PBL)
        scr = sb("scr", [128, SCR])
        vc = sb("vc", [128, NV * DC])
        invc = sb("invc_sb", [128, 4 * IW])
        ident = sb("ident_sb", [128, 128])
        ones = sb("ones", [128, 128])
        rcols = sb("rcols", [128, 16])
        sscol = sb("sscol", [128, 16])
        sscol2 = sb("sscol2", [128, 32])
        rcol = sb("rcol", [128, 16])
        diag = [sb(f"diag{i}", [128, 128]) for i in range(2)]
        sc_t = [sb(f"sc_t{i}", [128, 128]) for i in range(2)]
        sp_t = [sb(f"sp_t{i}", [128, 2, 128]) for i in range(2)]
        ost = [sb(f"ost{i}", [128, 128]) for i in range(4)]
        ostg = [sb(f"ostg{i}", [128, 128]) for i in range(4)]
        ps = es.enter_context(nc.psum_tensor("ps", [128, 8 * 512], F32))
        sems = {}
        block = None

        o = 0
        ubP = []
        for i in range(2):
            ubP.append(scr[:, o:o + UWP]); o += UWP
        ubS = []
        for i in range(2):
            ubS.append(scr[:, o:o + NSEQ * (CH + T)].rearrange("p (s t) -> p s t", t=CH + T)); o += NSEQ * (CH + T)
        cvb = []
        for i in range(2):
            cvb.append(scr[:, o:o + TOT]); o += TOT
        L0_KEYS = ["ubP0", "ubP1", "ubS0", "ubS1", "cvP0", "cvP1", "cvS0", "cvS1"]
        pbs = [scr[:, 0:PBL], scr[:, PBL:2 * PBL]]
        t1 = scr[:, 2 * PBL:3 * PBL]
        t2 = scr[:, 3 * PBL:4 * PBL]
        L1_KEYS = ["pb0", "pb1", "pbh0", "pbh1", "t1", "t2"]
        HD = D // 2
        if PC * TOT // 2 >= 2 * D:
            aT32 = aT[:].bitcast(F32)
            aTflat = aT32.rearrange("p a b -> p (a b)")
            xh = [aTflat[:, q * HD:(q + 1) * HD] for q in range(4)]
        else:
            xts = sb("xts", [128, 2 * D])
            xh = [xts[:, q * HD:(q + 1) * HD] for q in range(4)]
        XT_KEYS = [("xt", q) for q in range(4)]
        AT_KEYS = [("aT", k) for k in range(PC)]

        def vcol(v, j):
            return vc[:, v * DC + j: v * DC + j + 1]

        def psO(X):
            return ps[:, X * OB:(X + 1) * OB]

        def plan(P, units, recording):
            op = P.op
            deferred = []

            MISC = [ps[:, 3072:3584], ps[:, 3584:4096]]
            misc_i = [0]

            def next_misc():
                misc_i[0] ^= 1
                return misc_i[0], MISC[misc_i[0]]

            pso_i = [0]

            def next_pso():
                pso_i[0] ^= 1
                return pso_i[0]

            ucount = [0]
            issued = [0]

            def load_unit(src, nk, ncols):
                i = ucount[0]
                ucount[0] += 1
                if recording:
                    units.append((src, nk, ncols))
                hi = i if recording else min(i + NWS - 1, len(units) - 1)
                while issued[0] <= hi:
                    n_ = issued[0]
                    issued[0] += 1
                    usrc, unk, uncols = units[n_]
                    s_ = n_ % NWS
                    uview = wr[s_][:, 0:unk * uncols].rearrange("p (k n) -> p k n", n=uncols)
                    op("pool", lambda e, uview=uview, usrc=usrc: e.dma_start(out=uview, in_=usrc),
                       writes=[("w", s_)], sem=f"w{s_}", inc=16)
                s = i % NWS
                view = wr[s][:, 0:nk * ncols].rearrange("p (k n) -> p k n", n=ncols)
                return view, ("w", s)

            def prefetch_first():
                if recording:
                    return
                while issued[0] < min(NWS, len(units)):
                    n_ = issued[0]
                    issued[0] += 1
                    usrc, unk, uncols = units[n_]
                    s_ = n_ % NWS
                    uview = wr[s_][:, 0:unk * uncols].rearrange("p (k n) -> p k n", n=uncols)
                    op("pool", lambda e, uview=uview, usrc=usrc: e.dma_start(out=uview, in_=usrc),
                       writes=[("w", s_)], after=[("xt", 0), ("xt", 1), ("xt", 2), ("xt", 3)], sem=f"w{s_}", inc=16)

            def flush_deferred():
                while deferred:
                    deferred.pop(0)()

            CL = [0]

            def tts():
                lo = CL[0]
                return [(c0, min(512, TOT - lo - c0)) for c0 in range(0, TOT - lo, 512)]

            def mm_terms(X, unit, ukey, col0, terms):
                TTl = tts()
                lo = CL[0]
                nmm = len(terms) * len(TTl)
                i = 0
                for idx, (ku, act, akey, ka) in enumerate(terms):
                    for (c0, n) in TTl:
                        i += 1
                        op("pe", lambda e, X=X, ku=ku, ka=ka, act=act, c0=c0, n=n, lo=lo, st=(idx == 0), sp_=(idx == len(terms) - 1):
                           e.matmul(out=ps[:, X * OB + c0: X * OB + c0 + n], lhsT=unit[:, ku, col0:col0 + 128],
                                    rhs=act[:, ka, lo + c0:lo + c0 + n], start=st, stop=sp_),
                           reads=[ukey] + [(ak_, ka) for ak_ in (akey if isinstance(akey, tuple) else (akey,))],
                           writes=[("ps", X)], signal=(i == nmm))
                flush_deferred()

            def mm_chunk(X, unit, ukey, col0, act, akey, ks):
                mm_terms(X, unit, ukey, col0, [(ku, act, akey, ka) for (ku, ka) in ks])

            def mm_pair(Xs, unit, ukey, act, akey, ks):
                TTl = tts()
                lo = CL[0]
                for idx, (ku, ka) in enumerate(ks):
                    for jj, X in enumerate(Xs):
                        for ti, (c0, n) in enumerate(TTl):
                            op("pe", lambda e, X=X, jj=jj, ku=ku, ka=ka, c0=c0, n=n, lo=lo, st=(idx == 0), sp_=(idx == len(ks) - 1):
                               e.matmul(out=ps[:, X * OB + c0: X * OB + c0 + n], lhsT=unit[:, ku, jj * 128:(jj + 1) * 128],
                                        rhs=act[:, ka, lo + c0:lo + c0 + n], start=st, stop=sp_),
                               reads=[ukey, (akey, ka)], writes=[("ps", X)],
                               signal=(idx == len(ks) - 1 and ti == len(TTl) - 1))
                flush_deferred()

            op("sp", lambda e: e.dma_start(out=vc[:, :], in_=vcd), writes=["vc"], sem="c0", inc=16)
            op("sp", lambda e: e.dma_start(out=ident[:, :], in_=identd), writes=["ident"], sem="c1", inc=16)
            op("sp", lambda e: e.dma_start(out=invc[:, :], in_=invd), writes=["invc"], sem="c2", inc=16)
            op("dve", lambda e: e.memset(ones[:, :], 1.0), writes=["ones"])
            op("dve", lambda e: e.memset(scr[:, :], 0.0), writes=L0_KEYS)

            P.alias(XT_KEYS, AT_KEYS)
            ntile = (TOT + 127) // 128
            evac_i = [0]
            HC = DC // 2
            GS = min(4, HC)

            def evac(fn_act, fn_dve, reads, writes):
                evac_i[0] ^= 1
                if evac_i[0]:
                    op("act", fn_act, reads=reads, writes=writes)
                else:
                    op("dve", fn_dve, reads=reads, writes=writes)

            junk = nT[:].rearrange("p a b -> p (a b)")[:, 0:HD]
            op("dve", lambda e: e.memset(sscol2[:, :], 1.0), writes=["sscol2"])
            for i in range(ntile):
                r0 = i * 128
                nr = min(128, TOT - r0)
                if i == (ntile * 6) // 10:
                    prefetch_first()
                for hf in range(2):
                    q = (2 * i + hf) % 4
                    op("sp", lambda e, q=q, r0=r0, nr=nr, hf=hf: e.dma_start(out=xh[q][0:nr, :],
                                                                             in_=xin[r0:r0 + nr, hf * HD:(hf + 1) * HD]),
                       writes=[("xt", q)], sem=f"x{q}", inc=16)
                    op("act", lambda e, q=q, i=i, hf=hf, nr=nr: e.activation(out=junk[0:nr, :], in_=xh[q][0:nr, :], func=AF.Square,
                                                                             accum_out=sscol2[0:nr, 2 * i + hf:2 * i + hf + 1]),
                       reads=[("xt", q)], writes=["sscol2", "junk"])
                    for jb in range(0, HC, GS):
                        mi, mb = next_misc()
                        for jj in range(GS):
                            jl = jb + jj
                            op("pe", lambda e, mb=mb, jj=jj, jl=jl, q=q, nr=nr:
                               e.transpose(out=mb[:, jj * 128: jj * 128 + nr], in_=xh[q][0:nr, jl * 128:(jl + 1) * 128],
                                           identity=ident[0:nr, 0:nr]),
                               reads=[("xt", q), "ident"], writes=[("psm", mi)], signal=(jj == GS - 1))
                        src = mb[:, 0:GS * 128].rearrange("p (a b) -> p a b", b=128)[:, :, 0:nr]
                        j0 = hf * HC + jb
                        dst = hT[:, j0:j0 + GS, r0:r0 + nr]
                        evac(lambda e, dst=dst, src=src: e.copy(out=dst, in_=src),
                             lambda e, dst=dst, src=src: e.tensor_copy(out=dst, in_=src),
                             reads=[("psm", mi)], writes=[("hT", j0 + q_) for q_ in range(GS)])
            P.alias(AT_KEYS, XT_KEYS)
            s2v = sscol2[:, 0:2 * ntile].rearrange("p (a b) -> p a b", b=2)
            op("dve", lambda e: e.tensor_tensor(out=sscol[:, 0:ntile], in0=s2v[:, :, 0], in1=s2v[:, :, 1], op=ALU.add),
               reads=["sscol2"], writes=["sscol"])

            def stats_chunk(j):
                b = tA[j % 2]
                bk = f"tA{j % 2}"
                lo = CL[0]
                op("act", lambda e, b=b, j=j, lo=lo: e.activation(out=b[:, lo:TOT], in_=hT[:, j, lo:TOT], func=AF.Square),
                   reads=[("hT", j)], writes=[bk])
                if j == 0:
                    op("dve", lambda e, b=b, lo=lo: e.tensor_copy(out=acc[:, lo:TOT], in_=b[:, lo:TOT]), reads=[bk], writes=["acc"])
                else:
                    op("dve", lambda e, b=b, lo=lo: e.tensor_tensor(out=acc[:, lo:TOT], in0=acc[:, lo:TOT], in1=b[:, lo:TOT], op=ALU.add),
                       reads=[bk, "acc"], writes=["acc"])

            NTT = (TOT + 127) // 128

            def col_sums(c_off, ntl, ncols_total):
                mi, mb = next_misc()
                for i in range(ntl):
                    r0 = i * 128
                    nr = min(128, ncols_total - r0)
                    op("pe", lambda e, mb=mb, i=i, r0=r0, nr=nr: e.matmul(out=mb[0:nr, 2 * i:2 * i + 2],
                                                                          lhsT=acc[:, c_off + r0:c_off + r0 + nr],
                                                                          rhs=ones[:, 0:2], start=True, stop=True),
                       reads=["acc", "ones"], writes=[("psm", mi)], signal=(i == ntl - 1))
                nfull = ncols_total // 128
                mbv = mb[:, 0:2 * ntl].rearrange("p (a b) -> p a b", b=2)
                if nfull > 0:
                    op("dve", lambda e, mbv=mbv: e.tensor_scalar(out=rcol[:, 0:nfull], in0=mbv[:, 0:nfull, 0], scalar1=1.0 / D,
                                                                 scalar2=EPS, op0=ALU.mult, op1=ALU.add),
                       reads=[("psm", mi)], writes=["rcol"])
                if nfull < ntl:
                    nrl = ncols_total - nfull * 128
                    op("dve", lambda e, mbv=mbv: e.tensor_scalar(out=rcol[0:nrl, nfull:ntl], in0=mbv[0:nrl, nfull:ntl, 0],
                                                                 scalar1=1.0 / D, scalar2=EPS, op0=ALU.mult, op1=ALU.add),
                       reads=[("psm", mi)], writes=["rcol"])

            def rcol_finish(ntl):
                op("act", lambda e: e.activation(out=rcol[:, 0:ntl], in_=rcol[:, 0:ntl], func=AF.Sqrt),
                   reads=["rcol"], writes=["rcol"])
                op("dve", lambda e: e.reciprocal(out=rcol[:, 0:ntl], in_=rcol[:, 0:ntl]), reads=["rcol"], writes=["rcol"])

            def rcol_from_sscol(ntl):
                op("dve", lambda e: e.tensor_scalar(out=rcol[:, 0:ntl], in0=sscol[:, 0:ntl], scalar1=1.0 / D, scalar2=EPS,
                                                    op0=ALU.mult, op1=ALU.add), reads=["sscol"], writes=["rcol"])
                rcol_finish(ntl)

            def rstd_rows():
                X = next_pso()
                nfull = TOT // 128
                R = tA[0]
                rkeys = [("Rw", i) for i in range(nfull)]
                P.alias(rkeys, ["tA0"])
                for i in range(nfull):
                    op("dve", lambda e, i=i: e.tensor_scalar(out=R[:, i * 128:(i + 1) * 128], in0=ident[:, :],
                                                             scalar1=rcol[:, i:i + 1], scalar2=None, op0=ALU.mult),
                       reads=["rcol", "ident"], writes=[("Rw", i)])
                    if i % 4 == 3 or i == nfull - 1:
                        c0 = (i // 4) * 512
                        c1 = (i + 1) * 128
                        op("pe", lambda e, X=X, c0=c0, c1=c1: e.matmul(out=ps[:, X * OB + c0: X * OB + c1], lhsT=ones[:, :],
                                                                       rhs=R[:, c0:c1], start=True, stop=True),
                           reads=[("Rw", q) for q in range((i // 4) * 4, i + 1)] + ["ones"], writes=[("ps", X)])
                if nfull * 128 < TOT:
                    r0 = nfull * 128
                    nr = TOT - r0
                    dg = diag[0]
                    op("dve", lambda e, dg=dg, nr=nr: e.tensor_scalar(out=dg[0:nr, 0:nr], in0=ident[0:nr, 0:nr],
                                                                      scalar1=rcol[0:nr, nfull:nfull + 1], scalar2=None, op0=ALU.mult),
                       reads=["rcol", "ident"], writes=[("diag", 0)])
                    op("pe", lambda e, X=X, dg=dg, r0=r0, nr=nr: e.matmul(out=ps[:, X * OB + r0: X * OB + r0 + nr],
                                                                          lhsT=ones[0:nr, :], rhs=dg[0:nr, 0:nr],
                                                                          start=True, stop=True),
                       reads=[("diag", 0), "ones"], writes=[("ps", X)])
                P.alias(["tA0"], rkeys)
                op("act", lambda e, X=X: e.copy(out=rstd[:, :], in_=psO(X)[:, 0:TOT]), reads=[("ps", X)], writes=["rstd"])

            def stats_finish():
                col_sums(0, NTT, TOT)
                rcol_finish(NTT)
                rstd_rows()

            def apply_norm(v):
                for j in range(DC):
                    op("dve", lambda e, j=j, lo=CL[0]: e.scalar_tensor_tensor(out=nT[:, j, lo:TOT], in0=hT[:, j, lo:TOT], scalar=vcol(v, j),
                                                                           in1=rstd[:, lo:TOT], op0=ALU.mult, op1=ALU.mult),
                       reads=[("hT", j), "rstd", "vc"], writes=[("nT", j)])

            def resid_matmul(unit_src_fn, nk, last, gscale=None):
                for ob in range(D // 512):
                    unit, ukey = load_unit(unit_src_fn(ob), nk, 512)
                    for jo in range(4):
                        j = ob * 4 + jo
                        X = next_pso()
                        mm_chunk(X, unit, ukey, jo * 128, aT, "aT", [(k, k) for k in range(nk)])
                        op("dve", lambda e, j=j, X=X, lo=CL[0]: e.tensor_tensor(out=hT[:, j, lo:TOT], in0=hT[:, j, lo:TOT],
                                                                                in1=psO(X)[:, 0:TOT - lo], op=ALU.add),
                           reads=[("ps", X), ("hT", j)], writes=[("hT", j)])
                        if last:
                            stats_chunk(j)
                            if gscale is not None:
                                op("act", lambda e, j=j, lo=CL[0]: e.activation(out=hT[:, j, lo:TOT], in_=hT[:, j, lo:TOT], func=AF.Copy,
                                                                             scale=vcol(gscale, j)),
                                   reads=[("hT", j), "vc"], writes=[("hT", j)])

            def ffn(layer, gscale=None):
                parts = []
                f0 = 0
                nparts = (FC + PC - 1) // PC
                base, rem = FC // nparts, FC % nparts
                for pi in range(nparts):
                    n = base + (1 if pi < rem else 0)
                    parts.append((f0, n))
                    f0 += n
                for pi, (f0, n) in enumerate(parts):
                    for q0 in range(0, n, 2):
                        nq = min(2, n - q0)
                        fa = f0 + q0
                        gsrc = w_gu[layer, :, fa * 128:(fa + nq) * 128].rearrange("(k p) n -> p k n", p=128)
                        usrc = w_gu[layer, :, F + fa * 128:F + (fa + nq) * 128].rearrange("(k p) n -> p k n", p=128)
                        gun, gk = load_unit(gsrc, DC, nq * 128)
                        gXs = [next_pso() for jj in range(nq)]
                        first = (pi == 0 and q0 == 0 and nq == 2)
                        if first:
                            mm_pair(gXs, gun, gk, nT, "nT", [(k, k) for k in range(DC)])
                        for jj in range(nq):
                            X = gXs[jj]
                            if not first:
                                mm_chunk(X, gun, gk, jj * 128, nT, "nT", [(k, k) for k in range(DC)])
                            op("act", lambda e, jj=jj, X=X, lo=CL[0]: e.activation(out=tA[jj][:, lo:TOT], in_=psO(X)[:, 0:TOT - lo],
                                                                                func=AF.Silu),
                               reads=[("ps", X)], writes=[f"tA{jj}"])
                        uun, uk = load_unit(usrc, DC, nq * 128)
                        for jj in range(nq):
                            X = next_pso()
                            mm_chunk(X, uun, uk, jj * 128, nT, "nT", [(k, k) for k in range(DC)])
                            fl = q0 + jj
                            op("dve", lambda e, jj=jj, X=X, fl=fl, lo=CL[0]: e.tensor_tensor(out=aT[:, fl, lo:TOT], in0=tA[jj][:, lo:TOT],
                                                                                            in1=psO(X)[:, 0:TOT - lo], op=ALU.mult),
                               reads=[("ps", X), f"tA{jj}"], writes=[("aT", fl)])
                    resid_matmul(lambda ob, f0=f0, n=n: w_dn[layer, f0 * 128:(f0 + n) * 128, ob * 512:(ob + 1) * 512]
                                 .rearrange("(k p) n -> p k n", p=128), n, last=(pi == len(parts) - 1),
                                 gscale=(gscale if pi == len(parts) - 1 else None))

            so_i = [0]

            def state_out(src_ap, keys, ncols, dsts):
                def go():
                    mi, mb = next_misc()
                    op("pe", lambda e, mb=mb: e.transpose(out=mb[0:ncols, 0:128], in_=src_ap, identity=ident[:, :]),
                       reads=list(keys) + ["ident"], writes=[("psm", mi)])
                    s = so_i[0] % 4
                    so_i[0] += 1
                    op("act", lambda e, s=s, mb=mb: e.copy(out=ostg[s][0:ncols, :], in_=mb[0:ncols, 0:128]),
                       reads=[("psm", mi)], writes=[("ostg", s)])
                    for (r0, nrw, dst) in dsts:
                        op("sp", lambda e, s=s, r0=r0, nrw=nrw, dst=dst: e.dma_start(out=dst, in_=ostg[s][r0:r0 + nrw, :]),
                           reads=[("ostg", s)], sem=f"so{s}", inc=16)
                deferred.append(go)

            ost_i = [0]

            def next_ost():
                ost_i[0] = (ost_i[0] + 1) % len(ost)
                return ost[ost_i[0]], ("ost", ost_i[0])

            P.alias([("nT", j) for j in range(DC)], ["junk"])
            prefetch_first()
            rcol_from_sscol(NTT)
            rstd_rows()
            apply_norm(0)

            nparts0 = (DC + PC - 1) // PC
            assert DC % nparts0 == 0
            pc0 = DC // nparts0
            full_k = [(k, k) for k in range(DC)]
            for pa in range(nparts0):
                for q0 in range(0, pc0, 2):
                    nq = min(2, pc0 - q0)
                    ja = pa * pc0 + q0
                    csrc = w_in[:, D + ja * 128: D + (ja + nq) * 128].rearrange("(k p) n -> p k n", p=128)
                    vsrc = w_in[:, 2 * D + ja * 128: 2 * D + (ja + nq) * 128].rearrange("(k p) n -> p k n", p=128)
                    bsrc = w_in[:, ja * 128:(ja + nq) * 128].rearrange("(k p) n -> p k n", p=128)
                    for jj in range(nq):
                        j = ja + jj
                        s = j % 2
                        op("sp", lambda e, s=s, j=j: e.dma_start(out=sc_t[s][0:NSEQ * CH, :], in_=sconv[:, j * 128:(j + 1) * 128]),
                           writes=[("sc_t", s)], sem=f"sc{s}", inc=16)
                        mi, mb = next_misc()
                        op("pe", lambda e, mb=mb, s=s: e.transpose(out=mb[:, 0:NSEQ * CH], in_=sc_t[s][0:NSEQ * CH, :],
                                                                   identity=ident[0:NSEQ * CH, 0:NSEQ * CH]),
                           reads=[("sc_t", s), "ident"], writes=[("psm", mi)])
                        op("act", lambda e, mb=mb, jj=jj: e.copy(out=ubS[jj][:, :, 0:CH],
                                                                 in_=mb[:, 0:NSEQ * CH].rearrange("p (s r) -> p s r", r=CH)),
                           reads=[("psm", mi)], writes=[f"ubS{jj}"])
                    cun, ck = load_unit(csrc, DC, nq * 128)
                    cXs = [next_pso() for jj in range(nq)]
                    if pa == 0 and q0 == 0 and nq == 2:
                        mm_pair(cXs, cun, ck, nT, "nT", full_k)
                    for jj in range(nq):
                        X = cXs[jj]
                        if not (pa == 0 and q0 == 0 and nq == 2):
                            mm_chunk(X, cun, ck, jj * 128, nT, "nT", full_k)
                        op("act", lambda e, jj=jj, X=X: e.copy(out=tA[jj][:, :], in_=psO(X)[:, 0:TOT]),
                           reads=[("ps", X)], writes=[f"tA{jj}"])
                    vun, vk = load_unit(vsrc, DC, nq * 128)
                    for jj in range(nq):
                        j = ja + jj
                        X = next_pso()
                        mm_chunk(X, vun, vk, jj * 128, nT, "nT", full_k)
                        op("dve", lambda e, jj=jj, X=X: e.tensor_tensor(out=ubP[jj][:, CH:CH + NP], in0=tA[jj][:, 0:NP],
                                                                        in1=psO(X)[:, 0:NP], op=ALU.mult),
                           reads=[("ps", X), f"tA{jj}"], writes=[f"ubP{jj}"])
                        op("dve", lambda e, jj=jj, X=X: e.tensor_tensor(
                            out=ubS[jj][:, :, CH:CH + T], in0=tA[jj][:, NP:TOT].rearrange("p (s t) -> p s t", t=T),
                            in1=psO(X)[:, NP:TOT].rearrange("p (s t) -> p s t", t=T), op=ALU.mult),
                           reads=[("ps", X), f"tA{jj}"], writes=[f"ubS{jj}"])
                        cvP = cvb[jj][:, 0:NP]
                        cvS = cvb[jj][:, NP:TOT].rearrange("p (s t) -> p s t", t=T)
                        op("act", lambda e, jj=jj, j=j, cvP=cvP: e.activation(out=cvP, in_=ubP[jj][:, 2:2 + NP], func=AF.Copy,
                                                                              scale=vcol(7, j)),
                           reads=[f"ubP{jj}", "vc"], writes=[f"cvP{jj}"])
                        op("act", lambda e, jj=jj, j=j, cvS=cvS: e.activation(out=cvS, in_=ubS[jj][:, :, 2:2 + T], func=AF.Copy,
                                                                              scale=vcol(7, j)),
                           reads=[f"ubS{jj}", "vc"], writes=[f"cvS{jj}"])
                        for tap in (1, 0):
                            op("dve", lambda e, jj=jj, j=j, tap=tap, cvP=cvP: e.scalar_tensor_tensor(
                                out=cvP, in0=ubP[jj][:, tap:tap + NP], scalar=vcol(5 + tap, j), in1=cvP,
                                op0=ALU.mult, op1=ALU.add),
                               reads=[f"ubP{jj}", f"cvP{jj}", "vc"], writes=[f"cvP{jj}"])
                            op("dve", lambda e, jj=jj, j=j, tap=tap, cvS=cvS: e.scalar_tensor_tensor(
                                out=cvS, in0=ubS[jj][:, :, tap:tap + T], scalar=vcol(5 + tap, j), in1=cvS,
                                op0=ALU.mult, op1=ALU.add),
                               reads=[f"ubS{jj}", f"cvS{jj}", "vc"], writes=[f"cvS{jj}"])
                        so, sok = next_ost()
                        op("act", lambda e, jj=jj, so=so: e.copy(out=so[:, 0:CH], in_=ubP[jj][:, NP:NP + CH]),
                           reads=[f"ubP{jj}"], writes=[sok])
                        op("act", lambda e, jj=jj, so=so: e.copy(
                            out=so[:, CH:CH + NSEQ * CH].rearrange("p (s r) -> p s r", r=CH), in_=ubS[jj][:, :, T:T + CH]),
                           reads=[f"ubS{jj}"], writes=[sok])
                        state_out(so[:, 0:CH + NSEQ * CH], [sok], CH + NSEQ * CH,
                                  [(0, CH, ocp[:, j * 128:(j + 1) * 128]),
                                   (CH, NSEQ * CH, ocs[:, j * 128:(j + 1) * 128])])
                    bun, bk = load_unit(bsrc, DC, nq * 128)
                    for jj in range(nq):
                        X = next_pso()
                        mm_chunk(X, bun, bk, jj * 128, nT, "nT", full_k)
                        gl = q0 + jj
                        op("dve", lambda e, jj=jj, X=X, gl=gl: e.tensor_tensor(out=aT[:, gl, :], in0=cvb[jj][:, :],
                                                                              in1=psO(X)[:, 0:TOT], op=ALU.mult),
                           reads=[("ps", X), f"cvP{jj}", f"cvS{jj}"], writes=[("aT", gl)])
                resid_matmul(lambda ob, pa=pa: w_out[pa * pc0 * 128:(pa + 1) * pc0 * 128, ob * 512:(ob + 1) * 512]
                             .rearrange("(k p) n -> p k n", p=128), pc0, last=(pa == nparts0 - 1))
            stats_finish()
            apply_norm(2)
            ffn(0)

            stats_finish()
            flush_deferred()
            P.alias(L1_KEYS, L0_KEYS)
            for j in range(DC):
                P.alias([("nTfix", j)], [("nT", j)])
            SW = PH + T
            for pbx, pbk in zip(pbs, ("pbh0", "pbh1")):
                op("dve", lambda e, pbx=pbx: e.memset(pbx[:, 0:PH], 0.0), writes=[pbk])
            def spool_dma(j):
                s = j % 2
                for hh in range(2):
                    op("sp", lambda e, s=s, hh=hh, j=j: e.dma_start(out=sp_t[s][0:SH, hh, :],
                                                                    in_=spool[hh * SH:(hh + 1) * SH, j * 128:(j + 1) * 128]),
                       writes=[("sp_t", s)], sem=f"spl{s}", inc=16)

            def pool_prep(j):
                s = j % 2
                pb, pbk, pbh = pbs[s], f"pb{s}", f"pbh{s}"
                pbS = pb[:, PH + NP: PBL].rearrange("p (s t) -> p s t", t=SW)
                if j == 0:
                    spool_dma(0)
                if j + 1 < DC:
                    spool_dma(j + 1)
                mi, mb = next_misc()
                for hh in range(2):
                    op("pe", lambda e, mb=mb, s=s, hh=hh: e.transpose(out=mb[:, hh * SH:(hh + 1) * SH], in_=sp_t[s][0:SH, hh, :],
                                                                     identity=ident[0:SH, 0:SH]),
                       reads=[("sp_t", s), "ident"], writes=[("psm", mi)], signal=(hh == 1))
                op("act", lambda e, mb=mb, pbS=pbS: e.copy(out=pbS[:, :, 0:PH],
                                                           in_=mb[:, 0:2 * SH].rearrange("p (s r) -> p s r", r=PH)),
                   reads=[("psm", mi)], writes=[pbh])
                op("dve", lambda e, j=j, pb=pb: e.scalar_tensor_tensor(out=pb[:, PH:PH + NP], in0=hT[:, j, 0:NP], scalar=vcol(1, j),
                                                                       in1=rstd[:, 0:NP], op0=ALU.mult, op1=ALU.mult),
                   reads=[("hT", j), "rstd", "vc"], writes=[pbk])
                op("dve", lambda e, j=j, pbS=pbS: e.scalar_tensor_tensor(
                    out=pbS[:, :, PH:SW], in0=hT[:, j, NP:TOT].rearrange("p (s t) -> p s t", t=T), scalar=vcol(1, j),
                    in1=rstd[:, NP:TOT].rearrange("p (s t) -> p s t", t=T), op0=ALU.mult, op1=ALU.mult),
                   reads=[("hT", j), "rstd", "vc"], writes=[pbk])
                sl = ((j // GC) % 2) * GC + (j % GC)
                op("act", lambda e, sl=sl, pb=pb: e.activation(out=aT[:, sl, 0:NP], in_=pb[:, PH:PH + NP], func=AF.Copy, scale=-1.0),
                   reads=[pbk], writes=[("aT", sl)])
                op("act", lambda e, sl=sl, pbS=pbS: e.activation(out=aT[:, sl, NP:TOT].rearrange("p (s t) -> p s t", t=T),
                                                                 in_=pbS[:, :, PH:SW], func=AF.Copy, scale=-1.0),
                   reads=[pbk], writes=[("aT", sl)])
                so, sok = next_ost()
                op("act", lambda e, so=so, pb=pb: e.copy(out=so[:, 0:PH], in_=pb[:, NP:NP + PH]), reads=[pbk], writes=[sok])
                state_out(so[:, 0:PH], [sok], PH, [(0, PH, opp[:, j * 128:(j + 1) * 128])])
                for hh in range(2):
                    so, sok = next_ost()
                    op("act", lambda e, so=so, hh=hh, pbS=pbS: e.copy(
                        out=so[:, 0:SH].rearrange("p (s r) -> p s r", r=PH),
                        in_=pbS[:, hh * (NSEQ // 2):(hh + 1) * (NSEQ // 2), T:SW]),
                       reads=[pbk, pbh], writes=[sok])
                    state_out(so[:, 0:SH], [sok], SH, [(0, SH, ops[hh * SH:(hh + 1) * SH, j * 128:(j + 1) * 128])])
                flush_deferred()

            last_fin = [1]

            def pool_compute(j):
                g = j // GC
                w = 2 ** (g + 1)
                s = j % 2
                pb, pbk, pbh = pbs[s], f"pb{s}", f"pbh{s}"
                pbS = pb[:, PH + NP: PBL].rearrange("p (s t) -> p s t", t=SW)
                cur, curk = pb, [pbk, pbh]
                bufs = [(t1, "t1"), (t2, "t2")]
                sh = 1
                bi = 1 - last_fin[0]
                while sh < w:
                    nb, nbk = bufs[bi % 2]
                    bi += 1
                    lo = 2 * sh - 1
                    op("dve", lambda e, nb=nb, cur=cur, lo=lo, sh=sh: e.tensor_tensor(
                        out=nb[:, lo:PBL], in0=cur[:, lo:PBL], in1=cur[:, lo - sh:PBL - sh], op=ALU.add),
                       reads=curk, writes=[nbk])
                    cur, curk = nb, [nbk]
                    sh *= 2
                last_fin[0] = (bi - 1) % 2
                curS = cur[:, PH + NP: PBL].rearrange("p (s t) -> p s t", t=SW)
                op("act", lambda e, j=j, cur=cur, w=w: e.activation(out=nT[:, j, IW:NP], in_=cur[:, PH + IW:PH + NP], func=AF.Copy,
                                                                    scale=1.0 / w),
                   reads=curk, writes=[("nT", j)])
                op("act", lambda e, j=j, curS=curS, w=w: e.activation(out=nT[:, j, NP:TOT].rearrange("p (s t) -> p s t", t=T),
                                                                      in_=curS[:, :, PH:SW], func=AF.Copy, scale=1.0 / w),
                   reads=curk, writes=[("nT", j)])
                op("dve", lambda e, j=j, cur=cur, g=g: e.tensor_tensor(out=nT[:, j, 0:IW], in0=cur[:, PH:PH + IW],
                                                                      in1=invc[:, g * IW:(g + 1) * IW], op=ALU.mult),
                   reads=curk + ["invc"], writes=[("nTfix", j)])

            mmq = []
            tailq = []

            def plan_mm(n):
                for _ in range(n):
                    if not mmq:
                        return
                    g_, jo, unit, ukey = mmq.pop(0)
                    jj_ = g_ * GC + jo
                    X = next_pso()
                    mm_terms(X, unit, ukey, jo * 128,
                             [(k, nT, ("nT", "nTfix"), g_ * GC + k) for k in range(GC)] +
                             [(k, aT, "aT", (g_ % 2) * GC + k) for k in range(GC)])

                    def tail_a(jj_=jj_, X=X):
                        op("dve", lambda e, j=jj_, X=X, lo=CL[0]: e.scalar_tensor_tensor(out=hT[:, j, lo:TOT], in0=psO(X)[:, 0:TOT - lo],
                                                                                      scalar=vcol(8, j), in1=hT[:, j, lo:TOT],
                                                                                      op0=ALU.mult, op1=ALU.add),
                           reads=[("ps", X), ("hT", jj_), "vc"], writes=[("hT", jj_)])
                    tailq.append((tail_a, jj_))

            def run_tails():
                js = []
                while tailq:
                    fn, jj_ = tailq.pop(0)
                    fn()
                    js.append(jj_)
                for jj_ in js:
                    stats_chunk(jj_)

            CL[0] = HALO
            pool_prep(0)
            for j in range(DC):
                if j + 1 < DC:
                    pool_prep(j + 1)
                pool_compute(j)
                run_tails()
                if j % GC == GC - 1:
                    g = j // GC
                    unit, ukey = load_unit(w_pool[g].rearrange("(k p) n -> p k n", p=128), GC, GC * 128)
                    for jo in range(GC):
                        mmq.append((g, jo, unit, ukey))
                plan_mm(2)
            while mmq or tailq:
                run_tails()
                plan_mm(2)
            stats_finish()
            apply_norm(3)
            ffn(1, gscale=4)

            P.alias(XT_KEYS, AT_KEYS)
            notile = (NOUT + 127) // 128
            col_sums(HALO, notile, NOUT)
            rcol_finish(notile)
            for i in range(notile):
                r0 = i * 128
                nr = min(128, NOUT - r0)
                c0 = HALO + r0
                for hf in range(2):
                    q = (2 * i + hf) % 4
                    for jb in range(0, HC, GS):
                        mi, mb = next_misc()
                        for jj in range(GS):
                            j = hf * HC + jb + jj
                            op("pe", lambda e, mb=mb, jj=jj, j=j, c0=c0, nr=nr:
                               e.transpose(out=mb[0:nr, jj * 128:(jj + 1) * 128], in_=hT[:, j, c0:c0 + nr], identity=ident[:, :]),
                               reads=[("hT", j), "ident"], writes=[("psm", mi)], signal=(jj == GS - 1))
                        dst = xh[q][0:nr, jb * 128:(jb + GS) * 128]
                        src = mb[0:nr, 0:GS * 128]
                        rc = rcol[0:nr, i:i + 1]
                        evac(lambda e, dst=dst, src=src, rc=rc: e.activation(out=dst, in_=src, func=AF.Copy, scale=rc),
                             lambda e, dst=dst, src=src, rc=rc: e.tensor_scalar(out=dst, in0=src, scalar1=rc, scalar2=None, op0=ALU.mult),
                             reads=[("psm", mi), "rcol"], writes=[("xt", q)])
                    op("sp", lambda e, q=q, r0=r0, nr=nr, hf=hf: e.dma_start(out=y[r0:r0 + nr, hf * HD:(hf + 1) * HD], in_=xh[q][0:nr, :]),
                       reads=[("xt", q)], sem=f"yo{q}", inc=16)
            flush_deferred()
            P.final_wait("sp", [n for n in P.semnames if n.startswith("yo") or n.startswith("so")])

        P1 = Prog()
        units = []
        plan(P1, units, True)
        P = Prog()
        plan(P, units, False)

        for name in P.semnames:
            sems[name] = es.enter_context(nc.semaphore(name))
        with nc.Block() as block:
            def emit(ename):
                def body(eng):
                    for waits, fn, incspec in P.streams[ename]:
                        for (sname, val) in waits:
                            eng.wait_ge(sems[sname], val)
                        if fn is None:
                            continue
                        ins = fn(eng)
                        if incspec is not None:
                            ins.then_inc(sems[incspec[0]], incspec[1])
                return body
            block.tensor(emit("pe"))
            block.scalar(emit("act"))
            block.vector(emit("dve"))
            block.gpsimd(emit("pool"))
            block.sync(emit("sp"))
    return nc


def _col_layout(v, DC):
    return np.ascontiguousarray(v.reshape(DC, 128).T)


def run(cfg, x_prompt, x_sample, state_conv, state_pool, meta_tokens, norm_mix, norm_ffn, norm_final,
        conv_w_in, conv_w_dw, conv_w_out, pool_w, pool_scale, ffn_w_gate_up, ffn_w_down, trace=False):
    D, F, NPO, NSEQ = cfg["D"], cfg["F"], cfg["NPO"], cfg["NSEQ"]
    DC = D // 128
    B = x_prompt.shape[0]
    SEQ = x_prompt.shape[1]
    f32 = np.float32
    nc = build_program(cfg)

    vecs = [norm_mix[0], norm_mix[1], norm_ffn[0], norm_ffn[1], norm_final,
            conv_w_dw[0, 0], conv_w_dw[0, 1], conv_w_dw[0, 2], pool_scale[0]]
    vcols = np.ascontiguousarray(np.concatenate([_col_layout(np.asarray(v, f32), DC) for v in vecs], axis=1))
    ident = np.eye(128, dtype=f32)
    shared = dict(
        vcols=vcols, ident=ident,
        w_in=np.ascontiguousarray(conv_w_in[0], f32), w_out=np.ascontiguousarray(conv_w_out[0], f32),
        w_pool=np.ascontiguousarray(pool_w[0], f32), w_gu=np.ascontiguousarray(ffn_w_gate_up, f32),
        w_dn=np.ascontiguousarray(ffn_w_down, f32),
    )
    wins = np.array([2.0, 4.0, 8.0, 16.0], f32)
    in_maps = []
    for c in range(N_CORES):
        b, half = c // 2, c % 2
        full = np.concatenate([np.asarray(meta_tokens, f32), np.asarray(x_prompt[b], f32)], axis=0)
        start = half * NPO
        lo = start - HALO
        if lo < 0:
            seg = np.concatenate([np.zeros((-lo, D), f32), full[0:start + NPO]], axis=0)
        else:
            seg = full[lo:start + NPO]
        xs = np.asarray(x_sample[c * NSEQ:(c + 1) * NSEQ], f32).reshape(NSEQ * T, D)
        xin = np.ascontiguousarray(np.concatenate([seg, xs], axis=0))
        pos = (lo + np.arange(IW)).astype(np.float64)
        cnt = np.minimum(wins[:, None].astype(np.float64), np.maximum(pos[None, :], 0.0) + 1.0)
        invc = np.broadcast_to((1.0 / cnt).astype(f32).reshape(1, 4 * IW), (128, 4 * IW))
        m = dict(shared)
        m.update(
            xin=xin,
            sconv=np.ascontiguousarray(np.asarray(state_conv[0, c * NSEQ:(c + 1) * NSEQ], f32).reshape(NSEQ * CH, D)),
            spool=np.ascontiguousarray(np.asarray(state_pool[0, c * NSEQ:(c + 1) * NSEQ], f32).reshape(NSEQ * PH, D)),
            invc=np.ascontiguousarray(invc),
        )
        in_maps.append(m)
    res = run_bass_kernel_spmd(nc, in_maps, core_ids=list(range(N_CORES)), trace=trace)
    R = res.results
    DECB = x_sample.shape[0]
    y_prompt = np.empty((B, SEQ, D), f32)
    y_sample = np.empty((DECB, T, D), f32)
    ncp = np.empty((1, B, CH, D), f32)
    npp = np.empty((1, B, PH, D), f32)
    ncs = np.empty((1, DECB, CH, D), f32)
    nps = np.empty((1, DECB, PH, D), f32)
    for c in range(N_CORES):
        b, half = c // 2, c % 2
        yc = np.asarray(R[c]["y"])
        if half == 0:
            y_prompt[b, 0:NPO - N_META] = yc[N_META:NPO]
        else:
            y_prompt[b, NPO - N_META:] = yc[0:NPO]
            ncp[0, b] = np.asarray(R[c]["ocp"])
            npp[0, b] = np.asarray(R[c]["opp"])
        y_sample[c * NSEQ:(c + 1) * NSEQ] = yc[NPO:].reshape(NSEQ, T, D)
        ncs[0, c * NSEQ:(c + 1) * NSEQ] = np.asarray(R[c]["ocs"]).reshape(NSEQ, CH, D)
        nps[0, c * NSEQ:(c + 1) * NSEQ] = np.asarray(R[c]["ops"]).reshape(NSEQ, PH, D)
    out = (y_prompt, y_sample, ncp, npp, ncs, nps)
    if trace:
        return out, res
    return out


def kernel(x_prompt, x_sample, state_conv, state_pool, meta_tokens, norm_mix, norm_ffn, norm_final,
           conv_w_in, conv_w_dw, conv_w_out, pool_w, pool_scale, ffn_w_gate_up, ffn_w_down):
    args = [np.asarray(a) for a in (x_prompt, x_sample, state_conv, state_pool, meta_tokens, norm_mix, norm_ffn,
                                    norm_final, conv_w_in, conv_w_dw, conv_w_out, pool_w, pool_scale,
                                    ffn_w_gate_up, ffn_w_down)]
    return run(FULL_CFG, *args)
```

```python
import numpy as np
import concourse.bass as bass
import concourse.mybir as mybir
from concourse.bass_utils import run_bass_kernel_spmd

F32 = mybir.dt.float32
BF16 = mybir.dt.bfloat16
ALU = mybir.AluOpType
AF = mybir.ActivationFunctionType

N_CORES = 8
N_META = 16
HALO = 18
T = 8
CH = 2
PH = 15
EPS = 1e-6
IW = 48
OB = 1536


def make_cfg(D, F, SEQ, DEC_BATCH, PC):
    cfg = dict(D=D, F=F, SEQ=SEQ, DEC_BATCH=DEC_BATCH, PC=PC)
    cfg["NPO"] = (N_META + SEQ) // 2
    cfg["NSEQ"] = DEC_BATCH // N_CORES
    return cfg


FULL_CFG = make_cfg(2048, 5632, 2048, 128, 8)


class Prog:
    ENG = ("pe", "act", "dve", "pool", "sp")

    def __init__(self):
        self.streams = {e: [] for e in self.ENG}
        self.cnt = {}
        self.last_w = {}
        self.readers = {}
        self.waited = {e: {} for e in self.ENG}
        self.pending = {e: [] for e in self.ENG}
        self.semnames = []

    def _sem(self, name):
        if name not in self.cnt:
            self.cnt[name] = 0
            self.semnames.append(name)
        return name

    def op(self, eng, fn, reads=(), writes=(), sem=None, inc=None, signal=True, after=()):
        own = eng if eng in ("pe", "act", "dve", "pool") else None
        if sem is None:
            sem = own
            inc = 1
        self._sem(sem)
        deps = {}

        def add(d, raw):
            if d is None:
                return
            s, v = d
            if s == own and eng == "pe":
                return
            if v > deps.get(s, 0):
                deps[s] = v

        for k in reads:
            add(self.last_w.get(k), True)
        for k in after:
            add(self.last_w.get(k), True)
        for k in writes:
            add(self.last_w.get(k), False)
            for r in self.readers.get(k, {}).items():
                add(r, False)
        waits = []
        wd = self.waited[eng]
        for s, v in deps.items():
            if wd.get(s, 0) < v:
                wd[s] = v
                waits.append((s, v))
        self.pending[eng].append((tuple(reads), tuple(writes)))
        incspec = None
        if signal:
            self.cnt[sem] += inc
            val = self.cnt[sem]
            incspec = (sem, inc)
            for rd, wr in self.pending[eng]:
                for k in rd:
                    d = self.readers.setdefault(k, {})
                    if d.get(sem, 0) < val:
                        d[sem] = val
                for k in wr:
                    self.last_w[k] = (sem, val)
                    self.readers[k] = {}
            self.pending[eng] = []
        self.streams[eng].append((waits, fn, incspec))

    def alias(self, new_keys, old_keys):
        for nk in new_keys:
            rd = self.readers.setdefault(nk, {})
            for ok in old_keys:
                for s, v in self.readers.get(ok, {}).items():
                    if rd.get(s, 0) < v:
                        rd[s] = v
                lw = self.last_w.get(ok)
                if lw is not None and rd.get(lw[0], 0) < lw[1]:
                    rd[lw[0]] = lw[1]

    def final_wait(self, eng, sems):
        waits = [(s, self.cnt[s]) for s in sems if self.cnt.get(s, 0) > 0]
        self.streams[eng].append((waits, None, None))


def build_program(cfg):
    D, F, NPO, NSEQ, PC = cfg["D"], cfg["F"], cfg["NPO"], cfg["NSEQ"], cfg["PC"]
    DC, FC = D // 128, F // 128
    GC = DC // 4
    NP = HALO + NPO
    NS = NSEQ * T
    TOT = NP + NS
    NOUT = NPO + NS
    assert TOT % 2 == 0 and TOT <= OB
    TTs = [(c0, min(512, TOT - c0)) for c0 in range(0, TOT, 512)]
    UWP = CH + NP
    PBL = PH + NP + NSEQ * (PH + T)
    SH = (NSEQ // 2) * PH
    assert SH <= 128 and SH % 2 == 0 and NSEQ * CH <= 128
    NV = 9
    WSLOT = max(DC * 256, PC * 512, GC * 512)
    NWS = 3

    nc = bass.Bass("TRN2", target_bir_lowering=False)

    def din(name, shape):
        return nc.dram_tensor(name, list(shape), F32, kind="ExternalInput").ap()

    def dout(name, shape):
        return nc.dram_tensor(name, list(shape), F32, kind="ExternalOutput").ap()

    xin = din("xin", [TOT, D])
    sconv = din("sconv", [NSEQ * CH, D])
    spool = din("spool", [NSEQ * PH, D])
    vcd = din("vcols", [128, NV * DC])
    invd = din("invc", [128, 4 * IW])
    identd = din("ident", [128, 128])
    w_in = din("w_in", [D, 3 * D])
    w_out = din("w_out", [D, D])
    w_pool = din("w_pool", [4, D // 4, D // 4])
    w_gu = din("w_gu", [2, D, 2 * F])
    w_dn = din("w_dn", [2, F, D])
    y = dout("y", [NOUT, D])
    ocp = dout("ocp", [CH, D])
    opp = dout("opp", [PH, D])
    ocs = dout("ocs", [NSEQ * CH, D])
    ops = dout("ops", [NSEQ * PH, D])

    from contextlib import ExitStack
    with ExitStack() as es:
        def sb(name, shape, dt=F32):
            return es.enter_context(nc.sbuf_tensor(name, list(shape), dt))

        hT = sb("hT", [128, DC, TOT])
        nT = sb("nT", [128, DC, TOT], BF16)
        aT = sb("aT", [128, PC, TOT], BF16)
        wr = [sb(f"wr{s}", [128, WSLOT], BF16) for s in range(NWS)]
        acc = sb("acc", [128, TOT])
        rstd = sb("rstd", [128, TOT])
        tA = [sb(f"tA{i}", [128, TOT]) for i in range(2)]
        SCR = max(2 * UWP + 2 * NSEQ * (CH + T) + 2 * TOT, 4 * PBL)
        scr = sb("scr", [128, SCR])
        vc = sb("vc", [128, NV * DC])
        invc = sb("invc_sb", [128, 4 * IW])
        ident = sb("ident_sb", [128, 128])
        ones = sb("ones", [128, 128])
        rcols = sb("rcols", [128, 16])
        sscol = sb("sscol", [128, 16])
        sscol2 = sb("sscol2", [128, 32])
        rcol = sb("rcol", [128, 16])
        diag = [sb(f"diag{i}", [128, 128]) for i in range(2)]
        sc_t = [sb(f"sc_t{i}", [128, 128]) for i in range(2)]
        sp_t = [sb(f"sp_t{i}", [128, 2, 128]) for i in range(2)]
        ost = [sb(f"ost{i}", [128, 128]) for i in range(4)]
        ostg = [sb(f"ostg{i}", [128, 128]) for i in range(4)]
        ps = es.enter_context(nc.psum_tensor("ps", [128, 8 * 512], F32))
        sems = {}
        block = None

        o = 0
        ubP = []
        for i in range(2):
            ubP.append(scr[:, o:o + UWP]); o += UWP
        ubS = []
        for i in range(2):
            ubS.append(scr[:, o:o + NSEQ * (CH + T)].rearrange("p (s t) -> p s t", t=CH + T)); o += NSEQ * (CH + T)
        cvb = []
        for i in range(2):
            cvb.append(scr[:, o:o + TOT]); o += TOT
        L0_KEYS = ["ubP0", "ubP1", "ubS0", "ubS1", "cvP0", "cvP1", "cvS0", "cvS1"]
        pbs = [scr[:, 0:PBL], scr[:, PBL:2 * PBL]]
        t1 = scr[:, 2 * PBL:3 * PBL]
        t2 = scr[:, 3 * PBL:4 * PBL]
        L1_KEYS = ["pb0", "pb1", "pbh0", "pbh1", "t1", "t2"]
        HD = D // 2
        if PC * TOT // 2 >= 2 * D:
            aT32 = aT[:].bitcast(F32)
            aTflat = aT32.rearrange("p a b -> p (a b)")
            xh = [aTflat[:, q * HD:(q + 1) * HD] for q in range(4)]
        else:
            xts = sb("xts", [128, 2 * D])
            xh = [xts[:, q * HD:(q + 1) * HD] for q in range(4)]
        XT_KEYS = [("xt", q) for q in range(4)]
        AT_KEYS = [("aT", k) for k in range(PC)]

        def vcol(v, j):
            return vc[:, v * DC + j: v * DC + j + 1]

        def psO(X):
            return ps[:, X * OB:(X + 1) * OB]

        def plan(P, units, recording):
            op = P.op
            deferred = []

            MISC = [ps[:, 3072:3584], ps[:, 3584:4096]]
            misc_i = [0]

            def next_misc():
                misc_i[0] ^= 1
                return misc_i[0], MISC[misc_i[0]]

            pso_i = [0]

            def next_pso():
                pso_i[0] ^= 1
                return pso_i[0]

            ucount = [0]
            issued = [0]

            def load_unit(src, nk, ncols):
                i = ucount[0]
                ucount[0] += 1
                if recording:
                    units.append((src, nk, ncols))
                hi = i if recording else min(i + NWS - 1, len(units) - 1)
                while issued[0] <= hi:
                    n_ = issued[0]
                    issued[0] += 1
                    usrc, unk, uncols = units[n_]
                    s_ = n_ % NWS
                    uview = wr[s_][:, 0:unk * uncols].rearrange("p (k n) -> p k n", n=uncols)
                    op("pool", lambda e, uview=uview, usrc=usrc: e.dma_start(out=uview, in_=usrc),
                       writes=[("w", s_)], sem=f"w{s_}", inc=16)
                s = i % NWS
                view = wr[s][:, 0:nk * ncols].rearrange("p (k n) -> p k n", n=ncols)
                return view, ("w", s)

            def prefetch_first():
                if recording:
                    return
                while issued[0] < min(NWS, len(units)):
                    n_ = issued[0]
                    issued[0] += 1
                    usrc, unk, uncols = units[n_]
                    s_ = n_ % NWS
                    uview = wr[s_][:, 0:unk * uncols].rearrange("p (k n) -> p k n", n=uncols)
                    op("pool", lambda e, uview=uview, usrc=usrc: e.dma_start(out=uview, in_=usrc),
                       writes=[("w", s_)], after=[("xt", 0), ("xt", 1), ("xt", 2), ("xt", 3)], sem=f"w{s_}", inc=16)

            def flush_deferred():
                while deferred:
                    deferred.pop(0)()

            CL = [0]

            def tts():
                lo = CL[0]
                return [(c0, min(512, TOT - lo - c0)) for c0 in range(0, TOT - lo, 512)]

            def mm_terms(X, unit, ukey, col0, terms):
                TTl = tts()
                lo = CL[0]
                nmm = len(terms) * len(TTl)
                i = 0
                for idx, (ku, act, akey, ka) in enumerate(terms):
                    for (c0, n) in TTl:
                        i += 1
                        op("pe", lambda e, X=X, ku=ku, ka=ka, act=act, c0=c0, n=n, lo=lo, st=(idx == 0), sp_=(idx == len(terms) - 1):
                           e.matmul(out=ps[:, X * OB + c0: X * OB + c0 + n], lhsT=unit[:, ku, col0:col0 + 128],
                                    rhs=act[:, ka, lo + c0:lo + c0 + n], start=st, stop=sp_),
                           reads=[ukey] + [(ak_, ka) for ak_ in (akey if isinstance(akey, tuple) else (akey,))],
                           writes=[("ps", X)], signal=(i == nmm))
                flush_deferred()

            def mm_chunk(X, unit, ukey, col0, act, akey, ks):
                mm_terms(X, unit, ukey, col0, [(ku, act, akey, ka) for (ku, ka) in ks])

            def mm_pair(Xs, unit, ukey, act, akey, ks):
                TTl = tts()
                lo = CL[0]
                for idx, (ku, ka) in enumerate(ks):
                    for jj, X in enumerate(Xs):
                        for ti, (c0, n) in enumerate(TTl):
                            op("pe", lambda e, X=X, jj=jj, ku=ku, ka=ka, c0=c0, n=n, lo=lo, st=(idx == 0), sp_=(idx == len(ks) - 1):
                               e.matmul(out=ps[:, X * OB + c0: X * OB + c0 + n], lhsT=unit[:, ku, jj * 128:(jj + 1) * 128],
                                        rhs=act[:, ka, lo + c0:lo + c0 + n], start=st, stop=sp_),
                               reads=[ukey, (akey, ka)], writes=[("ps", X)],
                               signal=(idx == len(ks) - 1 and ti == len(TTl) - 1))
                flush_deferred()

            op("sp", lambda e: e.dma_start(out=vc[:, :], in_=vcd), writes=["vc"], sem="c0", inc=16)
            op("sp", lambda e: e.dma_start(out=ident[:, :], in_=identd), writes=["ident"], sem="c1", inc=16)
            op("sp", lambda e: e.dma_start(out=invc[:, :], in_=invd), writes=["invc"], sem="c2", inc=16)
            op("dve", lambda e: e.memset(ones[:, :], 1.0), writes=["ones"])
            op("dve", lambda e: e.memset(scr[:, :], 0.0), writes=L0_KEYS)

            P.alias(XT_KEYS, AT_KEYS)
            ntile = (TOT + 127) // 128
            evac_i = [0]
            HC = DC // 2
            GS = min(4, HC)

            def evac(fn_act, fn_dve, reads, writes):
                evac_i[0] ^= 1
                if evac_i[0]:
                    op("act", fn_act, reads=reads, writes=writes)
                else:
                    op("dve", fn_dve, reads=reads, writes=writes)

            junk = nT[:].rearrange("p a b -> p (a b)")[:, 0:HD]
            op("dve", lambda e: e.memset(sscol2[:, :], 1.0), writes=["sscol2"])
            for i in range(ntile):
                r0 = i * 128
                nr = min(128, TOT - r0)
                if i == (ntile * 6) // 10:
                    prefetch_first()
                for hf in range(2):
                    q = (2 * i + hf) % 4
                    op("sp", lambda e, q=q, r0=r0, nr=nr, hf=hf: e.dma_start(out=xh[q][0:nr, :],
                                                                             in_=xin[r0:r0 + nr, hf * HD:(hf + 1) * HD]),
                       writes=[("xt", q)], sem=f"x{q}", inc=16)
                    op("act", lambda e, q=q, i=i, hf=hf, nr=nr: e.activation(out=junk[0:nr, :], in_=xh[q][0:nr, :], func=AF.Square,
                                                                             accum_out=sscol2[0:nr, 2 * i + hf:2 * i + hf + 1]),
                       reads=[("xt", q)], writes=["sscol2", "junk"])
                    for jb in range(0, HC, GS):
                        mi, mb = next_misc()
                        for jj in range(GS):
                            jl = jb + jj
                            op("pe", lambda e, mb=mb, jj=jj, jl=jl, q=q, nr=nr:
                               e.transpose(out=mb[:, jj * 128: jj * 128 + nr], in_=xh[q][0:nr, jl * 128:(jl + 1) * 128],
                                           identity=ident[0:nr, 0:nr]),
                               reads=[("xt", q), "ident"], writes=[("psm", mi)], signal=(jj == GS - 1))
                        src = mb[:, 0:GS * 128].rearrange("p (a b) -> p a b", b=128)[:, :, 0:nr]
                        j0 = hf * HC + jb
                        dst = hT[:, j0:j0 + GS, r0:r0 + nr]
                        evac(lambda e, dst=dst, src=src: e.copy(out=dst, in_=src),
                             lambda e, dst=dst, src=src: e.tensor_copy(out=dst, in_=src),
                             reads=[("psm", mi)], writes=[("hT", j0 + q_) for q_ in range(GS)])
            P.alias(AT_KEYS, XT_KEYS)
            s2v = sscol2[:, 0:2 * ntile].rearrange("p (a b) -> p a b", b=2)
            op("dve", lambda e: e.tensor_tensor(out=sscol[:, 0:ntile], in0=s2v[:, :, 0], in1=s2v[:, :, 1], op=ALU.add),
               reads=["sscol2"], writes=["sscol"])

            def stats_chunk(j):
                b = tA[j % 2]
                bk = f"tA{j % 2}"
                lo = CL[0]
                op("act", lambda e, b=b, j=j, lo=lo: e.activation(out=b[:, lo:TOT], in_=hT[:, j, lo:TOT], func=AF.Square),
                   reads=[("hT", j)], writes=[bk])
                if j == 0:
                    op("dve", lambda e, b=b, lo=lo: e.tensor_copy(out=acc[:, lo:TOT], in_=b[:, lo:TOT]), reads=[bk], writes=["acc"])
                else:
                    op("dve", lambda e, b=b, lo=lo: e.tensor_tensor(out=acc[:, lo:TOT], in0=acc[:, lo:TOT], in1=b[:, lo:TOT], op=ALU.add),
                       reads=[bk, "acc"], writes=["acc"])

            NTT = (TOT + 127) // 128

            def col_sums(c_off, ntl, ncols_total):
                mi, mb = next_misc()
                for i in range(ntl):
                    r0 = i * 128
                    nr = min(128, ncols_total - r0)
                    op("pe", lambda e, mb=mb, i=i, r0=r0, nr=nr: e.matmul(out=mb[0:nr, 2 * i:2 * i + 2],
                                                                          lhsT=acc[:, c_off + r0:c_off + r0 + nr],
                                                                          rhs=ones[:, 0:2], start=True, stop=True),
                       reads=["acc", "ones"], writes=[("psm", mi)], signal=(i == ntl - 1))
                nfull = ncols_total // 128
                mbv = mb[:, 0:2 * ntl].rearrange("p (a b) -> p a b", b=2)
                if nfull > 0:
                    op("dve", lambda e, mbv=mbv: e.tensor_scalar(out=rcol[:, 0:nfull], in0=mbv[:, 0:nfull, 0], scalar1=1.0 / D,
                                                                 scalar2=EPS, op0=ALU.mult, op1=ALU.add),
                       reads=[("psm", mi)], writes=["rcol"])
                if nfull < ntl:
                    nrl = ncols_total - nfull * 128
                    op("dve", lambda e, mbv=mbv: e.tensor_scalar(out=rcol[0:nrl, nfull:ntl], in0=mbv[0:nrl, nfull:ntl, 0],
                                                                 scalar1=1.0 / D, scalar2=EPS, op0=ALU.mult, op1=ALU.add),
                       reads=[("psm", mi)], writes=["rcol"])

            def rcol_finish(ntl):
                op("act", lambda e: e.activation(out=rcol[:, 0:ntl], in_=rcol[:, 0:ntl], func=AF.Sqrt),
                   reads=["rcol"], writes=["rcol"])
                op("dve", lambda e: e.reciprocal(out=rcol[:, 0:ntl], in_=rcol[:, 0:ntl]), reads=["rcol"], writes=["rcol"])

            def rcol_from_sscol(ntl):
                op("dve", lambda e: e.tensor_scalar(out=rcol[:, 0:ntl], in0=sscol[:, 0:ntl], scalar1=1.0 / D, scalar2=EPS,
                                                    op0=ALU.mult, op1=ALU.add), reads=["sscol"], writes=["rcol"])
                rcol_finish(ntl)

            def rstd_rows():
                X = next_pso()
                nfull = TOT // 128
                R = tA[0]
                rkeys = [("Rw", i) for i in range(nfull)]
                P.alias(rkeys, ["tA0"])
                for i in range(nfull):
                    op("dve", lambda e, i=i: e.tensor_scalar(out=R[:, i * 128:(i + 1) * 128], in0=ident[:, :],
                                                             scalar1=rcol[:, i:i + 1], scalar2=None, op0=ALU.mult),
                       reads=["rcol", "ident"], writes=[("Rw", i)])
                    if i % 4 == 3 or i == nfull - 1:
                        c0 = (i // 4) * 512
                        c1 = (i + 1) * 128
                        op("pe", lambda e, X=X, c0=c0, c1=c1: e.matmul(out=ps[:, X * OB + c0: X * OB + c1], lhsT=ones[:, :],
                                                                       rhs=R[:, c0:c1], start=True, stop=True),
                           reads=[("Rw", q) for q in range((i // 4) * 4, i + 1)] + ["ones"], writes=[("ps", X)])
                if nfull * 128 < TOT:
                    r0 = nfull * 128
                    nr = TOT - r0
                    dg = diag[0]
                    op("dve", lambda e, dg=dg, nr=nr: e.tensor_scalar(out=dg[0:nr, 0:nr], in0=ident[0:nr, 0:nr],
                                                                      scalar1=rcol[0:nr, nfull:nfull + 1], scalar2=None, op0=ALU.mult),
                       reads=["rcol", "ident"], writes=[("diag", 0)])
                    op("pe", lambda e, X=X, dg=dg, r0=r0, nr=nr: e.matmul(out=ps[:, X * OB + r0: X * OB + r0 + nr],
                                                                          lhsT=ones[0:nr, :], rhs=dg[0:nr, 0:nr],
                                                                          start=True, stop=True),
                       reads=[("diag", 0), "ones"], writes=[("ps", X)])
                P.alias(["tA0"], rkeys)
                op("act", lambda e, X=X: e.copy(out=rstd[:, :], in_=psO(X)[:, 0:TOT]), reads=[("ps", X)], writes=["rstd"])

            def pe_keepwarm(n):
                mi, mb = next_misc()
                for q in range(n):
                    op("pe", lambda e, mb=mb: e.matmul(out=mb[:, 0:min(512, TOT)], lhsT=nT[:, 0, 0:128], rhs=nT[:, 0, 0:min(512, TOT)],
                                                       start=True, stop=True),
                       reads=[("nT", 0)], writes=[("psm", mi)], signal=(q == n - 1))

            def stats_finish():
                pe_keepwarm(18)
                col_sums(0, NTT, TOT)
                rcol_finish(NTT)
                rstd_rows()

            def apply_norm(v):
                for j in range(DC):
                    op("dve", lambda e, j=j, lo=CL[0]: e.scalar_tensor_tensor(out=nT[:, j, lo:TOT], in0=hT[:, j, lo:TOT], scalar=vcol(v, j),
                                                                           in1=rstd[:, lo:TOT], op0=ALU.mult, op1=ALU.mult),
                       reads=[("hT", j), "rstd", "vc"], writes=[("nT", j)])

            def resid_matmul(unit_src_fn, nk, last, gscale=None):
                for ob in range(D // 512):
                    unit, ukey = load_unit(unit_src_fn(ob), nk, 512)
                    for jo in range(4):
                        j = ob * 4 + jo
                        X = next_pso()
                        mm_chunk(X, unit, ukey, jo * 128, aT, "aT", [(k, k) for k in range(nk)])
                        op("dve", lambda e, j=j, X=X, lo=CL[0]: e.tensor_tensor(out=hT[:, j, lo:TOT], in0=hT[:, j, lo:TOT],
                                                                                in1=psO(X)[:, 0:TOT - lo], op=ALU.add),
                           reads=[("ps", X), ("hT", j)], writes=[("hT", j)])
                        if last:
                            stats_chunk(j)
                            if gscale is not None:
                                op("act", lambda e, j=j, lo=CL[0]: e.activation(out=hT[:, j, lo:TOT], in_=hT[:, j, lo:TOT], func=AF.Copy,
                                                                             scale=vcol(gscale, j)),
                                   reads=[("hT", j), "vc"], writes=[("hT", j)])

            def ffn(layer, gscale=None):
                parts = []
                f0 = 0
                nparts = (FC + PC - 1) // PC
                base, rem = FC // nparts, FC % nparts
                for pi in range(nparts):
                    n = base + (1 if pi < rem else 0)
                    parts.append((f0, n))
                    f0 += n
                for pi, (f0, n) in enumerate(parts):
                    for q0 in range(0, n, 2):
                        nq = min(2, n - q0)
                        fa = f0 + q0
                        gsrc = w_gu[layer, :, fa * 128:(fa + nq) * 128].rearrange("(k p) n -> p k n", p=128)
                        usrc = w_gu[layer, :, F + fa * 128:F + (fa + nq) * 128].rearrange("(k p) n -> p k n", p=128)
                        gun, gk = load_unit(gsrc, DC, nq * 128)
                        gXs = [next_pso() for jj in range(nq)]
                        first = (pi == 0 and q0 == 0 and nq == 2)
                        if first:
                            mm_pair(gXs, gun, gk, nT, "nT", [(k, k) for k in range(DC)])
                        for jj in range(nq):
                            X = gXs[jj]
                            if not first:
                                mm_chunk(X, gun, gk, jj * 128, nT, "nT", [(k, k) for k in range(DC)])
                            op("act", lambda e, jj=jj, X=X, lo=CL[0]: e.activation(out=tA[jj][:, lo:TOT], in_=psO(X)[:, 0:TOT - lo],
                                                                                func=AF.Silu),
                               reads=[("ps", X)], writes=[f"tA{jj}"])
                        uun, uk = load_unit(usrc, DC, nq * 128)
                        for jj in range(nq):
                            X = next_pso()
                            mm_chunk(X, uun, uk, jj * 128, nT, "nT", [(k, k) for k in range(DC)])
                            fl = q0 + jj
                            op("dve", lambda e, jj=jj, X=X, fl=fl, lo=CL[0]: e.tensor_tensor(out=aT[:, fl, lo:TOT], in0=tA[jj][:, lo:TOT],
                                                                                            in1=psO(X)[:, 0:TOT - lo], op=ALU.mult),
                               reads=[("ps", X), f"tA{jj}"], writes=[("aT", fl)])
                    resid_matmul(lambda ob, f0=f0, n=n: w_dn[layer, f0 * 128:(f0 + n) * 128, ob * 512:(ob + 1) * 512]
                                 .rearrange("(k p) n -> p k n", p=128), n, last=(pi == len(parts) - 1),
                                 gscale=(gscale if pi == len(parts) - 1 else None))

            so_i = [0]

            def state_out(src_ap, keys, ncols, dsts):
                def go():
                    mi, mb = next_misc()
                    op("pe", lambda e, mb=mb: e.transpose(out=mb[0:ncols, 0:128], in_=src_ap, identity=ident[:, :]),
                       reads=list(keys) + ["ident"], writes=[("psm", mi)])
                    s = so_i[0] % 4
                    so_i[0] += 1
                    op("act", lambda e, s=s, mb=mb: e.copy(out=ostg[s][0:ncols, :], in_=mb[0:ncols, 0:128]),
                       reads=[("psm", mi)], writes=[("ostg", s)])
                    for (r0, nrw, dst) in dsts:
                        op("sp", lambda e, s=s, r0=r0, nrw=nrw, dst=dst: e.dma_start(out=dst, in_=ostg[s][r0:r0 + nrw, :]),
                           reads=[("ostg", s)], sem=f"so{s}", inc=16)
                deferred.append(go)

            ost_i = [0]

            def next_ost():
                ost_i[0] = (ost_i[0] + 1) % len(ost)
                return ost[ost_i[0]], ("ost", ost_i[0])

            P.alias([("nT", j) for j in range(DC)], ["junk"])
            prefetch_first()
            rcol_from_sscol(NTT)
            rstd_rows()
            apply_norm(0)

            nparts0 = (DC + PC - 1) // PC
            assert DC % nparts0 == 0
            pc0 = DC // nparts0
            full_k = [(k, k) for k in range(DC)]
            for pa in range(nparts0):
                for q0 in range(0, pc0, 2):
                    nq = min(2, pc0 - q0)
                    ja = pa * pc0 + q0
                    csrc = w_in[:, D + ja * 128: D + (ja + nq) * 128].rearrange("(k p) n -> p k n", p=128)
                    vsrc = w_in[:, 2 * D + ja * 128: 2 * D + (ja + nq) * 128].rearrange("(k p) n -> p k n", p=128)
                    bsrc = w_in[:, ja * 128:(ja + nq) * 128].rearrange("(k p) n -> p k n", p=128)
                    for jj in range(nq):
                        j = ja + jj
                        s = j % 2
                        op("sp", lambda e, s=s, j=j: e.dma_start(out=sc_t[s][0:NSEQ * CH, :], in_=sconv[:, j * 128:(j + 1) * 128]),
                           writes=[("sc_t", s)], sem=f"sc{s}", inc=16)
                        mi, mb = next_misc()
                        op("pe", lambda e, mb=mb, s=s: e.transpose(out=mb[:, 0:NSEQ * CH], in_=sc_t[s][0:NSEQ * CH, :],
                                                                   identity=ident[0:NSEQ * CH, 0:NSEQ * CH]),
                           reads=[("sc_t", s), "ident"], writes=[("psm", mi)])
                        op("act", lambda e, mb=mb, jj=jj: e.copy(out=ubS[jj][:, :, 0:CH],
                                                                 in_=mb[:, 0:NSEQ * CH].rearrange("p (s r) -> p s r", r=CH)),
                           reads=[("psm", mi)], writes=[f"ubS{jj}"])
                    cun, ck = load_unit(csrc, DC, nq * 128)
                    cXs = [next_pso() for jj in range(nq)]
                    if pa == 0 and q0 == 0 and nq == 2:
                        mm_pair(cXs, cun, ck, nT, "nT", full_k)
                    for jj in range(nq):
                        X = cXs[jj]
                        if not (pa == 0 and q0 == 0 and nq == 2):
                            mm_chunk(X, cun, ck, jj * 128, nT, "nT", full_k)
                        op("act", lambda e, jj=jj, X=X: e.copy(out=tA[jj][:, :], in_=psO(X)[:, 0:TOT]),
                           reads=[("ps", X)], writes=[f"tA{jj}"])
                    vun, vk = load_unit(vsrc, DC, nq * 128)
                    for jj in range(nq):
                        j = ja + jj
                        X = next_pso()
                        mm_chunk(X, vun, vk, jj * 128, nT, "nT", full_k)
                        op("dve", lambda e, jj=jj, X=X: e.tensor_tensor(out=ubP[jj][:, CH:CH + NP], in0=tA[jj][:, 0:NP],
                                                                        in1=psO(X)[:, 0:NP], op=ALU.mult),
                           reads=[("ps", X), f"tA{jj}"], writes=[f"ubP{jj}"])
                        op("dve", lambda e, jj=jj, X=X: e.tensor_tensor(
                            out=ubS[jj][:, :, CH:CH + T], in0=tA[jj][:, NP:TOT].rearrange("p (s t) -> p s t", t=T),
                            in1=psO(X)[:, NP:TOT].rearrange("p (s t) -> p s t", t=T), op=ALU.mult),
                           reads=[("ps", X), f"tA{jj}"], writes=[f"ubS{jj}"])
                        cvP = cvb[jj][:, 0:NP]
                        cvS = cvb[jj][:, NP:TOT].rearrange("p (s t) -> p s t", t=T)
                        op("act", lambda e, jj=jj, j=j, cvP=cvP: e.activation(out=cvP, in_=ubP[jj][:, 2:2 + NP], func=AF.Copy,
                                                                              scale=vcol(7, j)),
                           reads=[f"ubP{jj}", "vc"], writes=[f"cvP{jj}"])
                        op("act", lambda e, jj=jj, j=j, cvS=cvS: e.activation(out=cvS, in_=ubS[jj][:, :, 2:2 + T], func=AF.Copy,
                                                                              scale=vcol(7, j)),
                           reads=[f"ubS{jj}", "vc"], writes=[f"cvS{jj}"])
                        for tap in (1, 0):
                            op("dve", lambda e, jj=jj, j=j, tap=tap, cvP=cvP: e.scalar_tensor_tensor(
                                out=cvP, in0=ubP[jj][:, tap:tap + NP], scalar=vcol(5 + tap, j), in1=cvP,
                                op0=ALU.mult, op1=ALU.add),
                               reads=[f"ubP{jj}", f"cvP{jj}", "vc"], writes=[f"cvP{jj}"])
                            op("dve", lambda e, jj=jj, j=j, tap=tap, cvS=cvS: e.scalar_tensor_tensor(
                                out=cvS, in0=ubS[jj][:, :, tap:tap + T], scalar=vcol(5 + tap, j), in1=cvS,
                                op0=ALU.mult, op1=ALU.add),
                               reads=[f"ubS{jj}", f"cvS{jj}", "vc"], writes=[f"cvS{jj}"])
                        so, sok = next_ost()
                        op("act", lambda e, jj=jj, so=so: e.copy(out=so[:, 0:CH], in_=ubP[jj][:, NP:NP + CH]),
                           reads=[f"ubP{jj}"], writes=[sok])
                        op("act", lambda e, jj=jj, so=so: e.copy(
                            out=so[:, CH:CH + NSEQ * CH].rearrange("p (s r) -> p s r", r=CH), in_=ubS[jj][:, :, T:T + CH]),
                           reads=[f"ubS{jj}"], writes=[sok])
                        state_out(so[:, 0:CH + NSEQ * CH], [sok], CH + NSEQ * CH,
                                  [(0, CH, ocp[:, j * 128:(j + 1) * 128]),
                                   (CH, NSEQ * CH, ocs[:, j * 128:(j + 1) * 128])])
                    bun, bk = load_unit(bsrc, DC, nq * 128)
                    for jj in range(nq):
                        X = next_pso()
                        mm_chunk(X, bun, bk, jj * 128, nT, "nT", full_k)
                        gl = q0 + jj
                        op("dve", lambda e, jj=jj, X=X, gl=gl: e.tensor_tensor(out=aT[:, gl, :], in0=cvb[jj][:, :],
                                                                              in1=psO(X)[:, 0:TOT], op=ALU.mult),
                           reads=[("ps", X), f"cvP{jj}", f"cvS{jj}"], writes=[("aT", gl)])
                resid_matmul(lambda ob, pa=pa: w_out[pa * pc0 * 128:(pa + 1) * pc0 * 128, ob * 512:(ob + 1) * 512]
                             .rearrange("(k p) n -> p k n", p=128), pc0, last=(pa == nparts0 - 1))
            stats_finish()
            apply_norm(2)
            ffn(0)

            stats_finish()
            flush_deferred()
            P.alias(L1_KEYS, L0_KEYS)
            for j in range(DC):
                P.alias([("nTfix", j)], [("nT", j)])
            SW = PH + T
            for pbx, pbk in zip(pbs, ("pbh0", "pbh1")):
                op("dve", lambda e, pbx=pbx: e.memset(pbx[:, 0:PH], 0.0), writes=[pbk])
            def spool_dma(j):
                s = j % 2
                for hh in range(2):
                    op("sp", lambda e, s=s, hh=hh, j=j: e.dma_start(out=sp_t[s][0:SH, hh, :],
                                                                    in_=spool[hh * SH:(hh + 1) * SH, j * 128:(j + 1) * 128]),
                       writes=[("sp_t", s)], sem=f"spl{s}", inc=16)

            def pool_prep(j):
                s = j % 2
                pb, pbk, pbh = pbs[s], f"pb{s}", f"pbh{s}"
                pbS = pb[:, PH + NP: PBL].rearrange("p (s t) -> p s t", t=SW)
                if j == 0:
                    spool_dma(0)
                if j + 1 < DC:
                    spool_dma(j + 1)
                mi, mb = next_misc()
                for hh in range(2):
                    op("pe", lambda e, mb=mb, s=s, hh=hh: e.transpose(out=mb[:, hh * SH:(hh + 1) * SH], in_=sp_t[s][0:SH, hh, :],
                                                                     identity=ident[0:SH, 0:SH]),
                       reads=[("sp_t", s), "ident"], writes=[("psm", mi)], signal=(hh == 1))
                op("act", lambda e, mb=mb, pbS=pbS: e.copy(out=pbS[:, :, 0:PH],
                                                           in_=mb[:, 0:2 * SH].rearrange("p (s r) -> p s r", r=PH)),
                   reads=[("psm", mi)], writes=[pbh])
                op("dve", lambda e, j=j, pb=pb: e.scalar_tensor_tensor(out=pb[:, PH:PH + NP], in0=hT[:, j, 0:NP], scalar=vcol(1, j),
                                                                       in1=rstd[:, 0:NP], op0=ALU.mult, op1=ALU.mult),
                   reads=[("hT", j), "rstd", "vc"], writes=[pbk])
                op("dve", lambda e, j=j, pbS=pbS: e.scalar_tensor_tensor(
                    out=pbS[:, :, PH:SW], in0=hT[:, j, NP:TOT].rearrange("p (s t) -> p s t", t=T), scalar=vcol(1, j),
                    in1=rstd[:, NP:TOT].rearrange("p (s t) -> p s t", t=T), op0=ALU.mult, op1=ALU.mult),
                   reads=[("hT", j), "rstd", "vc"], writes=[pbk])
                sl = ((j // GC) % 2) * GC + (j % GC)
                op("act", lambda e, sl=sl, pb=pb: e.activation(out=aT[:, sl, 0:NP], in_=pb[:, PH:PH + NP], func=AF.Copy, scale=-1.0),
                   reads=[pbk], writes=[("aT", sl)])
                op("act", lambda e, sl=sl, pbS=pbS: e.activation(out=aT[:, sl, NP:TOT].rearrange("p (s t) -> p s t", t=T),
                                                                 in_=pbS[:, :, PH:SW], func=AF.Copy, scale=-1.0),
                   reads=[pbk], writes=[("aT", sl)])
                so, sok = next_ost()
                op("act", lambda e, so=so, pb=pb: e.copy(out=so[:, 0:PH], in_=pb[:, NP:NP + PH]), reads=[pbk], writes=[sok])
                state_out(so[:, 0:PH], [sok], PH, [(0, PH, opp[:, j * 128:(j + 1) * 128])])
                for hh in range(2):
                    so, sok = next_ost()
                    op("act", lambda e, so=so, hh=hh, pbS=pbS: e.copy(
                        out=so[:, 0:SH].rearrange("p (s r) -> p s r", r=PH),
                        in_=pbS[:, hh * (NSEQ // 2):(hh + 1) * (NSEQ // 2), T:SW]),
                       reads=[pbk, pbh], writes=[sok])
                    state_out(so[:, 0:SH], [sok], SH, [(0, SH, ops[hh * SH:(hh + 1) * SH, j * 128:(j + 1) * 128])])
                flush_deferred()

            last_fin = [1]

            def pool_compute(j):
                g = j // GC
                w = 2 ** (g + 1)
                s = j % 2
                pb, pbk, pbh = pbs[s], f"pb{s}", f"pbh{s}"
                pbS = pb[:, PH + NP: PBL].rearrange("p (s t) -> p s t", t=SW)
                cur, curk = pb, [pbk, pbh]
                bufs = [(t1, "t1"), (t2, "t2")]
                sh = 1
                bi = 1 - last_fin[0]
                while sh < w:
                    nb, nbk = bufs[bi % 2]
                    bi += 1
                    lo = 2 * sh - 1
                    op("dve", lambda e, nb=nb, cur=cur, lo=lo, sh=sh: e.tensor_tensor(
                        out=nb[:, lo:PBL], in0=cur[:, lo:PBL], in1=cur[:, lo - sh:PBL - sh], op=ALU.add),
                       reads=curk, writes=[nbk])
                    cur, curk = nb, [nbk]
                    sh *= 2
                last_fin[0] = (bi - 1) % 2
                curS = cur[:, PH + NP: PBL].rearrange("p (s t) -> p s t", t=SW)
                op("act", lambda e, j=j, cur=cur, w=w: e.activation(out=nT[:, j, IW:NP], in_=cur[:, PH + IW:PH + NP], func=AF.Copy,
                                                                    scale=1.0 / w),
                   reads=curk, writes=[("nT", j)])
                op("act", lambda e, j=j, curS=curS, w=w: e.activation(out=nT[:, j, NP:TOT].rearrange("p (s t) -> p s t", t=T),
                                                                      in_=curS[:, :, PH:SW], func=AF.Copy, scale=1.0 / w),
                   reads=curk, writes=[("nT", j)])
                op("dve", lambda e, j=j, cur=cur, g=g: e.tensor_tensor(out=nT[:, j, 0:IW], in0=cur[:, PH:PH + IW],
                                                                      in1=invc[:, g * IW:(g + 1) * IW], op=ALU.mult),
                   reads=curk + ["invc"], writes=[("nTfix", j)])

            mmq = []
            tailq = []

            def plan_mm(n):
                for _ in range(n):
                    if not mmq:
                        return
                    g_, jo, unit, ukey = mmq.pop(0)
                    jj_ = g_ * GC + jo
                    X = next_pso()
                    mm_terms(X, unit, ukey, jo * 128,
                             [(k, nT, ("nT", "nTfix"), g_ * GC + k) for k in range(GC)] +
                             [(k, aT, "aT", (g_ % 2) * GC + k) for k in range(GC)])

                    def tail_a(jj_=jj_, X=X):
                        op("dve", lambda e, j=jj_, X=X, lo=CL[0]: e.scalar_tensor_tensor(out=hT[:, j, lo:TOT], in0=psO(X)[:, 0:TOT - lo],
                                                                                      scalar=vcol(8, j), in1=hT[:, j, lo:TOT],
                                                                                      op0=ALU.mult, op1=ALU.add),
                           reads=[("ps", X), ("hT", jj_), "vc"], writes=[("hT", jj_)])
                    tailq.append((tail_a, jj_))

            def run_tails():
                js = []
                while tailq:
                    fn, jj_ = tailq.pop(0)
                    fn()
                    js.append(jj_)
                for jj_ in js:
                    stats_chunk(jj_)

            CL[0] = HALO
            pool_prep(0)
            for j in range(DC):
                if j + 1 < DC:
                    pool_prep(j + 1)
                pool_compute(j)
                run_tails()
                if j % GC == GC - 1:
                    g = j // GC
                    unit, ukey = load_unit(w_pool[g].rearrange("(k p) n -> p k n", p=128), GC, GC * 128)
                    for jo in range(GC):
                        mmq.append((g, jo, unit, ukey))
                plan_mm(2)
            while mmq or tailq:
                run_tails()
                plan_mm(2)
            stats_finish()
            apply_norm(3)
            ffn(1, gscale=4)

            P.alias(XT_KEYS, AT_KEYS)
            notile = (NOUT + 127) // 128
            pe_keepwarm(18)
            col_sums(HALO, notile, NOUT)
            rcol_finish(notile)
            for i in range(notile):
                r0 = i * 128
                nr = min(128, NOUT - r0)
                c0 = HALO + r0
                for hf in range(2):
                    q = (2 * i + hf) % 4
                    for jb in range(0, HC, GS):
                        mi, mb = next_misc()
                        for jj in range(GS):
                            j = hf * HC + jb + jj
                            op("pe", lambda e, mb=mb, jj=jj, j=j, c0=c0, nr=nr:
                               e.transpose(out=mb[0:nr, jj * 128:(jj + 1) * 128], in_=hT[:, j, c0:c0 + nr], identity=ident[:, :]),
                               reads=[("hT", j), "ident"], writes=[("psm", mi)], signal=(jj == GS - 1))
                        dst = xh[q][0:nr, jb * 128:(jb + GS) * 128]
                        src = mb[0:nr, 0:GS * 128]
                        rc = rcol[0:nr, i:i + 1]
                        evac(lambda e, dst=dst, src=src, rc=rc: e.activation(out=dst, in_=src, func=AF.Copy, scale=rc),
                             lambda e, dst=dst, src=src, rc=rc: e.tensor_scalar(out=dst, in0=src, scalar1=rc, scalar2=None, op0=ALU.mult),
                             reads=[("psm", mi), "rcol"], writes=[("xt", q)])
                    op("sp", lambda e, q=q, r0=r0, nr=nr, hf=hf: e.dma_start(out=y[r0:r0 + nr, hf * HD:(hf + 1) * HD], in_=xh[q][0:nr, :]),
                       reads=[("xt", q)], sem=f"yo{q}", inc=16)
            flush_deferred()
            P.final_wait("sp", [n for n in P.semnames if n.startswith("yo") or n.startswith("so")])

        P1 = Prog()
        units = []
        plan(P1, units, True)
        P = Prog()
        plan(P, units, False)

        for name in P.semnames:
            sems[name] = es.enter_context(nc.semaphore(name))
        with nc.Block() as block:
            def emit(ename):
                def body(eng):
                    for waits, fn, incspec in P.streams[ename]:
                        for (sname, val) in waits:
                            eng.wait_ge(sems[sname], val)
                        if fn is None:
                            continue
                        ins = fn(eng)
                        if incspec is not None:
                            ins.then_inc(sems[incspec[0]], incspec[1])
                return body
            block.tensor(emit("pe"))
            block.scalar(emit("act"))
            block.vector(emit("dve"))
            block.gpsimd(emit("pool"))
            block.sync(emit("sp"))
    return nc


def _col_layout(v, DC):
    return np.ascontiguousarray(v.reshape(DC, 128).T)


def run(cfg, x_prompt, x_sample, state_conv, state_pool, meta_tokens, norm_mix, norm_ffn, norm_final,
        conv_w_in, conv_w_dw, conv_w_out, pool_w, pool_scale, ffn_w_gate_up, ffn_w_down, trace=False):
    D, F, NPO, NSEQ = cfg["D"], cfg["F"], cfg["NPO"], cfg["NSEQ"]
    DC = D // 128
    B = x_prompt.shape[0]
    SEQ = x_prompt.shape[1]
    f32 = np.float32
    nc = build_program(cfg)

    vecs = [norm_mix[0], norm_mix[1], norm_ffn[0], norm_ffn[1], norm_final,
            conv_w_dw[0, 0], conv_w_dw[0, 1], conv_w_dw[0, 2], pool_scale[0]]
    vcols = np.ascontiguousarray(np.concatenate([_col_layout(np.asarray(v, f32), DC) for v in vecs], axis=1))
    ident = np.eye(128, dtype=f32)
    shared = dict(
        vcols=vcols, ident=ident,
        w_in=np.ascontiguousarray(conv_w_in[0], f32), w_out=np.ascontiguousarray(conv_w_out[0], f32),
        w_pool=np.ascontiguousarray(pool_w[0], f32), w_gu=np.ascontiguousarray(ffn_w_gate_up, f32),
        w_dn=np.ascontiguousarray(ffn_w_down, f32),
    )
    wins = np.array([2.0, 4.0, 8.0, 16.0], f32)
    in_maps = []
    for c in range(N_CORES):
        b, half = c // 2, c % 2
        full = np.concatenate([np.asarray(meta_tokens, f32), np.asarray(x_prompt[b], f32)], axis=0)
        start = half * NPO
        lo = start - HALO
        if lo < 0:
            seg = np.concatenate([np.zeros((-lo, D), f32), full[0:start + NPO]], axis=0)
        else:
            seg = full[lo:start + NPO]
        xs = np.asarray(x_sample[c * NSEQ:(c + 1) * NSEQ], f32).reshape(NSEQ * T, D)
        xin = np.ascontiguousarray(np.concatenate([seg, xs], axis=0))
        pos = (lo + np.arange(IW)).astype(np.float64)
        cnt = np.minimum(wins[:, None].astype(np.float64), np.maximum(pos[None, :], 0.0) + 1.0)
        invc = np.broadcast_to((1.0 / cnt).astype(f32).reshape(1, 4 * IW), (128, 4 * IW))
        m = dict(shared)
        m.update(
            xin=xin,
            sconv=np.ascontiguousarray(np.asarray(state_conv[0, c * NSEQ:(c + 1) * NSEQ], f32).reshape(NSEQ * CH, D)),
            spool=np.ascontiguousarray(np.asarray(state_pool[0, c * NSEQ:(c + 1) * NSEQ], f32).reshape(NSEQ * PH, D)),
            invc=np.ascontiguousarray(invc),
        )
        in_maps.append(m)
    res = run_bass_kernel_spmd(nc, in_maps, core_ids=list(range(N_CORES)), trace=trace)
    R = res.results
    DECB = x_sample.shape[0]
    y_prompt = np.empty((B, SEQ, D), f32)
    y_sample = np.empty((DECB, T, D), f32)
    ncp = np.empty((1, B, CH, D), f32)
    npp = np.empty((1, B, PH, D), f32)
    ncs = np.empty((1, DECB, CH, D), f32)
    nps = np.empty((1, DECB, PH, D), f32)
    for c in range(N_CORES):
        b, half = c // 2, c % 2
        yc = np.asarray(R[c]["y"])
        if half == 0:
            y_prompt[b, 0:NPO - N_META] = yc[N_META:NPO]
        else:
            y_prompt[b, NPO - N_META:] = yc[0:NPO]
            ncp[0, b] = np.asarray(R[c]["ocp"])
            npp[0, b] = np.asarray(R[c]["opp"])
        y_sample[c * NSEQ:(c + 1) * NSEQ] = yc[NPO:].reshape(NSEQ, T, D)
        ncs[0, c * NSEQ:(c + 1) * NSEQ] = np.asarray(R[c]["ocs"]).reshape(NSEQ, CH, D)
        nps[0, c * NSEQ:(c + 1) * NSEQ] = np.asarray(R[c]["ops"]).reshape(NSEQ, PH, D)
    out = (y_prompt, y_sample, ncp, npp, ncs, nps)
    if trace:
        return out, res
    return out


def kernel(x_prompt, x_sample, state_conv, state_pool, meta_tokens, norm_mix, norm_ffn, norm_final,
           conv_w_in, conv_w_dw, conv_w_out, pool_w, pool_scale, ffn_w_gate_up, ffn_w_down):
    args = [np.asarray(a) for a in (x_prompt, x_sample, state_conv, state_pool, meta_tokens, norm_mix, norm_ffn,
                                    norm_final, conv_w_in, conv_w_dw, conv_w_out, pool_w, pool_scale,
                                    ffn_w_gate_up, ffn_w_down)]
    return run(FULL_CFG, *args)
```
